# Optimizing a Trainium2 kernel written in Bass

```python
import math
import jax, jax.numpy as jnp
from jax import lax
import numpy as np

D_MODEL = 2048
BATCH = 8
SEQ = 2048
DEPTH = 4

D_FF = 5632
A_DILATION_PAIRS = ((128, 1), (512, 4), (2048, 16))
A_HEADS_PER_GROUP = 4
A_HEADS = A_HEADS_PER_GROUP * len(A_DILATION_PAIRS)
A_HEAD_DIM = 128
A_OUT = A_HEADS_PER_GROUP * A_HEAD_DIM
ALIBI_MAX_BIAS = 8.0
B_HEADS = 8
B_HEAD_DIM = 128
B_WIDTH = B_HEADS * B_HEAD_DIM
B_CONV = 4
B_CHUNK = 64
C_HEAD_DIM = 64
C_HEADS = 16
C_WIDTH = C_HEADS * C_HEAD_DIM
C_DECAY_LORA = 96
C_ICLR_LORA = 96
C_GATE_LORA = 256
C_GN_EPS = 64e-5
A_COLS = 3 * A_HEADS * A_HEAD_DIM
B_COLS = 4 * B_WIDTH + 2 * B_HEADS
C_COLS = 3 * C_WIDTH + C_DECAY_LORA + C_ICLR_LORA + C_GATE_LORA
G_COLS = 3 * D_MODEL
IN_COLS = A_COLS + B_COLS + C_COLS + G_COLS

NORM_EPS = 1e-6
NEG_INF = -1e30

kernel_name = 'hybrid_dilated_gdn_rwkv7_macaron'


def split_cols(t, widths):
    return jnp.split(t, np.cumsum(widths)[:-1].tolist(), axis=-1)


def rms_norm(x, g, eps=NORM_EPS):
    xf = x.astype(jnp.float32)
    y = xf * lax.rsqrt(jnp.mean(xf * xf, axis=-1, keepdims=True) + eps)
    return (y * g.astype(jnp.float32)).astype(x.dtype)


def l2_normalize(x, eps=1e-6):
    xf = x.astype(jnp.float32)
    return xf * lax.rsqrt(jnp.sum(xf * xf, axis=-1, keepdims=True) + eps)


def swiglu(h, w_gu, w_down):
    gate, up = jnp.split(h @ w_gu, 2, axis=-1)
    return (jax.nn.silu(gate) * up) @ w_down


def to_heads(t, n_heads):
    return t.reshape(*t.shape[:-1], n_heads, -1)


def token_shift(x, mu):
    prev = jnp.pad(x, ((0, 0), (1, 0), (0, 0)))[:, :-1]
    return x + (prev - x) * mu


def causal_depthwise_conv(x, w):
    K, S = w.shape[0], x.shape[1]
    xp = jnp.pad(x, ((0, 0), (K - 1, 0), (0, 0)))
    return sum(xp[:, j:j + S] * w[j] for j in range(K))


def banded_causal_attention(q, k, v, slopes, steps, dilation):
    N, H, L, Dh = q.shape
    blk = steps
    nb = -(-L // blk)
    lp = nb * blk
    pad = ((0, 0), (0, 0), (0, lp - L), (0, 0))
    q, k, v = (jnp.pad(t, pad) for t in (q, k, v))

    def band(t):
        prev = jnp.pad(t, ((0, 0), (0, 0), (blk, 0), (0, 0)))[:, :, :lp].reshape(N, H, nb, blk, Dh)
        return jnp.concatenate([prev, t.reshape(N, H, nb, blk, Dh)], axis=3)

    s = jnp.einsum('nhbqd,nhbkd->nhbqk', q.reshape(N, H, nb, blk, Dh), band(k)).astype(jnp.float32) * (Dh ** -0.5)
    qi = jnp.arange(blk)[:, None]
    ki = jnp.arange(2 * blk)[None, :]
    delta = qi + blk - ki
    key_pos = jnp.arange(nb)[:, None, None] * blk - blk + ki[None]
    valid = (delta >= 0) & (delta <= steps) & (key_pos >= 0)
    bias = -slopes[:, None, None, None] * (delta * dilation).astype(jnp.float32)
    s = jnp.where(valid, s + bias, NEG_INF)
    lse = jax.nn.logsumexp(s, axis=-1)
    p = jnp.exp(s - lse[..., None])
    o = jnp.einsum('nhbqk,nhbkd->nhbqd', p.astype(v.dtype), band(v))
    return o.reshape(N, H, lp, Dh)[:, :, :L], lse.reshape(N, H, lp)[:, :, :L]


def mixer_dilated_attention(q, k, v):
    Bn, S = q.shape[0], q.shape[1]
    hpg, dh = A_HEADS_PER_GROUP, A_HEAD_DIM
    slopes = 2.0 ** (-ALIBI_MAX_BIAS * (jnp.arange(A_HEADS, dtype=jnp.float32) + 1.0) / A_HEADS)
    outs, lses = [], []
    for gi, (win, dil) in enumerate(A_DILATION_PAIRS):
        hs = slice(gi * hpg, (gi + 1) * hpg)
        L = S // dil

        def to_residue(t):
            t = t[:, :, hs].reshape(Bn, L, dil, hpg, dh).transpose(0, 2, 3, 1, 4)
            return t.reshape(Bn * dil, hpg, L, dh)

        o, lse = banded_causal_attention(to_residue(q), to_residue(k), to_residue(v), slopes[hs], win // dil, dil)
        outs.append(o.reshape(Bn, dil, hpg, L, dh).transpose(0, 3, 1, 2, 4).reshape(Bn, S, hpg, dh))
        lses.append(lse.reshape(Bn, dil, hpg, L).transpose(0, 3, 1, 2).reshape(Bn, S, hpg))
    wts = jax.nn.softmax(jnp.stack(lses, axis=2), axis=2)
    o = jnp.einsum('bsgh,bsghd->bshd', wts.astype(q.dtype), jnp.stack(outs, axis=2))
    return o.reshape(Bn, S, A_OUT)


def chunked_gated_delta_rule(q, k, v, log_decay, beta):
    Bn, H, S, Dk = q.shape
    Dv = v.shape[-1]
    C = B_CHUNK
    n = S // C
    q, k, v = (t.reshape(Bn, H, n, C, t.shape[-1]) for t in (q, k, v))
    beta = beta.reshape(Bn, H, n, C)
    g = jnp.cumsum(log_decay.reshape(Bn, H, n, C), axis=-1)
    tril = jnp.tril(jnp.ones((C, C), dtype=bool))
    tril_strict = jnp.tril(jnp.ones((C, C), dtype=bool), -1)
    decay = jnp.exp(jnp.where(tril, g[..., :, None] - g[..., None, :], NEG_INF))
    kb = k * beta[..., None]
    vb = v * beta[..., None]
    m = jnp.where(tril_strict, jnp.einsum('bhncd,bhnjd->bhncj', kb, k) * decay, 0.0)
    eye = jnp.eye(C, dtype=jnp.float32)
    rhs = jnp.concatenate([vb, kb * jnp.exp(g)[..., None]], axis=-1)
    sol = lax.linalg.triangular_solve(eye + m, rhs, left_side=True, lower=True, unit_diagonal=True)
    u, w = sol[..., :Dv], sol[..., Dv:]
    att = jnp.where(tril, jnp.einsum('bhncd,bhnjd->bhncj', q, k) * decay, 0.0)
    q_dec = q * jnp.exp(g)[..., None]
    k_dec = k * jnp.exp(g[..., -1:] - g)[..., None]
    chunk_decay = jnp.exp(g[..., -1])

    def step(state, xs):
        q_c, k_c, u_c, w_c, att_c, cd = xs
        v_new = u_c - jnp.einsum('bhck,bhkv->bhcv', w_c, state)
        o_c = jnp.einsum('bhck,bhkv->bhcv', q_c, state) + jnp.einsum('bhcj,bhjv->bhcv', att_c, v_new)
        state = state * cd[..., None, None] + jnp.einsum('bhck,bhcv->bhkv', k_c, v_new)
        return state, o_c

    xs = tuple(jnp.moveaxis(t, 2, 0) for t in (q_dec, k_dec, u, w, att, chunk_decay))
    s0 = jnp.zeros((Bn, H, Dk, Dv), jnp.float32)
    _, o = lax.scan(step, s0, xs)
    return jnp.moveaxis(o, 0, 2).reshape(Bn, H, S, Dv)


def mixer_gated_deltanet(qkv, z, alpha, beta_logit, conv_w, a_log, dt_bias, norm_w):
    Bn, S = qkv.shape[0], qkv.shape[1]
    f32 = jnp.float32
    qkv = jax.nn.silu(causal_depthwise_conv(qkv, conv_w))
    q, k, v = jnp.split(qkv, 3, axis=-1)
    heads = lambda t: to_heads(t, B_HEADS).transpose(0, 2, 1, 3)
    q = l2_normalize(heads(q)) * (B_HEAD_DIM ** -0.5)
    k = l2_normalize(heads(k))
    v = heads(v).astype(f32)
    log_decay = (-jnp.exp(a_log.astype(f32)) * jax.nn.softplus(alpha.astype(f32) + dt_bias.astype(f32))).transpose(0, 2, 1)
    beta = jax.nn.sigmoid(beta_logit.astype(f32)).transpose(0, 2, 1)
    o = chunked_gated_delta_rule(q, k, v, log_decay, beta).transpose(0, 2, 1, 3)
    o = rms_norm(o, norm_w) * jax.nn.silu(to_heads(z, B_HEADS).astype(f32))
    return o.reshape(Bn, S, B_WIDTH).astype(z.dtype)


def rwkv7_recurrence(r, log_w, k, v, kk, a):
    xs = tuple(jnp.moveaxis(t.astype(jnp.float32), 1, 0) for t in (r, log_w, k, v, kk, a))
    Bn, H, N = r.shape[0], r.shape[2], r.shape[3]

    def step(state, inp):
        r_t, lw_t, k_t, v_t, kk_t, a_t = inp
        sa = jnp.einsum('bhvk,bhk->bhv', state, -kk_t)
        state = (state * jnp.exp(lw_t)[:, :, None, :]
                 + sa[..., None] * (kk_t * a_t)[:, :, None, :]
                 + v_t[..., None] * k_t[:, :, None, :])
        return state, jnp.einsum('bhvk,bhk->bhv', state, r_t)

    s0 = jnp.zeros((Bn, H, N, N), jnp.float32)
    _, o = lax.scan(step, s0, xs)
    return jnp.moveaxis(o, 0, 1)


def mixer_rwkv7(cols, mu, w0, w2, a0, a2, g2, k_k, k_a, r_k, gn_w, gn_b):
    Bn, S = cols.shape[0], cols.shape[1]
    cols = token_shift(cols, mu)
    r, k, v, xw, xa, xg = split_cols(cols, (C_WIDTH, C_WIDTH, C_WIDTH, C_DECAY_LORA, C_ICLR_LORA, C_GATE_LORA))
    w = -jax.nn.softplus(-(w0 + jnp.tanh(xw) @ w2)) - 0.5
    log_w = -jnp.exp(w.astype(jnp.float32))
    a = jax.nn.sigmoid(a0 + xa @ a2)
    g = jax.nn.sigmoid(xg) @ g2
    kk = l2_normalize(to_heads(k * k_k, C_HEADS))
    k = k * (1 + (a - 1) * k_a)
    rh, kh, vh = to_heads(r, C_HEADS), to_heads(k, C_HEADS), to_heads(v, C_HEADS)
    o = rwkv7_recurrence(rh, to_heads(log_w, C_HEADS), kh, vh, kk, to_heads(a, C_HEADS))
    mean = jnp.mean(o, axis=-1, keepdims=True)
    var = jnp.mean(jnp.square(o - mean), axis=-1, keepdims=True)
    o = ((o - mean) * lax.rsqrt(var + C_GN_EPS)).reshape(Bn, S, C_WIDTH) * gn_w + gn_b
    bonus = jnp.sum(rh * kh * r_k, axis=-1, keepdims=True) * vh
    o = (o + bonus.reshape(Bn, S, C_WIDTH)) * g
    return o.astype(cols.dtype)


def setup_inputs(seed: int = 0) -> dict:
    key = jax.random.key(seed)
    ks = iter(list(jax.random.split(key, 40)))
    nrm = lambda shape, scale: jax.random.normal(next(ks), shape, jnp.float32) * scale
    unif = lambda shape, lo, hi: jax.random.uniform(next(ks), shape, jnp.float32, lo, hi)
    gain = lambda shape: 1.0 + nrm(shape, 0.02)
    L = DEPTH
    x = nrm((BATCH, SEQ, D_MODEL), 1.0)
    ffn1_norm = gain((L, D_MODEL))
    ffn1_w_gu = nrm((L, D_MODEL, 2 * D_FF), D_MODEL ** -0.5)
    ffn1_w_down = nrm((L, D_FF, D_MODEL), D_FF ** -0.5)
    mix_norm = gain((L, D_MODEL))
    w_in = nrm((L, D_MODEL, IN_COLS), D_MODEL ** -0.5)
    b_conv = nrm((L, B_CONV, 3 * B_WIDTH), B_CONV ** -0.5)
    b_a_log = jnp.log(unif((L, B_HEADS), 1.0, 16.0))
    dt = jnp.exp(unif((L, B_HEADS), math.log(1e-3), math.log(1e-1)))
    b_dt_bias = dt + jnp.log(-jnp.expm1(-dt))
    b_norm = gain((L, B_HEAD_DIM))
    c_mu = unif((L, C_COLS), 0.0, 1.0)
    c_w0 = unif((L, C_WIDTH), -6.5, -1.5)
    c_w2 = nrm((L, C_DECAY_LORA, C_WIDTH), 0.1 * C_DECAY_LORA ** -0.5)
    c_a0 = nrm((L, C_WIDTH), 0.1)
    c_a2 = nrm((L, C_ICLR_LORA, C_WIDTH), 0.5 * C_ICLR_LORA ** -0.5)
    c_g2 = nrm((L, C_GATE_LORA, C_WIDTH), C_GATE_LORA ** -0.5)
    c_k_k = 0.85 + nrm((L, C_WIDTH), 0.05)
    c_k_a = 1.0 + nrm((L, C_WIDTH), 0.05)
    c_r_k = nrm((L, C_HEADS, C_HEAD_DIM), 0.1)
    c_gn_w = gain((L, C_WIDTH))
    c_gn_b = nrm((L, C_WIDTH), 0.01)
    proj_a = nrm((L, A_OUT, D_MODEL), A_OUT ** -0.5)
    proj_b = nrm((L, B_WIDTH, D_MODEL), B_WIDTH ** -0.5)
    proj_c = nrm((L, C_WIDTH, D_MODEL), C_WIDTH ** -0.5)
    w_out = nrm((L, D_MODEL, D_MODEL), D_MODEL ** -0.5)
    ffn2_norm = gain((L, D_MODEL))
    ffn2_w_gu = nrm((L, D_MODEL, 2 * D_FF), D_MODEL ** -0.5)
    ffn2_w_down = nrm((L, D_FF, D_MODEL), D_FF ** -0.5)
    final_norm = gain((D_MODEL,))
    return {'x': x, 'ffn1_norm': ffn1_norm, 'ffn1_w_gu': ffn1_w_gu, 'ffn1_w_down': ffn1_w_down,
            'mix_norm': mix_norm, 'w_in': w_in, 'b_conv': b_conv, 'b_a_log': b_a_log,
            'b_dt_bias': b_dt_bias, 'b_norm': b_norm, 'c_mu': c_mu, 'c_w0': c_w0, 'c_w2': c_w2,
            'c_a0': c_a0, 'c_a2': c_a2, 'c_g2': c_g2, 'c_k_k': c_k_k, 'c_k_a': c_k_a, 'c_r_k': c_r_k,
            'c_gn_w': c_gn_w, 'c_gn_b': c_gn_b, 'proj_a': proj_a, 'proj_b': proj_b, 'proj_c': proj_c,
            'w_out': w_out, 'ffn2_norm': ffn2_norm, 'ffn2_w_gu': ffn2_w_gu, 'ffn2_w_down': ffn2_w_down,
            'final_norm': final_norm}


def reference(x, ffn1_norm, ffn1_w_gu, ffn1_w_down, mix_norm, w_in, b_conv, b_a_log, b_dt_bias, b_norm,
              c_mu, c_w0, c_w2, c_a0, c_a2, c_g2, c_k_k, c_k_a, c_r_k, c_gn_w, c_gn_b,
              proj_a, proj_b, proj_c, w_out, ffn2_norm, ffn2_w_gu, ffn2_w_down, final_norm):
    Bn, S = x.shape[0], x.shape[1]
    for l in range(DEPTH):
        x = x + 0.5 * swiglu(rms_norm(x, ffn1_norm[l]), ffn1_w_gu[l], ffn1_w_down[l])
        h = rms_norm(x, mix_norm[l])
        cols_a, cols_b, cols_c, cols_g = split_cols(h @ w_in[l], (A_COLS, B_COLS, C_COLS, G_COLS))
        qa, ka, va = (to_heads(t, A_HEADS) for t in jnp.split(cols_a, 3, axis=-1))
        y_a = mixer_dilated_attention(qa, ka, va)
        qkv_b, z_b, alpha_b, beta_b = split_cols(cols_b, (3 * B_WIDTH, B_WIDTH, B_HEADS, B_HEADS))
        y_b = mixer_gated_deltanet(qkv_b, z_b, alpha_b, beta_b, b_conv[l], b_a_log[l], b_dt_bias[l], b_norm[l])
        y_c = mixer_rwkv7(cols_c, c_mu[l], c_w0[l], c_w2[l], c_a0[l], c_a2[l], c_g2[l],
                          c_k_k[l], c_k_a[l], c_r_k[l], c_gn_w[l], c_gn_b[l])
        g_a, g_b, g_c = jnp.split(jax.nn.sigmoid(cols_g), 3, axis=-1)
        merged = g_a * (y_a @ proj_a[l]) + g_b * (y_b @ proj_b[l]) + g_c * (y_c @ proj_c[l])
        x = x + merged @ w_out[l]
        x = x + 0.5 * swiglu(rms_norm(x, ffn2_norm[l]), ffn2_w_gu[l], ffn2_w_down[l])
    return rms_norm(x, final_norm)
```

```python
import contextlib
import numpy as np
import concourse.bass as bass
import concourse.mybir as mybir
from concourse.bass_utils import run_bass_kernel_spmd

F32 = mybir.dt.float32
BF16 = mybir.dt.bfloat16
AF = mybir.ActivationFunctionType
ALU = mybir.AluOpType
AX = mybir.AxisListType

D = 2048
S = 2048
DEPTH = 4
FF = 5632
KC = D // 128
NTG = S // 512
A_COLS, B_COLS, C_COLS, G_COLS = 4608, 4112, 3520, 6144
IN_COLS = A_COLS + B_COLS + C_COLS + G_COLS
B0 = A_COLS
C0 = A_COLS + B_COLS
G0 = C0 + C_COLS
NEG = -1.0e30


class Buf:
    __slots__ = ("key", "w", "r")

    def __init__(self, key):
        self.key = key
        self.w = []
        self.r = {}


class KB:
    NPOOL = 8

    def __init__(self, nc, stack):
        self.nc = nc
        self.stack = stack
        self.eng = {"pe": nc.tensor, "dve": nc.vector, "act": nc.scalar, "pool": nc.gpsimd, "sp": nc.sync}
        self.sem = {}
        self.cnt = {}
        for e in self.eng:
            self.sem[e] = stack.enter_context(nc.semaphore("s_" + e))
            self.cnt[e] = 0
        self.dsem = {}
        self.dcnt = {}
        for q in ("sp", "pool"):
            self.dsem[q] = [stack.enter_context(nc.semaphore(f"d_{q}{i}")) for i in range(self.NPOOL)]
            self.dcnt[q] = 0
        self.seen = {e: {} for e in self.eng}
        self.bufs = {}
        self.n_ins = 0
        self.n_wait = 0

    def sb(self, name, shape, dtype):
        return self.stack.enter_context(self.nc.sbuf_tensor(name, list(shape), dtype))

    def ps(self, name, shape, dtype=F32):
        return self.stack.enter_context(self.nc.psum_tensor(name, list(shape), dtype))

    def buf(self, *key):
        b = self.bufs.get(key)
        if b is None:
            b = self.bufs[key] = Buf(key)
        return b

    def _wait(self, e, ev):
        semkey, sem, val, src = ev
        if self.seen[e].get(semkey, 0) >= val:
            return
        self.eng[e].wait_ge(sem, val)
        self.seen[e][semkey] = val
        self.n_wait += 1

    def _deps(self, e, reads, writes, is_dma=False):
        for b in reads:
            for ev in b.w:
                if ev[3] == e and e == "pe" and not is_dma:
                    continue
                self._wait(e, ev)
        for b in writes:
            for ev in b.w:
                if ev[3] == e and not is_dma and ev[0][0] == "c":
                    continue
                self._wait(e, ev)
            for ev in b.r.values():
                if ev[3] == e and not is_dma and ev[0][0] == "c":
                    continue
                self._wait(e, ev)

    def _record(self, ev, reads, writes):
        for b in writes:
            b.w = [ev]
            b.r = {}
        for b in reads:
            b.r[ev[0]] = ev

    def op(self, e, fn, reads=(), writes=()):
        self._deps(e, reads, writes)
        ins = fn(self.eng[e])
        self.cnt[e] += 1
        ins.then_inc(self.sem[e], 1)
        ev = (("c", e), self.sem[e], self.cnt[e], e)
        self._record(ev, reads, writes)
        self.n_ins += 1
        return ins

    def mm_group(self, fns, reads=(), writes=()):
        e = "pe"
        self._deps(e, reads, writes)
        ins = None
        for fn in fns:
            ins = fn(self.eng[e])
            self.n_ins += 1
        self.cnt[e] += 1
        ins.then_inc(self.sem[e], 1)
        ev = (("c", e), self.sem[e], self.cnt[e], e)
        self._record(ev, reads, writes)

    def dma(self, q, out, in_, reads=(), writes=(), **kw):
        e = q
        i = self.dcnt[q]
        slot = i % self.NPOOL
        rnd = i // self.NPOOL
        sem = self.dsem[q][slot]
        semkey = ("d", q, slot)
        if rnd > 0:
            self._wait(e, (semkey, sem, 16 * rnd, e))
        self._deps(e, reads, writes, is_dma=True)
        ins = self.eng[e].dma_start(out=out, in_=in_, **kw)
        ins.then_inc(sem, 16)
        self.dcnt[q] += 1
        ev = (semkey, sem, 16 * (rnd + 1), e)
        self._record(ev, reads, writes)
        self.n_ins += 1
        return ev

    def all_events(self):
        evs = []
        for q in self.dsem:
            n = self.dcnt[q]
            for slot in range(self.NPOOL):
                k = (n - slot + self.NPOOL - 1) // self.NPOOL
                if k > 0:
                    evs.append((("d", q, slot), self.dsem[q][slot], 16 * k, q))
        for x in self.cnt:
            if self.cnt[x] > 0:
                evs.append((("c", x), self.sem[x], self.cnt[x], x))
        return evs

    def barrier(self, engines=("pe", "dve", "act", "pool", "sp")):
        evs = self.all_events()
        for e in engines:
            for ev in evs:
                if ev[0] == ("c", e):
                    continue
                self._wait(e, ev)

    def wait_all(self, e):
        for ev in self.all_events():
            if ev[0] == ("c", e):
                continue
            self._wait(e, ev)


NPV = 256
PV_F1, PV_MIX, PV_F2, PV_FIN = 0, 16, 32, 48
PV_CONV = 64
PV_MU = 160
PV_W0, PV_A0, PV_KK, PV_KA, PV_RK, PV_GNW, PV_GNB = 188, 196, 204, 212, 220, 228, 236
PV_ALOG, PV_DTB = 244, 245

C_CHUNKS = [(i * 128, 128) for i in range(24)] + [(3072, 96), (3168, 96), (3264, 128), (3392, 128)]


def _cols(v):
    return np.ascontiguousarray(v.reshape(-1, 128).T)


def pack_pv(inp, l):
    pv = np.zeros((128, NPV), np.float32)
    pv[:, PV_F1:PV_F1 + 16] = _cols(inp["ffn1_norm"][l])
    pv[:, PV_MIX:PV_MIX + 16] = _cols(inp["mix_norm"][l])
    pv[:, PV_F2:PV_F2 + 16] = _cols(inp["ffn2_norm"][l])
    pv[:, PV_FIN:PV_FIN + 16] = _cols(inp["final_norm"])
    bc = inp["b_conv"][l]
    for cc in range(24):
        for j in range(4):
            pv[:, PV_CONV + cc * 4 + j] = bc[j, cc * 128:(cc + 1) * 128]
    mu = inp["c_mu"][l]
    for ci, (c0, n) in enumerate(C_CHUNKS):
        pv[:n, PV_MU + ci] = mu[c0:c0 + n]
    pv[:, PV_W0:PV_W0 + 8] = _cols(inp["c_w0"][l])
    pv[:, PV_A0:PV_A0 + 8] = _cols(inp["c_a0"][l])
    pv[:, PV_KK:PV_KK + 8] = _cols(inp["c_k_k"][l])
    pv[:, PV_KA:PV_KA + 8] = _cols(inp["c_k_a"][l])
    pv[:, PV_RK:PV_RK + 8] = _cols(inp["c_r_k"][l].reshape(-1))
    pv[:, PV_GNW:PV_GNW + 8] = _cols(inp["c_gn_w"][l])
    pv[:, PV_GNB:PV_GNB + 8] = _cols(inp["c_gn_b"][l])
    pv[0:8, PV_ALOG] = inp["b_a_log"][l]
    pv[0:8, PV_DTB] = inp["b_dt_bias"][l]
    return pv


def make_consts():
    c = {}
    c["ident"] = np.eye(128, dtype=np.float32)
    s = np.arange(128)[:, None]
    t = np.arange(128)[None, :]
    same = (s // 64) == (t // 64)
    m = np.zeros((128, 6 * 128), np.float32)
    m[:, 0:128] = (same & (t > s))
    m[:, 128:256] = (same & (t >= s))
    m[:, 256:384] = (same & (t < s))
    m[:, 384:512] = np.where(same & (t >= s), 0.0, NEG)
    m[:, 512:640] = np.where(same & (t <= s), 0.0, NEG)
    m[:, 640:768] = same
    c["cmask"] = m
    slopes = 2.0 ** (-8.0 * (np.arange(12, dtype=np.float64) + 1.0) / 12)
    dil = [1, 4, 16]
    qi = np.arange(128)[:, None]
    ki = np.arange(256)[None, :]
    delta = qi + 128 - ki
    valid = (delta >= 0) & (delta <= 128)
    ab = np.zeros((128, 12, 256), np.float32)
    for h in range(12):
        d = dil[h // 4]
        ab[:, h, :] = np.where(valid, -slopes[h] * (delta * d), NEG)
    c["abias"] = ab.reshape(128, 12 * 256)
    sel = np.zeros((16, 16 * 128), np.float32)
    for h in range(16):
        sel[h, h * 128:(h + 1) * 128] = 1.0
    c["sel"] = sel
    rm = np.ones((128, S), np.float32)
    rm[:, 0::64] = 0.0
    c["rmask"] = rm
    return c


W_SHAPES = {
    "ffn1_w_gu": [D, 2 * FF], "ffn1_w_down": [FF, D], "w_in": [D, IN_COLS],
    "c_w2": [96, 1024], "c_a2": [96, 1024], "c_g2": [256, 1024],
    "proj_a": [512, D], "proj_b": [1024, D], "proj_c": [1024, D], "w_out": [D, D],
    "ffn2_w_gu": [D, 2 * FF], "ffn2_w_down": [FF, D],
}
ARENA_F32 = 18432


class Prog:
    def __init__(self, nl=DEPTH, mode="full"):
        self.nl = nl
        self.mode = mode
        nc = bass.Bass("TRN2", target_bir_lowering=False)
        self.nc = nc
        dt = nc.dram_tensor
        self.xT_in = dt("xT", [D, S], F32, kind="ExternalInput").ap()
        self.w = {k: dt(k, [nl] + shp, F32, kind="ExternalInput").ap() for k, shp in W_SHAPES.items()}
        self.pv_d = dt("pv", [nl, 128, NPV], F32, kind="ExternalInput").ap()
        self.bnorm_d = dt("bnorm", [nl, 128, 128], F32, kind="ExternalInput").ap()
        self.c_ident = dt("c_ident", [128, 128], F32, kind="ExternalInput").ap()
        self.c_cmask = dt("c_cmask", [128, 768], F32, kind="ExternalInput").ap()
        self.c_abias = dt("c_abias", [128, 12 * 256], F32, kind="ExternalInput").ap()
        self.c_sel = dt("c_sel", [16, 16 * 128], F32, kind="ExternalInput").ap()
        self.c_rmask = dt("c_rmask", [128, S], F32, kind="ExternalInput").ap()
        dbg = (mode != "full")
        kind_s = "ExternalOutput" if dbg else "Internal"
        self.outT = dt("outT", [D, S], F32, kind="ExternalOutput").ap()
        self.XT = dt("XT", [D, S], F32, kind=kind_s).ap()
        self.AT = dt("AT", [FF, S], BF16, kind="Internal").ap()
        self.COLS = dt("COLS", [G0, S], F32, kind=("ExternalInput" if mode.startswith("mix_in") else kind_s)).ap()
        self.GT = dt("GT", [G_COLS, S], BF16, kind="Internal").ap()
        self.YT = dt("YT", [2560, S], BF16, kind=kind_s).ap()
        self.MT = dt("MT", [D, S], BF16, kind="Internal").ap()
        with contextlib.ExitStack() as st:
            self.kb = kb = KB(nc, st)
            self.ACTT = kb.sb("ACTT", [128, 45056], BF16)
            self.WB = [kb.sb("WB0", [128, 8192], BF16), kb.sb("WB1", [128, 8192], BF16)]
            self.AR = kb.sb("ARENA", [128, ARENA_F32], F32)
            self.ident_f = kb.sb("ident_f", [128, 128], F32)
            self.ident_b = kb.sb("ident_b", [128, 128], BF16)
            self.onesD = kb.sb("onesD", [128, 128], BF16)
            self.ones_f = kb.sb("ones_f", [128, 128], F32)
            self.cmask = kb.sb("cmask", [128, 768], F32)
            self.pv = [kb.sb("pv0", [128, NPV], F32), kb.sb("pv1", [128, NPV], F32)]
            self.pb = [kb.ps(f"pb{i}", [128, 512], F32) for i in range(6)]
            self.pbuf = [kb.buf("pb", i) for i in range(6)]
            self.psT = [kb.ps(f"psT{i}", [128, 1024], BF16) for i in range(2)]
            self.psTb = [kb.buf("psT", i) for i in range(2)]
            self.grot = 0
            self.wrot = 0
            self.build()
            kb.wait_all("sp")
            kb.wait_all("pe")
            kb.wait_all("act")
            kb.wait_all("dve")
            kb.wait_all("pool")

    def ar(self, off, n, dtype=F32, shape=None):
        v = self.AR[:, off:off + n]
        if dtype == BF16:
            v = v.bitcast(BF16)
        return v

    def ac(self, off, n, dtype=BF16):
        v = self.ACTT[:, off:off + n]
        if dtype == F32:
            v = v.bitcast(F32)
        return v

    def hT(self):
        return self.ACTT[:, 0:KC * S].rearrange("p (k t) -> p k t", k=KC)

    def load_consts(self):
        kb = self.kb
        kb.dma("sp", self.ident_f[:], self.c_ident[:, :], writes=[kb.buf("ident_f")])
        kb.dma("pool", self.ident_b[:], self.c_ident[:, :], writes=[kb.buf("ident_b")])
        kb.dma("sp", self.cmask[:], self.c_cmask[:, :], writes=[kb.buf("cmask")])
        kb.op("dve", lambda e: e.memset(self.onesD[:], 1.0 / D), writes=[kb.buf("onesD")])
        kb.op("dve", lambda e: e.memset(self.ones_f[:], 1.0), writes=[kb.buf("ones_f")])

    def load_pv(self, l):
        kb = self.kb
        kb.dma("sp", self.pv[l % 2][:], self.pv_d[l], writes=[kb.buf("pv", l % 2)])

    def copy_x_in(self):
        kb = self.kb
        for kc in range(KC):
            kb.dma("sp", self.XT[kc * 128:(kc + 1) * 128, :], self.xT_in[kc * 128:(kc + 1) * 128, :],
                   writes=[kb.buf("XT", kc, tg) for tg in range(NTG)])

    def norm(self, l, pvoff, final=False):
        kb = self.kb
        pv = self.pv[l % 2]
        pvb = kb.buf("pv", l % 2)
        xt = [self.ar(i * S, S) for i in range(2)]
        sq = [self.ar((2 + i) * S, S) for i in range(2)]
        sqh = [self.ar((2 + i) * S, S // 2, BF16) for i in range(2)]
        rstd = self.ar(4 * S, S)
        xtb = [kb.buf("n_xt", i) for i in range(2)]
        sqb = [kb.buf("n_sq", i) for i in range(2)]
        rsb = [kb.buf("n_rstd", tg) for tg in range(NTG)]
        hT = self.hT()
        for kc in range(KC):
            xb = [kb.buf("XT", kc, tg) for tg in range(NTG)]
            kb.dma("pool", xt[kc % 2], self.XT[kc * 128:(kc + 1) * 128, :], reads=xb, writes=[xtb[kc % 2]])
            kb.op("act", lambda e: e.activation(sqh[kc % 2], xt[kc % 2], AF.Square),
                  reads=[xtb[kc % 2]], writes=[sqb[kc % 2]])
            for tg in range(NTG):
                kb.op("pe", lambda e: e.matmul(self.pb[2 + tg][:], self.onesD[:], sqh[kc % 2][:, tg * 512:(tg + 1) * 512],
                                               start=(kc == 0), stop=(kc == KC - 1)),
                      reads=[sqb[kc % 2], kb.buf("onesD")], writes=[self.pbuf[2 + tg]])
        for tg in range(NTG):
            sl = slice(tg * 512, (tg + 1) * 512)
            kb.op("dve", lambda e: e.tensor_scalar(rstd[:, sl], self.pb[2 + tg][:], 1e-6, None, ALU.add),
                  reads=[self.pbuf[2 + tg]], writes=[rsb[tg]])
            kb.op("act", lambda e: e.activation(rstd[:, sl], rstd[:, sl], AF.Sqrt), reads=[rsb[tg]], writes=[rsb[tg]])
            kb.op("dve", lambda e: e.reciprocal(rstd[:, sl], rstd[:, sl]), reads=[rsb[tg]], writes=[rsb[tg]])
        for kc in range(KC):
            xb = [kb.buf("XT", kc, tg) for tg in range(NTG)]
            kb.dma("pool", xt[kc % 2], self.XT[kc * 128:(kc + 1) * 128, :], reads=xb, writes=[xtb[kc % 2]])
            g = pv[:, pvoff + kc:pvoff + kc + 1]
            if not final:
                kb.op("dve", lambda e: e.scalar_tensor_tensor(hT[:, kc, :], xt[kc % 2], g, rstd, ALU.mult, ALU.mult),
                      reads=[xtb[kc % 2], pvb] + rsb, writes=[kb.buf("hT", kc)])
            else:
                kb.op("dve", lambda e: e.scalar_tensor_tensor(sq[kc % 2], xt[kc % 2], g, rstd, ALU.mult, ALU.mult),
                      reads=[xtb[kc % 2], pvb] + rsb, writes=[sqb[kc % 2]])
                kb.dma("sp", self.outT[kc * 128:(kc + 1) * 128, :], sq[kc % 2], reads=[sqb[kc % 2]],
                       writes=[kb.buf("outT", kc)])

    def load_w(self, slot, pieces):
        kb = self.kb
        bufs = []
        for i, (dst, src) in enumerate(pieces):
            b = kb.buf("wb", slot, i)
            kb.dma("pool", dst, src.rearrange("(k p) n -> p k n", p=128), writes=[b])
            bufs.append(b)
        return bufs

    def gbank(self):
        i = self.grot % 4
        self.grot += 1
        return self.pb[i], self.pbuf[i]

    def ffn_gateup(self, l, wname):
        kb = self.kb
        W = self.w[wname][l]
        hT = self.hT()
        hb = [kb.buf("hT", kc) for kc in range(KC)]
        sg = [self.ar(5 * S + i * 512, 512) for i in range(2)]
        sgb = [kb.buf("f_sg", i) for i in range(2)]
        ast = [self.ar(5 * S + 1024 + i * 1024, 1024, BF16) for i in range(2)]
        astb = [kb.buf("f_ast", i) for i in range(2)]
        rot = 0
        jcount = 0
        for blk in range(FF // 256):
            slot = self.wrot % 2
            self.wrot += 1
            wb = self.WB[slot][:, 0:KC * 512].rearrange("p (k n) -> p k n", k=KC)
            pieces = []
            for half in range(2):
                ks = slice(half * 8, (half + 1) * 8)
                rs = slice(half * 1024, (half + 1) * 1024)
                pieces.append((wb[:, ks, 0:256], W[rs, blk * 256:(blk + 1) * 256]))
                pieces.append((wb[:, ks, 256:512], W[rs, FF + blk * 256:FF + (blk + 1) * 256]))
            wbufs = self.load_w(slot, pieces)
            for jj in range(2):
                j = blk * 2 + jj
                a_s = ast[jcount % 2]
                a_b = astb[jcount % 2]
                jcount += 1
                for tg in range(NTG):
                    ts_ = slice(tg * 512, (tg + 1) * 512)
                    pg, pgb = self.gbank()
                    pu, pub = self.gbank()
                    kb.mm_group([(lambda e, kc=kc: e.matmul(pg[:], wb[:, kc, jj * 128:(jj + 1) * 128], hT[:, kc, ts_],
                                                            start=(kc == 0), stop=(kc == KC - 1))) for kc in range(KC)],
                                reads=hb + wbufs, writes=[pgb])
                    kb.mm_group([(lambda e, kc=kc: e.matmul(pu[:], wb[:, kc, 256 + jj * 128:256 + (jj + 1) * 128], hT[:, kc, ts_],
                                                            start=(kc == 0), stop=(kc == KC - 1))) for kc in range(KC)],
                                reads=hb + wbufs, writes=[pub])
                    s_ = sg[rot % 2]
                    s_b = sgb[rot % 2]
                    rot += 1
                    kb.op("act", lambda e: e.activation(s_, pg[:], AF.Silu), reads=[pgb], writes=[s_b])
                    kb.op("dve", lambda e: e.tensor_tensor(a_s[:, ts_], s_, pu[:], ALU.mult), reads=[s_b, pub], writes=[a_b])
                kb.dma("sp", self.AT[j * 128:(j + 1) * 128, :], a_s, reads=[a_b], writes=[kb.buf("AT", j)])

    def ffn_down(self, l, wname, scale=0.5):
        kb = self.kb
        W = self.w[wname][l]
        NJ = FF // 128
        aT = self.ACTT[:, 0:NJ * 1024].rearrange("p (j t) -> p j t", j=NJ)
        xo = [self.ar(i * 512, 512) for i in range(4)]
        xob = [kb.buf("d_xo", i) for i in range(4)]
        rot = 0
        for half in range(2):
            ab = []
            for j in range(NJ):
                b = kb.buf("aTh", j)
                kb.dma("sp", aT[:, j, :], self.AT[j * 128:(j + 1) * 128, half * 1024:(half + 1) * 1024],
                       reads=[kb.buf("AT", j)], writes=[b])
                ab.append(b)
            for dc in range(KC):
                slot = self.wrot % 2
                self.wrot += 1
                wb = self.WB[slot][:, 0:NJ * 128].rearrange("p (k n) -> p k n", k=NJ)
                pieces = []
                for q in range(4):
                    pieces.append((wb[:, q * 11:(q + 1) * 11, :], W[q * 11 * 128:(q + 1) * 11 * 128, dc * 128:(dc + 1) * 128]))
                wbufs = self.load_w(slot, pieces)
                for tgl in range(2):
                    tg = half * 2 + tgl
                    x_ = xo[rot % 4]
                    x_b = xob[rot % 4]
                    rot += 1
                    xdb = kb.buf("XT", dc, tg)
                    kb.dma("pool", x_, self.XT[dc * 128:(dc + 1) * 128, tg * 512:(tg + 1) * 512], reads=[xdb], writes=[x_b])
                    p_, p_b = self.gbank()
                    kb.mm_group([(lambda e, j=j: e.matmul(p_[:], wb[:, j, :], aT[:, j, tgl * 512:(tgl + 1) * 512],
                                                          start=(j == 0), stop=(j == NJ - 1))) for j in range(NJ)],
                                reads=ab + wbufs, writes=[p_b])
                    kb.op("dve", lambda e: e.scalar_tensor_tensor(x_, p_[:], float(scale), x_, ALU.mult, ALU.add),
                          reads=[p_b, x_b], writes=[x_b])
                    kb.dma("sp", self.XT[dc * 128:(dc + 1) * 128, tg * 512:(tg + 1) * 512], x_, reads=[x_b], writes=[xdb])

    def build(self):
        kb = self.kb
        self.load_consts()
        if self.mode.startswith("mix_in"):
            self.load_pv(0)
            which = self.mode.split(":")[1]
            if "a" in which:
                self.mixer_a(0)
                kb.barrier()
            if "b" in which:
                self.mixer_b(0)
                kb.barrier()
            if "c" in which:
                self.mixer_c(0)
                kb.barrier()
            return
        self.copy_x_in()
        for l in range(self.nl):
            self.load_pv(l)
            self.norm(l, PV_F1)
            self.ffn_gateup(l, "ffn1_w_gu")
            kb.barrier()
            self.ffn_down(l, "ffn1_w_down")
            kb.barrier()
            if self.mode == "ffn1":
                break
            self.norm(l, PV_MIX)
            self.win_gemm(l)
            kb.barrier()
            self.mixer_a(l)
            kb.barrier()
            self.mixer_b(l)
            kb.barrier()
            self.mixer_c(l)
            kb.barrier()
            self.merge_proj(l)
            kb.barrier()
            self.wout(l)
            kb.barrier()
            if self.mode == "mix":
                break
            self.norm(l, PV_F2)
            self.ffn_gateup(l, "ffn2_w_gu")
            kb.barrier()
            self.ffn_down(l, "ffn2_w_down")
            kb.barrier()
        self.norm(self.nl - 1, PV_FIN, final=True)


def make_in_maps(inputs, nl, n_cores=8, batch_ids=None):
    consts = make_consts()
    pv = np.stack([pack_pv(inputs, l) for l in range(nl)])
    bnorm = np.stack([np.ascontiguousarray(np.broadcast_to(inputs["b_norm"][l][None, :], (128, 128))) for l in range(nl)]).astype(np.float32)
    shared = {k: np.ascontiguousarray(inputs[k][:nl]) for k in W_SHAPES}
    shared.update({"pv": pv, "bnorm": bnorm, "c_ident": consts["ident"], "c_cmask": consts["cmask"],
                   "c_abias": consts["abias"], "c_sel": consts["sel"], "c_rmask": consts["rmask"]})
    if batch_ids is None:
        batch_ids = list(range(n_cores))
    maps = []
    for b in batch_ids:
        m = dict(shared)
        m["xT"] = np.ascontiguousarray(inputs["x"][b].T)
        maps.append(m)
    return maps


_PROG_CACHE = {}


def kernel(**inputs):
    inputs = {k: np.asarray(v) for k, v in inputs.items()}
    key = ("full", DEPTH)
    if key not in _PROG_CACHE:
        _PROG_CACHE[key] = Prog(DEPTH, "full")
    prog = _PROG_CACHE[key]
    maps = make_in_maps(inputs, DEPTH)
    res = run_bass_kernel_spmd(prog.nc, maps, core_ids=list(range(8)))
    out = np.stack([np.ascontiguousarray(r["outT"].T) for r in res.results]).astype(np.float32)
    return out


def win_chunks():
    ch = []
    for i in range(36 + 32):
        ch.append((i * 128, 128, ("C", i * 128)))
    ch.append((B0 + 4096, 16, ("C", B0 + 4096)))
    for (c0, n) in C_CHUNKS:
        ch.append((C0 + c0, n, ("C", C0 + c0)))
    for i in range(48):
        ch.append((G0 + i * 128, 128, ("G", i * 128)))
    return ch


def _win_gemm(self, l):
    kb = self.kb
    W = self.w["w_in"][l]
    hT = self.hT()
    hb = [kb.buf("hT", kc) for kc in range(KC)]
    chunks = win_chunks()
    blocks = []
    cur = []
    for c in chunks:
        if cur and (cur[0][0] + sum(x[1] for x in cur) == c[0]) and (sum(x[1] for x in cur) + c[1] <= 512):
            cur.append(c)
        else:
            if cur:
                blocks.append(cur)
            cur = [c]
    blocks.append(cur)
    st = [self.ar(i * S, S) for i in range(3)]
    stb = [kb.buf("w_st", i) for i in range(3)]
    rot = 0
    ecount = 0
    for blkc in blocks:
        c0 = blkc[0][0]
        bw = sum(x[1] for x in blkc)
        slot = self.wrot % 2
        self.wrot += 1
        wb = self.WB[slot][:, 0:KC * bw].rearrange("p (k n) -> p k n", k=KC)
        pieces = []
        for q in range(4):
            pieces.append((wb[:, q * 4:(q + 1) * 4, :], W[q * 512:(q + 1) * 512, c0:c0 + bw]))
        wbufs = self.load_w(slot, pieces)
        off = 0
        for (cc0, n, dest) in blkc:
            s_ = st[rot % 3]
            s_b = stb[rot % 3]
            rot += 1
            isg = dest[0] == "G"
            sv = s_.bitcast(BF16)[:, 0:S] if isg else s_
            for tg in range(NTG):
                ts_ = slice(tg * 512, (tg + 1) * 512)
                p_, p_b = self.gbank()
                kb.mm_group([(lambda e, kc=kc: e.matmul(p_[0:n, :], wb[:, kc, off:off + n], hT[:, kc, ts_],
                                                        start=(kc == 0), stop=(kc == KC - 1))) for kc in range(KC)],
                            reads=hb + wbufs, writes=[p_b])
                if isg:
                    kb.op("act", lambda e: e.activation(sv[0:n, ts_], p_[0:n, :], AF.Sigmoid), reads=[p_b], writes=[s_b])
                else:
                    eng = "act" if (ecount % 2 == 0) else "dve"
                    ecount += 1
                    if eng == "act":
                        kb.op("act", lambda e: e.activation(sv[0:n, ts_], p_[0:n, :], AF.Copy), reads=[p_b], writes=[s_b])
                    else:
                        kb.op("dve", lambda e: e.tensor_copy(sv[0:n, ts_], p_[0:n, :]), reads=[p_b], writes=[s_b])
            if isg:
                kb.dma("sp", self.GT[dest[1]:dest[1] + n, :], sv[0:n, :], reads=[s_b], writes=[kb.buf("GT", dest[1])])
            else:
                kb.dma("sp", self.COLS[dest[1]:dest[1] + n, :], sv[0:n, :], reads=[s_b], writes=[kb.buf("COLS", dest[1])])
            off += n


Prog.win_gemm = _win_gemm


def _mixer_a(self, l):
    kb = self.kb
    SCALE = 128 ** -0.5
    DIL = [1, 4, 16]
    abias = self.ar(0, 3072)
    oT = [self.ar(3072 + g * S, S) for g in range(3)]
    lB = [self.ar(3072 + (3 + g) * S, S) for g in range(3)]
    Mt = self.ar(3072 + 6 * S, S)
    sc0 = 8 * S
    s_sb = [self.ac(sc0 + i * 512, 512, F32) for i in range(2)]
    p_sb = [self.ac(sc0 + 1024 + i * 256, 256) for i in range(2)]
    pT_sb = [self.ac(sc0 + 1536 + i * 256, 256) for i in range(2)]
    o_n = [self.ac(sc0 + 2048 + i * 256, 256, F32) for i in range(2)]
    lrep = [self.ac(sc0 + 2560 + i * 256, 256, F32) for i in range(2)]
    cols_ = self.ac(sc0 + 3072, 64, F32)

    def att(i):
        return self.ACTT[:, i * S:(i + 1) * S]
    qkv = [[att(hb * 3 + j) for j in range(3)] for hb in range(2)]
    vtok = att(6)
    ybf = att(7)
    b_abias = kb.buf("a_abias")
    kb.dma("sp", abias, self.c_abias[:, :], writes=[b_abias])
    identb = kb.buf("ident_b")
    identf = kb.buf("ident_f")
    blkrot = 0
    hcount = 0
    for slot in range(4):
        for g in range(3):
            h = g * 4 + slot
            d = DIL[g]
            nb = 16 // d
            hb = hcount % 2
            hcount += 1
            q_, k_, v_ = qkv[hb]
            bq, bk, bv = [kb.buf("a_qkv", hb, j) for j in range(3)]
            for j, (t_, b_) in enumerate(((q_, bq), (k_, bk), (v_, bv))):
                r0 = j * 1536 + h * 128
                kb.dma("pool", t_, self.COLS[r0:r0 + 128, :], reads=[kb.buf("COLS", r0)], writes=[b_])

            def sl(start, cnt):
                return slice(start, start + (cnt - 1) * d + 1, d) if d > 1 else slice(start, start + cnt)
            bvt = kb.buf("a_vtok")
            blist = [(r, b) for r in range(d) for b in range(nb)]
            for q4 in range(4):
                pT = self.psT[q4 % 2]
                pTb = self.psTb[q4 % 2]
                for i4 in range(4):
                    r, b = blist[q4 * 4 + i4]
                    st0 = 128 * b * d + r
                    kb.op("pe", lambda e: e.transpose(pT[:, i4 * 128:(i4 + 1) * 128], v_[:, sl(st0, 128)], self.ident_b[:]),
                          reads=[bv, identb], writes=[pTb])
                if q4 % 2:
                    kb.op("act", lambda e: e.activation(vtok[:, q4 * 512:(q4 + 1) * 512], pT[:, 0:512], AF.Copy), writes=[pTb, bvt])
                else:
                    kb.op("dve", lambda e: e.tensor_copy(vtok[:, q4 * 512:(q4 + 1) * 512], pT[:, 0:512]), writes=[pTb, bvt])
            boT = kb.buf("a_oT", g)
            blB = kb.buf("a_lB", g)
            def setup(r, b, u):
                v = dict(u=u)
                bi = r * nb + b
                st0 = 128 * b * d + r
                v["st0"] = st0
                v["qs"] = q_[:, sl(st0, 128)]
                if b >= 1:
                    v["nk"] = 256
                    v["ks"] = k_[:, sl(st0 - 128 * d, 256)]
                    v["bias"] = abias[:, h * 256:(h + 1) * 256]
                    v["kblks"] = [bi - 1, bi]
                else:
                    v["nk"] = 128
                    v["ks"] = k_[:, sl(st0, 128)]
                    v["bias"] = abias[:, h * 256 + 128:(h + 1) * 256]
                    v["kblks"] = [bi]
                v["ps_s"], v["ps_sb"] = self.pb[u], self.pbuf[u]
                v["ps_o"], v["ps_ob"] = self.pb[2 + u], self.pbuf[2 + u]
                v["ps_m"], v["ps_mb"] = self.pb[4 + u], self.pbuf[4 + u]
                v["pT"], v["pTb"] = self.psT[u], self.psTb[u]
                v["cb"] = kb.buf("a_cols", u)
                v["c"] = [cols_[:, u * 8 + i:u * 8 + i + 1] for i in range(6)]
                v["bs"] = kb.buf("a_s", u)
                v["ss"] = s_sb[u][:, 0:v["nk"]]
                v["bp"] = kb.buf("a_p", u)
                v["pp"] = p_sb[u][:, 0:v["nk"]]
                v["bpt"] = kb.buf("a_pT", u)
                v["bon"] = kb.buf("a_on", u)
                v["blr"] = kb.buf("a_lrep", u)
                return v

            def st1(v):
                kb.op("pe", lambda e: e.matmul(v["ps_s"][:, 0:v["nk"]], v["qs"], v["ks"], start=True, stop=True), reads=[bq, bk], writes=[v["ps_sb"]])

            def st2(v):
                c_m, c_nm = v["c"][0], v["c"][1]
                kb.op("dve", lambda e: e.scalar_tensor_tensor(v["ss"], v["ps_s"][:, 0:v["nk"]], SCALE, v["bias"], ALU.mult, ALU.add),
                      reads=[b_abias], writes=[v["ps_sb"], v["bs"]])
                kb.op("dve", lambda e: e.tensor_reduce(c_m, v["ss"], AX.X, ALU.max), reads=[v["bs"]], writes=[v["cb"]])
                kb.op("dve", lambda e: e.tensor_scalar(c_nm, c_m, -1.0, None, ALU.mult), reads=[v["cb"]], writes=[v["cb"]])

            def st3(v):
                c_m, c_nm, c_sum, c_ln, c_lse, c_rs = v["c"]
                kb.op("act", lambda e: e.activation(v["pp"], v["ss"], AF.Exp, bias=c_nm, scale=1.0, accum_out=c_sum),
                      reads=[v["bs"], v["cb"]], writes=[v["bp"], v["cb"]])
                kb.op("act", lambda e: e.activation(c_ln, c_sum, AF.Ln), reads=[v["cb"]], writes=[v["cb"]])

            def st4(v):
                c_m, c_nm, c_sum, c_ln, c_lse, c_rs = v["c"]
                kb.op("dve", lambda e: e.tensor_tensor(c_lse, c_ln, c_m, ALU.add), reads=[v["cb"]], writes=[v["cb"]])
                kb.op("dve", lambda e: e.reciprocal(c_rs, c_sum), reads=[v["cb"]], writes=[v["cb"]])
                kb.op("dve", lambda e: e.tensor_scalar(lrep[v["u"]], self.ones_f[:], c_lse, None, ALU.mult),
                      reads=[v["cb"], kb.buf("ones_f")], writes=[v["blr"]])

            def st5(v):
                for kk in range(v["nk"] // 128):
                    kb.op("pe", lambda e: e.transpose(v["pT"][:, kk * 128:(kk + 1) * 128], v["pp"][:, kk * 128:(kk + 1) * 128], self.ident_b[:]),
                          reads=[v["bp"], identb], writes=[v["pTb"]])

            def st6(v):
                kb.op("act", lambda e: e.activation(pT_sb[v["u"]][:, 0:v["nk"]], v["pT"][:, 0:v["nk"]], AF.Copy), writes=[v["pTb"], v["bpt"]])

            def st7(v):
                kblks = v["kblks"]
                kb.mm_group([(lambda e, kk=kk: e.matmul(v["ps_o"][:, 0:128], pT_sb[v["u"]][:, kk * 128:(kk + 1) * 128],
                                                        vtok[:, kblks[kk] * 128:(kblks[kk] + 1) * 128],
                                                        start=(kk == 0), stop=(kk == len(kblks) - 1)))
                             for kk in range(len(kblks))], reads=[v["bpt"], bvt], writes=[v["ps_ob"]])

            def st8(v):
                kb.op("act", lambda e: e.activation(o_n[v["u"]], v["ps_o"][:, 0:128], AF.Copy, scale=v["c"][5]),
                      reads=[v["cb"]], writes=[v["ps_ob"], v["bon"]])

            def st9(v):
                kb.op("pe", lambda e: e.transpose(v["ps_m"][:, 0:128], o_n[v["u"]], self.ident_f[:]), reads=[v["bon"], identf], writes=[v["ps_mb"]])
                kb.op("pe", lambda e: e.matmul(v["ps_m"][:, 128:256], lrep[v["u"]], self.ident_f[:], start=True, stop=True),
                      reads=[v["blr"], identf], writes=[v["ps_mb"]])

            def st10(v):
                kb.op("dve", lambda e: e.tensor_copy(oT[g][:, sl(v["st0"], 128)], v["ps_m"][:, 0:128]), writes=[v["ps_mb"], boT])
                kb.op("act", lambda e: e.activation(lB[g][:, sl(v["st0"], 128)], v["ps_m"][:, 128:256], AF.Copy), writes=[v["ps_mb"], blB])

            for i2 in range(0, len(blist), 2):
                pair = [setup(blist[i2 + q][0], blist[i2 + q][1], q) for q in range(2)]
                for stg in (st1, st2, st3, st4, st5, st6, st7, st8, st9, st10):
                    for v in pair:
                        stg(v)
        bo = [kb.buf("a_oT", g) for g in range(3)]
        bl = [kb.buf("a_lB", g) for g in range(3)]
        bM = kb.buf("a_M")
        kb.op("dve", lambda e: e.tensor_tensor(Mt, lB[0], lB[1], ALU.max), reads=[bl[0], bl[1]], writes=[bM])
        kb.op("dve", lambda e: e.tensor_tensor(Mt, Mt, lB[2], ALU.max), reads=[bM, bl[2]], writes=[bM])
        for g in range(3):
            kb.op("dve", lambda e: e.tensor_tensor(lB[g], lB[g], Mt, ALU.subtract), reads=[bl[g], bM], writes=[bl[g]])
            kb.op("act", lambda e: e.activation(lB[g], lB[g], AF.Exp), reads=[bl[g]], writes=[bl[g]])
        kb.op("dve", lambda e: e.tensor_tensor(Mt, lB[0], lB[1], ALU.add), reads=[bl[0], bl[1], bM], writes=[bM])
        kb.op("dve", lambda e: e.tensor_tensor(Mt, Mt, lB[2], ALU.add), reads=[bM, bl[2]], writes=[bM])
        kb.op("dve", lambda e: e.reciprocal(Mt, Mt), reads=[bM], writes=[bM])
        for g in range(3):
            kb.op("dve", lambda e: e.tensor_tensor(oT[g], oT[g], lB[g], ALU.mult), reads=[bo[g], bl[g]], writes=[bo[g]])
        kb.op("dve", lambda e: e.tensor_tensor(oT[0], oT[0], oT[1], ALU.add), reads=[bo[0], bo[1]], writes=[bo[0]])
        kb.op("dve", lambda e: e.tensor_tensor(oT[0], oT[0], oT[2], ALU.add), reads=[bo[0], bo[2]], writes=[bo[0]])
        by = kb.buf("a_ybf")
        kb.op("dve", lambda e: e.tensor_tensor(ybf, oT[0], Mt, ALU.mult), reads=[bo[0], bM], writes=[by])
        kb.dma("sp", self.YT[slot * 128:(slot + 1) * 128, :], ybf, reads=[by], writes=[kb.buf("YT", slot)])


Prog.mixer_a = _mixer_a


def _neumann(self, X0, X0T, bX, P, bP, scr, bscr, banks):
    self.neumann_multi([dict(X=X0, XT=X0T, bX=bX, P=P, bP=bP, scr=scr, bscr=bscr, banks=banks)])


def _neumann_multi(self, chains):
    kb = self.kb
    identf = kb.buf("ident_f")
    st = []
    for c in chains:
        st.append(dict(X=c["X"], XT=c["XT"], Xn=c["scr"][0], XTn=c["scr"][1], bX=c["bX"], bscr=c["bscr"],
                       P=c["P"], bP=c["bP"], banks=c["banks"]))
    for c in st:
        kb.op("dve", lambda e: e.tensor_tensor(c["P"], c["X"], self.ident_f[:], ALU.add), reads=[c["bX"], identf], writes=[c["bP"]])
    for lvl in range(5):
        last = lvl == 4
        for c in st:
            (pa, pab), (pbk, pbb), (pc, pcb) = c["banks"]
            if not last:
                kb.op("pe", lambda e: e.matmul(pa[:, 0:128], c["XT"], c["X"], start=True, stop=True), reads=[c["bX"]], writes=[pab])
            kb.op("pe", lambda e: e.matmul(pbk[:, 0:128], c["X"], c["XT"], start=True, stop=True), reads=[c["bX"]], writes=[pbb])
        for c in st:
            (pa, pab), (pbk, pbb), (pc, pcb) = c["banks"]
            if not last:
                kb.op("act", lambda e: e.activation(c["Xn"], pa[:, 0:128], AF.Copy), writes=[pab, c["bscr"]])
            kb.op("dve", lambda e: e.tensor_copy(c["XTn"], pbk[:, 0:128]), writes=[pbb, c["bscr"]])
        for c in st:
            (pa, pab), (pbk, pbb), (pc, pcb) = c["banks"]
            kb.op("pe", lambda e: e.matmul(pc[:, 0:128], c["XTn"], c["P"], start=True, stop=True), reads=[c["bscr"], c["bP"]], writes=[pcb])
        for c in st:
            (pa, pab), (pbk, pbb), (pc, pcb) = c["banks"]
            kb.op("dve", lambda e: e.tensor_tensor(c["P"], c["P"], pc[:, 0:128], ALU.add), reads=[c["bP"]], writes=[pcb, c["bP"]])
        for c in st:
            c["X"], c["XT"], c["Xn"], c["XTn"] = c["Xn"], c["XTn"], c["X"], c["XT"]
            c["bX"], c["bscr"] = c["bscr"], c["bX"]


Prog.neumann_multi = _neumann_multi
Prog.neumann = _neumann


def _mixer_b(self, l):
    kb = self.kb
    pv = self.pv[l % 2]
    pvb = kb.buf("pv", l % 2)
    U = [self.ar(i * S, S) for i in range(9)]
    T = [self.ac(i * S, S) for i in range(12)]
    qT, qdT, kT, kbT, vT, zs, wT, attT, kdec, kbg, vb, yT = T
    o3 = lambda t: t.rearrange("p (b d) -> p b d", b=16)
    sc0 = 12 * S

    def m128(i):
        return self.ac(sc0 + i * 256, 256, F32)
    Dt, Dl, t1, Xa, XaT, Pm, Xb, XbT, on_ = [m128(i) for i in range(9)]
    SET1 = [m128(40 + i) for i in range(8)]
    TTb1 = self.ac(sc0 + 48 * 256, 128)
    TTb = self.ac(sc0 + 9 * 256, 128)
    vnew = self.ac(sc0 + 9 * 256 + 128, 128)
    Sst = m128(10)
    Sbf = self.ac(sc0 + 11 * 256, 128)
    junk = m128(12)
    TQ = self.ac(sc0 + 13 * 256, 1024, F32).rearrange("p (b c) -> p b c", b=16)
    selt = self.ac(sc0 + 13 * 256 + 1024, 2048, F32)
    bnrep = self.ac(sc0 + 13 * 256 + 3072, 256, F32)
    cols_ = self.ac(sc0 + 13 * 256 + 3328, 64, F32)
    cm = self.cmask
    mU_s, mU_i, mL_s, nU_i, nL_i = [cm[:, i * 128:(i + 1) * 128] for i in range(5)]
    bcm = kb.buf("cmask")
    identf = kb.buf("ident_f")
    identb = kb.buf("ident_b")
    onesb = kb.buf("ones_f")
    R_g, R_be, R_x, R_bg, R_kd = U[0], U[1], U[2], U[3], U[4]
    bRg, bRbe, bRx, bRbg, bRkd = [kb.buf("b_R", i) for i in range(5)]
    bsel = kb.buf("b_sel")
    bbn = kb.buf("b_bn")
    bcols = kb.buf("b_cols")
    bTQ = kb.buf("b_TQ")
    abrow = B0 + 4096
    abbuf = kb.buf("COLS", abrow)
    kb.dma("sp", selt[0:8, 0:1024], self.c_sel[0:8, 0:1024], writes=[bsel])
    kb.dma("sp", bnrep, self.bnorm_d[l], writes=[bbn])
    kb.dma("sp", R_x[0:8, :], self.COLS[abrow:abrow + 8, :], reads=[abbuf], writes=[bRx])
    kb.dma("sp", R_be[0:8, :], self.COLS[abrow + 8:abrow + 16, :], reads=[abbuf], writes=[bRbe])
    kb.dma("sp", R_kd[0:8, :], self.c_rmask[0:8, :], writes=[bRkd])
    nea = cols_[0:8, 0:1]
    kb.op("act", lambda e: e.activation(nea, pv[0:8, PV_ALOG:PV_ALOG + 1], AF.Exp), reads=[pvb], writes=[bcols])
    kb.op("dve", lambda e: e.tensor_scalar(nea, nea, -1.0, None, ALU.mult), reads=[bcols], writes=[bcols])
    x8, be8, g8, bg8, kd8 = R_x[0:8, :], R_be[0:8, :], R_g[0:8, :], R_bg[0:8, :], R_kd[0:8, :]
    kb.op("dve", lambda e: e.tensor_scalar(x8, x8, pv[0:8, PV_DTB:PV_DTB + 1], None, ALU.add), reads=[bRx, pvb], writes=[bRx])
    kb.op("act", lambda e: e.activation(bg8, x8, AF.Abs), reads=[bRx], writes=[bRbg])
    kb.op("act", lambda e: e.activation(bg8, bg8, AF.Exp, scale=-1.0), reads=[bRbg], writes=[bRbg])
    kb.op("act", lambda e: e.activation(bg8, bg8, AF.Ln, bias=1.0), reads=[bRbg], writes=[bRbg])
    kb.op("dve", lambda e: e.scalar_tensor_tensor(bg8, x8, 0.0, bg8, ALU.max, ALU.add), reads=[bRx, bRbg], writes=[bRbg])
    kb.op("dve", lambda e: e.tensor_scalar(bg8, bg8, nea, None, ALU.mult), reads=[bRbg, bcols], writes=[bRbg])
    kb.op("dve", lambda e: e.tensor_tensor_scan(g8, kd8, bg8, 0.0, ALU.mult, ALU.add), reads=[bRkd, bRbg], writes=[bRg])
    kb.op("act", lambda e: e.activation(be8, be8, AF.Sigmoid), reads=[bRbe], writes=[bRbe])
    kb.op("act", lambda e: e.activation(x8, g8, AF.Exp), reads=[bRg], writes=[bRx])
    kb.op("dve", lambda e: e.tensor_tensor(bg8, be8, x8, ALU.mult), reads=[bRbe, bRx], writes=[bRbg])
    for c in range(32):
        cs = slice(c * 64, (c + 1) * 64)
        kb.op("dve", lambda e: e.tensor_scalar(kd8[:, cs], g8[:, cs], g8[:, c * 64 + 63:c * 64 + 64], -1.0, ALU.subtract, ALU.mult),
              reads=[bRg], writes=[bRkd])
    kb.op("act", lambda e: e.activation(kd8, kd8, AF.Exp), reads=[bRkd], writes=[bRkd])
    for blk in range(16):
        tsl = slice(blk * 128, (blk + 1) * 128)
        p_, p_b = self.pb[blk % 2], self.pbuf[blk % 2]
        for qi, (rr, rb) in enumerate(((g8, bRg), (be8, bRbe), (bg8, bRbg), (kd8, bRkd))):
            kb.op("pe", lambda e: e.matmul(p_[:, qi * 8:(qi + 1) * 8], rr[:, tsl], self.ident_f[0:8, 0:8], start=True, stop=True),
                  reads=[rb, identf], writes=[p_b])
        kb.op("act", lambda e: e.activation(TQ[:, blk, :], p_[:, 0:32], AF.Copy), writes=[p_b, bTQ])
    gB, egB, beB, Xs, Ys, u_all, o_all = U[2], U[3], U[4], U[5], U[6], U[7], U[8]
    bgB, begB, bbeB, bXs, bYs, bu, bo = [kb.buf("b_U", i) for i in range(7)]
    for h in range(8):
        for tg in range(NTG):
            ts_ = slice(tg * 512, (tg + 1) * 512)
            p_, p_b = self.pb[tg % 2], self.pbuf[tg % 2]
            kb.op("pe", lambda e: e.matmul(p_[:, :], selt[0:8, h * 128:(h + 1) * 128], g8[:, ts_], start=True, stop=True),
                  reads=[bsel, bRg], writes=[p_b])
            kb.op("dve", lambda e: e.tensor_copy(gB[:, ts_], p_[:, :]), writes=[p_b, bgB, bRx])
            kb.op("act", lambda e: e.activation(egB[:, ts_], p_[:, :], AF.Exp), writes=[p_b, begB, bRbg])
            p2, p2b = self.pb[2 + tg % 2], self.pbuf[2 + tg % 2]
            kb.op("pe", lambda e: e.matmul(p2[:, :], selt[0:8, h * 128:(h + 1) * 128], be8[:, ts_], start=True, stop=True),
                  reads=[bsel, bRbe], writes=[p2b])
            kb.op("act", lambda e: e.activation(beB[:, ts_], p2[:, :], AF.Copy), writes=[p2b, bbeB, bRkd])
        for j in range(3):
            r0 = B0 + j * 1024 + h * 128
            kb.dma("sp", Xs, self.COLS[r0:r0 + 128, :], reads=[kb.buf("COLS", r0)], writes=[bXs])
            cc = j * 8 + h
            wc = [pv[:, PV_CONV + cc * 4 + t:PV_CONV + cc * 4 + t + 1] for t in range(4)]
            kb.op("dve", lambda e: e.tensor_scalar(Ys, Xs, wc[3], None, ALU.mult), reads=[bXs, pvb], writes=[bYs])
            for sh in (1, 2, 3):
                kb.op("dve", lambda e: e.scalar_tensor_tensor(Ys[:, sh:], Xs[:, 0:S - sh], wc[3 - sh], Ys[:, sh:], ALU.mult, ALU.add),
                      reads=[bXs, pvb, bYs], writes=[bYs])
            kb.op("act", lambda e: e.activation(Ys, Ys, AF.Silu), reads=[bYs], writes=[bYs])
            if j < 2:
                kb.op("dve", lambda e: e.tensor_tensor(Xs, Ys, Ys, ALU.mult), reads=[bYs], writes=[bXs])
                for tg in range(NTG):
                    ts_ = slice(tg * 512, (tg + 1) * 512)
                    p_, p_b = self.pb[4 + tg % 2], self.pbuf[4 + tg % 2]
                    kb.op("pe", lambda e: e.matmul(p_[:, :], self.ones_f[:], Xs[:, ts_], start=True, stop=True),
                          reads=[bXs, onesb], writes=[p_b])
                    kb.op("dve", lambda e: e.tensor_scalar(Xs[:, ts_], p_[:, :], 1e-6, None, ALU.add), reads=[bXs], writes=[p_b, bXs])
                kb.op("act", lambda e: e.activation(Xs, Xs, AF.Sqrt), reads=[bXs], writes=[bXs])
                kb.op("dve", lambda e: e.reciprocal(Xs, Xs), reads=[bXs], writes=[bXs])
                if j == 0:
                    kb.op("dve", lambda e: e.scalar_tensor_tensor(Ys, Ys, 128 ** -0.5, Xs, ALU.mult, ALU.mult), reads=[bYs, bXs], writes=[bYs])
                    kb.op("act", lambda e: e.activation(qT, Ys, AF.Copy), reads=[bYs], writes=[kb.buf("b_T", 0)])
                    kb.op("dve", lambda e: e.tensor_tensor(qdT, Ys, egB, ALU.mult), reads=[bYs, begB], writes=[kb.buf("b_T", 1)])
                else:
                    kb.op("dve", lambda e: e.tensor_tensor(Ys, Ys, Xs, ALU.mult), reads=[bYs, bXs], writes=[bYs])
                    kb.op("act", lambda e: e.activation(kT, Ys, AF.Copy), reads=[bYs], writes=[kb.buf("b_T", 2)])
                    kb.op("dve", lambda e: e.tensor_tensor(kbT, Ys, beB, ALU.mult), reads=[bYs, bbeB], writes=[kb.buf("b_T", 3)])
            else:
                kb.op("act", lambda e: e.activation(vT, Ys, AF.Copy), reads=[bYs], writes=[kb.buf("b_T", 4)])
        r0 = B0 + 3072 + h * 128
        kb.dma("sp", Xs, self.COLS[r0:r0 + 128, :], reads=[kb.buf("COLS", r0)], writes=[bXs])
        kb.op("act", lambda e: e.activation(zs, Xs, AF.Silu), reads=[bXs], writes=[kb.buf("b_T", 5)])
        bq, bqd, bk, bkb, bv, bz, bw, batt, bkd, bkbg, bvb, by = [kb.buf("b_T", i) for i in range(12)]
        for blk in range(16):
            tsl = slice(blk * 128, (blk + 1) * 128)
            pT, pTb = self.psT[blk % 2], self.psTb[blk % 2]
            kb.op("pe", lambda e: e.transpose(pT[:, 0:128], kT[:, tsl], self.ident_b[:]), reads=[bk, identb], writes=[pTb])
            kb.op("pe", lambda e: e.transpose(pT[:, 128:256], vT[:, tsl], self.ident_b[:]), reads=[bv, identb], writes=[pTb])
            kb.op("act", lambda e: e.activation(kdec[:, tsl], pT[:, 0:128], AF.Copy, scale=TQ[:, blk, 24 + h:25 + h]),
                  reads=[bTQ], writes=[pTb, bkd])
            kb.op("dve", lambda e: e.tensor_scalar(kbg[:, tsl], pT[:, 0:128], TQ[:, blk, 16 + h:17 + h], None, ALU.mult),
                  reads=[bTQ], writes=[pTb, bkbg])
            kb.op("act", lambda e: e.activation(vb[:, tsl], pT[:, 128:256], AF.Copy, scale=TQ[:, blk, 8 + h:9 + h]),
                  reads=[bTQ], writes=[pTb, bvb])
        bm = [kb.buf("b_m", i) for i in range(8)]
        bDt, bDl, bt1, bXa, bXb, bP, bTTb, bon = bm
        bm1 = [kb.buf("b_m1", i) for i in range(8)]
        sets = [dict(Dt=Dt, Dl=Dl, t1=t1, Xa=Xa, XaT=XaT, Pm=Pm, Xb=Xb, XbT=XbT, TTb=TTb,
                     bDt=bDt, bDl=bDl, bt1=bt1, bXa=bXa, bXb=bXb, bP=bP, bTTb=bTTb, banks=(0, 1, 2)),
                dict(Dt=SET1[0], Dl=SET1[1], t1=SET1[2], Xa=SET1[3], XaT=SET1[4], Pm=SET1[5], Xb=SET1[6], XbT=SET1[7], TTb=TTb1,
                     bDt=bm1[0], bDl=bm1[1], bt1=bm1[2], bXa=bm1[3], bXb=bm1[4], bP=bm1[5], bTTb=bm1[6], banks=(3, 4, 5))]
        for bp in range(8):
            chains = []
            for q in range(2):
                blk = bp * 2 + q
                z_ = sets[q]
                tsl = slice(blk * 128, (blk + 1) * 128)
                gcol = TQ[:, blk, h:h + 1]
                (p0, p0b), (p1, p1b), (p2, p2b) = [(self.pb[i], self.pbuf[i]) for i in z_["banks"]]
                kb.op("pe", lambda e: e.matmul(p0[:, 0:128], kT[:, tsl], kbT[:, tsl], start=True, stop=True), reads=[bk, bkb], writes=[p0b])
                kb.op("pe", lambda e: e.matmul(p1[:, 0:128], kbT[:, tsl], kT[:, tsl], start=True, stop=True), reads=[bk, bkb], writes=[p1b])
                kb.op("pe", lambda e: e.matmul(p2[:, 0:128], kT[:, tsl], qT[:, tsl], start=True, stop=True), reads=[bk, bq], writes=[p2b])
                kb.op("dve", lambda e: e.tensor_scalar(z_["Dt"], gB[:, tsl], gcol, None, ALU.subtract), reads=[bgB, bTQ], writes=[z_["bDt"]])
                kb.op("dve", lambda e: e.tensor_tensor(z_["Dt"], z_["Dt"], nU_i, ALU.add), reads=[z_["bDt"], bcm], writes=[z_["bDt"]])
                kb.op("act", lambda e: e.activation(z_["Dt"], z_["Dt"], AF.Exp), reads=[z_["bDt"]], writes=[z_["bDt"]])
                kb.op("dve", lambda e: e.tensor_scalar(z_["Dl"], gB[:, tsl], gcol, -1.0, ALU.subtract, ALU.mult), reads=[bgB, bTQ], writes=[z_["bDl"]])
                kb.op("dve", lambda e: e.tensor_tensor(z_["Dl"], z_["Dl"], nL_i, ALU.add), reads=[z_["bDl"], bcm], writes=[z_["bDl"]])
                kb.op("act", lambda e: e.activation(z_["Dl"], z_["Dl"], AF.Exp), reads=[z_["bDl"]], writes=[z_["bDl"]])
                kb.op("dve", lambda e: e.tensor_tensor(z_["t1"], z_["Dt"], p0[:, 0:128], ALU.mult), reads=[z_["bDt"]], writes=[p0b, z_["bt1"]])
                kb.op("dve", lambda e: e.scalar_tensor_tensor(z_["Xa"], z_["t1"], -1.0, mU_s, ALU.mult, ALU.mult), reads=[z_["bt1"], bcm], writes=[z_["bXa"]])
                kb.op("dve", lambda e: e.tensor_tensor(z_["t1"], z_["Dl"], p1[:, 0:128], ALU.mult), reads=[z_["bDl"]], writes=[p1b, z_["bt1"]])
                kb.op("dve", lambda e: e.scalar_tensor_tensor(z_["XaT"], z_["t1"], -1.0, mL_s, ALU.mult, ALU.mult), reads=[z_["bt1"], bcm], writes=[z_["bXa"]])
                kb.op("dve", lambda e: e.tensor_tensor(attT[:, tsl], z_["Dt"], p2[:, 0:128], ALU.mult), reads=[z_["bDt"]], writes=[p2b, batt])
                chains.append(dict(X=z_["Xa"], XT=z_["XaT"], bX=z_["bXa"], P=z_["Pm"], bP=z_["bP"], scr=(z_["Xb"], z_["XbT"]), bscr=z_["bXb"],
                                   banks=((p0, p0b), (p1, p1b), (p2, p2b))))
            self.neumann_multi(chains)
            for q in range(2):
                blk = bp * 2 + q
                z_ = sets[q]
                tsl = slice(blk * 128, (blk + 1) * 128)
                (p0, p0b), (p1, p1b), (p2, p2b) = [(self.pb[i], self.pbuf[i]) for i in z_["banks"]]
                kb.op("act", lambda e: e.activation(z_["TTb"], z_["Pm"], AF.Copy), reads=[z_["bP"]], writes=[z_["bTTb"]])
                kb.op("pe", lambda e: e.matmul(p0[:, 0:128], z_["TTb"], vb[:, tsl], start=True, stop=True), reads=[z_["bTTb"], bvb], writes=[p0b])
                kb.op("pe", lambda e: e.matmul(p1[:, 0:128], kbg[:, tsl], z_["TTb"], start=True, stop=True), reads=[z_["bTTb"], bkbg], writes=[p1b])
                kb.op("act", lambda e: e.activation(u_all[:, tsl], p0[:, 0:128], AF.Copy), writes=[p0b, bu])
                kb.op("dve", lambda e: e.tensor_copy(wT[:, tsl], p1[:, 0:128]), writes=[p1b, bw])
        bS = kb.buf("b_S")
        Sbfs = [Sbf, self.ac(sc0 + 11 * 256 + 128, 128)]
        bSbs = [kb.buf("b_Sbf", 0), kb.buf("b_Sbf", 1)]
        bvn = kb.buf("b_vnew")
        kb.op("dve", lambda e: e.memset(Sst, 0.0), writes=[bS])
        kb.op("dve", lambda e: e.memset(Sbfs[0], 0.0), writes=[bSbs[0]])
        for c in range(32):
            blk, ch = c // 2, c % 2
            tsl = slice(blk * 128, (blk + 1) * 128)
            rs = slice(ch * 64, (ch + 1) * 64)
            tlast = c * 64 + 63
            Sc, bSc = Sbfs[c % 2], bSbs[c % 2]
            Sn, bSn = Sbfs[(c + 1) % 2], bSbs[(c + 1) % 2]
            p0, p0b = self.pb[0], self.pbuf[0]
            p1, p1b = self.pb[1], self.pbuf[1]
            p2, p2b = self.pb[2], self.pbuf[2]
            kb.op("pe", lambda e: e.matmul(p0[:, 0:128], wT[:, tsl], Sc, start=True, stop=True), reads=[bw, bSc], writes=[p0b])
            kb.op("dve", lambda e: e.tensor_tensor(vnew[rs, :], u_all[rs, tsl], p0[rs, 0:128], ALU.subtract), reads=[bu], writes=[p0b, bvn])
            kb.op("pe", lambda e: e.matmul(p2[:, 0:128], kdec[rs, tsl], vnew[rs, :], start=True, stop=True), reads=[bkd, bvn], writes=[p2b])
            kb.mm_group([lambda e: e.matmul(p1[:, 0:128], qdT[:, tsl], Sc, start=True, stop=False),
                         lambda e: e.matmul(p1[:, 0:128], attT[rs, tsl], vnew[rs, :], start=False, stop=True)],
                        reads=[bqd, bSc, batt, bvn], writes=[p1b])
            kb.op("dve", lambda e: e.scalar_tensor_tensor(Sn, Sst, egB[:, tlast:tlast + 1], p2[:, 0:128], ALU.mult, ALU.add),
                  reads=[bS, begB], writes=[p2b, bSn])
            kb.op("dve", lambda e: e.scalar_tensor_tensor(Sst, Sst, egB[:, tlast:tlast + 1], p2[:, 0:128], ALU.mult, ALU.add),
                  reads=[bS, begB], writes=[p2b, bS])
            kb.op("act", lambda e: e.activation(o_all[rs, tsl], p1[rs, 0:128], AF.Copy), writes=[p1b, bo])
        ons = [on_, SET1[0]]
        bons = [bon, bm1[0]]
        jks = [junk, SET1[1]]
        bjks = [kb.buf("b_junk"), bm1[1]]
        bmsc = [kb.buf("b_msc", 0), kb.buf("b_msc", 1)]
        for bp in range(8):
            pv_ = []
            for q in range(2):
                blk = bp * 2 + q
                pv_.append(dict(tsl=slice(blk * 128, (blk + 1) * 128), ms=cols_[:, 8 + q * 2:9 + q * 2], on=ons[q], bon=bons[q],
                                jk=jks[q], bjk=bjks[q], bc=bmsc[q], p3=self.pb[3 + q], p3b=self.pbuf[3 + q]))
            for w_ in pv_:
                kb.op("act", lambda e: e.activation(w_["jk"], o_all[:, w_["tsl"]], AF.Square, accum_out=w_["ms"]), reads=[bo], writes=[w_["bjk"], w_["bc"]])
            for w_ in pv_:
                kb.op("dve", lambda e: e.tensor_scalar(w_["ms"], w_["ms"], 1.0 / 128, 1e-6, ALU.mult, ALU.add), reads=[w_["bc"]], writes=[w_["bc"]])
            for w_ in pv_:
                kb.op("act", lambda e: e.activation(w_["ms"], w_["ms"], AF.Sqrt), reads=[w_["bc"]], writes=[w_["bc"]])
            for w_ in pv_:
                kb.op("dve", lambda e: e.reciprocal(w_["ms"], w_["ms"]), reads=[w_["bc"]], writes=[w_["bc"]])
            for w_ in pv_:
                kb.op("dve", lambda e: e.scalar_tensor_tensor(w_["on"], o_all[:, w_["tsl"]], w_["ms"], bnrep, ALU.mult, ALU.mult),
                      reads=[bo, w_["bc"], bbn], writes=[w_["bon"]])
            for w_ in pv_:
                kb.op("pe", lambda e: e.transpose(w_["p3"][:, 0:128], w_["on"], self.ident_f[:]), reads=[w_["bon"], identf], writes=[w_["p3b"]])
            for w_ in pv_:
                kb.op("dve", lambda e: e.tensor_tensor(yT[:, w_["tsl"]], zs[:, w_["tsl"]], w_["p3"][:, 0:128], ALU.mult), reads=[bz], writes=[w_["p3b"], by])
        kb.dma("sp", self.YT[512 + h * 128:512 + (h + 1) * 128, :], yT, reads=[by], writes=[kb.buf("YT", 4 + h)])


Prog.mixer_b = _mixer_b


def _mixer_c(self, l):
    kb = self.kb
    pv = self.pv[l % 2]
    pvb = kb.buf("pv", l % 2)
    U = [self.ar(i * S, S) for i in range(9)]
    T = [self.ac(i * S, S) for i in range(20)]
    TW, TA, SG0, SG1, aT, bT, kT, rT, VT, A_tok, B_tok, K_tok, V_tok, ArbT0, ArkT0, WtT, Gb, yT, ArbT1, ArkT1 = T
    ArbTs, ArkTs = [ArbT0, ArbT1], [ArkT0, ArkT1]
    sc0 = 20 * S

    def m128at(off):
        return self.ac(off, 256, F32)
    HS = []
    for hh_ in range(2):
        base = sc0 + hh_ * 1536
        HS.append(dict(Xa=m128at(base), XaT=m128at(base + 256), Xb=m128at(base + 512), XbT=m128at(base + 768), Pm=m128at(base + 1024),
                       TTb=self.ac(base + 1280, 128), AakT=self.ac(base + 1408, 128)))
    on_ = m128at(sc0 + 3072)
    o2 = sc0 + 3328
    for hh_ in range(2):
        HS[hh_]["AVb"] = self.ac(o2 + hh_ * 256, 64)
        HS[hh_]["Ub"] = self.ac(o2 + hh_ * 256 + 64, 64)
        HS[hh_]["Tbfs"] = [self.ac(o2 + hh_ * 256 + 128, 64), self.ac(o2 + hh_ * 256 + 192, 64)]
        HS[hh_]["Tst"] = self.ar(6 * S + hh_ * 128, 64)
        HS[hh_]["tmpS"] = self.ar(6 * S + hh_ * 128 + 64, 64)
    cols_ = self.ac(o2 + 512, 128, F32)
    junk = HS[1]["Pm"]
    Xa = HS[0]["Xa"]
    cm = self.cmask
    mU_s, mU_i, mL_s = cm[:, 0:128], cm[:, 128:256], cm[:, 256:384]
    bd64 = cm[:, 640:768]
    bcm = kb.buf("cmask")
    identf = kb.buf("ident_f")
    identb = kb.buf("ident_b")
    rmask, R, K, V, LW, At, X1, X2, o_all = U
    Y_all = U[1]
    EL = U[5]
    L_ = U[8]
    bU = [kb.buf("c_U", i) for i in range(9)]
    brm, bR, bK, bV, bLW, bAt, bX1, bX2, bo = bU
    bTl = [kb.buf("c_T", i) for i in range(20)]
    bTW, bTA, bSG0, bSG1, baT, bbT, bkT, brT, bVT, bAtok, bBtok, bKtok, bVtok, bArb0, bArk0, bWtT, bGb, byT, bArb1, bArk1 = bTl
    bArbs, bArks = [bArb0, bArb1], [bArk0, bArk1]
    bcols = kb.buf("c_cols")
    kb.dma("sp", rmask, self.c_rmask[:, :], writes=[brm])
    kb.op("dve", lambda e: e.memset(WtT, 0.0), writes=[bWtT])
    w2 = self.WB[0][0:96, 0:1024]
    a2 = self.WB[0][0:96, 1024:2048]
    g2 = self.WB[0][:, 2048:4096].rearrange("p (k n) -> p k n", k=2)
    bw = kb.buf("c_lw")
    kb.dma("pool", w2, self.w["c_w2"][l], writes=[kb.buf("c_lw", 0)])
    kb.dma("pool", a2, self.w["c_a2"][l], writes=[kb.buf("c_lw", 1)])
    kb.dma("pool", g2, self.w["c_g2"][l].rearrange("(k p) n -> p k n", p=128), writes=[kb.buf("c_lw", 2)])
    blw = [kb.buf("c_lw", i) for i in range(3)]

    def shift(dst, bdst, ci, n=128):
        mu = pv[0:n, PV_MU + ci:PV_MU + ci + 1]
        r0 = C0 + C_CHUNKS[ci][0]
        kb.dma("sp", X1[0:n, :], self.COLS[r0:r0 + n, :], reads=[kb.buf("COLS", r0)], writes=[bX1])
        kb.op("dve", lambda e: e.tensor_tensor(dst[0:n, 1:], X1[0:n, 0:S - 1], X1[0:n, 1:], ALU.subtract), reads=[bX1], writes=[bdst])
        kb.op("dve", lambda e: e.scalar_tensor_tensor(dst[0:n, 1:], dst[0:n, 1:], mu, X1[0:n, 1:], ALU.mult, ALU.add),
              reads=[bX1, pvb, bdst], writes=[bdst])
        kb.op("dve", lambda e: e.tensor_scalar(dst[0:n, 0:1], X1[0:n, 0:1], mu, None, ALU.mult), reads=[bX1, pvb, bdst], writes=[bdst])
        kb.op("dve", lambda e: e.tensor_tensor(dst[0:n, 0:1], X1[0:n, 0:1], dst[0:n, 0:1], ALU.subtract), reads=[bX1, bdst], writes=[bdst])

    shift(X2, bX2, 24, 96)
    kb.op("act", lambda e: e.activation(TW[0:96, :], X2[0:96, :], AF.Tanh), reads=[bX2], writes=[bTW])
    shift(X2, bX2, 25, 96)
    kb.op("act", lambda e: e.activation(TA[0:96, :], X2[0:96, :], AF.Copy), reads=[bX2], writes=[bTA])
    shift(X2, bX2, 26)
    kb.op("act", lambda e: e.activation(SG0, X2, AF.Sigmoid), reads=[bX2], writes=[bSG0])
    shift(X2, bX2, 27)
    kb.op("act", lambda e: e.activation(SG1, X2, AF.Sigmoid), reads=[bX2], writes=[bSG1])
    SG = [SG0, SG1]

    if getattr(self, 'cstage', 99) == 0:
        return
    for cc in range(8):
        c_w0, c_a0, c_kk, c_ka, c_rk, c_gw, c_gb = [pv[:, o + cc:o + cc + 1] for o in (PV_W0, PV_A0, PV_KK, PV_KA, PV_RK, PV_GNW, PV_GNB)]
        negw0 = cols_[:, 0:1]
        omka = cols_[:, 1:2]
        kb.op("dve", lambda e: e.tensor_scalar(negw0, c_w0, -1.0, None, ALU.mult), reads=[pvb], writes=[bcols])
        kb.op("dve", lambda e: e.tensor_scalar(omka, c_ka, -1.0, 1.0, ALU.mult, ALU.add), reads=[pvb], writes=[bcols])
        shift(R, bR, cc)
        shift(K, bK, 8 + cc)
        shift(V, bV, 16 + cc)
        csl = slice(cc * 128, (cc + 1) * 128)
        for tg in range(NTG):
            ts_ = slice(tg * 512, (tg + 1) * 512)
            p_, p_b = self.pb[tg % 2], self.pbuf[tg % 2]
            kb.op("pe", lambda e: e.matmul(p_[:, :], w2[:, csl], TW[0:96, ts_], start=True, stop=True), reads=[blw[0], bTW], writes=[p_b])
            kb.op("act", lambda e: e.activation(X2[:, ts_], p_[:, :], AF.Identity, bias=negw0, scale=-1.0), reads=[bcols], writes=[p_b, bX2])
            p2, p2b = self.pb[2 + tg % 2], self.pbuf[2 + tg % 2]
            kb.op("pe", lambda e: e.matmul(p2[:, :], a2[:, csl], TA[0:96, ts_], start=True, stop=True), reads=[blw[1], bTA], writes=[p2b])
            kb.op("act", lambda e: e.activation(At[:, ts_], p2[:, :], AF.Sigmoid, bias=c_a0, scale=1.0), reads=[pvb], writes=[p2b, bAt])
            p3, p3b = self.pb[4 + tg % 2], self.pbuf[4 + tg % 2]
            kb.mm_group([(lambda e, k2=k2: e.matmul(p3[:, :], g2[:, k2, csl], SG[k2][:, ts_], start=(k2 == 0), stop=(k2 == 1))) for k2 in range(2)],
                        reads=[blw[2], bSG0, bSG1], writes=[p3b])
            kb.op("dve", lambda e: e.tensor_copy(Gb[:, ts_], p3[:, :]), writes=[p3b, bGb])
        kb.op("act", lambda e: e.activation(X1, X2, AF.Abs), reads=[bX2], writes=[bX1])
        kb.op("act", lambda e: e.activation(X1, X1, AF.Exp, scale=-1.0), reads=[bX1], writes=[bX1])
        kb.op("act", lambda e: e.activation(X1, X1, AF.Ln, bias=1.0), reads=[bX1], writes=[bX1])
        kb.op("dve", lambda e: e.scalar_tensor_tensor(X1, X2, 0.0, X1, ALU.max, ALU.add), reads=[bX2, bX1], writes=[bX1])
        kb.op("act", lambda e: e.activation(X1, X1, AF.Exp, bias=-0.5, scale=-1.0), reads=[bX1], writes=[bX1])
        kb.op("dve", lambda e: e.tensor_scalar(LW, X1, -1.0, None, ALU.mult), reads=[bX1], writes=[bLW])
        if getattr(self, 'cstage', 99) == 1:
            return
        kb.op("dve", lambda e: e.tensor_scalar(X1, K, c_kk, None, ALU.mult), reads=[bK, pvb], writes=[bX1])
        kb.op("dve", lambda e: e.tensor_tensor(X2, X1, X1, ALU.mult), reads=[bX1], writes=[bX2])
        for tg in range(NTG):
            ts_ = slice(tg * 512, (tg + 1) * 512)
            p_, p_b = self.pb[tg % 2], self.pbuf[tg % 2]
            kb.op("pe", lambda e: e.matmul(p_[:, :], bd64, X2[:, ts_], start=True, stop=True), reads=[bcm, bX2], writes=[p_b])
            kb.op("dve", lambda e: e.tensor_scalar(X2[:, ts_], p_[:, :], 1e-6, None, ALU.add), reads=[bX2], writes=[p_b, bX2])
        kb.op("act", lambda e: e.activation(X2, X2, AF.Sqrt), reads=[bX2], writes=[bX2])
        kb.op("dve", lambda e: e.reciprocal(X2, X2), reads=[bX2], writes=[bX2])
        kb.op("dve", lambda e: e.tensor_tensor(X1, X1, X2, ALU.mult), reads=[bX1, bX2], writes=[bX1])
        kb.op("dve", lambda e: e.tensor_scalar(X2, At, c_ka, omka, ALU.mult, ALU.add), reads=[bAt, pvb, bcols], writes=[bX2])
        kb.op("dve", lambda e: e.tensor_tensor(K, K, X2, ALU.mult), reads=[bK, bX2], writes=[bK])
        kb.op("dve", lambda e: e.scalar_tensor_tensor(X2, R, c_rk, K, ALU.mult, ALU.mult), reads=[bR, bK, pvb], writes=[bX2])
        for tg in range(NTG):
            ts_ = slice(tg * 512, (tg + 1) * 512)
            p_, p_b = self.pb[2 + tg % 2], self.pbuf[2 + tg % 2]
            kb.op("pe", lambda e: e.matmul(p_[:, :], bd64, X2[:, ts_], start=True, stop=True), reads=[bcm, bX2], writes=[p_b])
            kb.op("dve", lambda e: e.tensor_tensor(X2[:, ts_], V[:, ts_], p_[:, :], ALU.mult), reads=[bV, bX2], writes=[p_b, bX2])
        if getattr(self, 'cstage', 99) == 2:
            return
        kb.op("dve", lambda e: e.tensor_tensor_scan(L_, rmask, LW, 0.0, ALU.mult, ALU.add), reads=[brm, bLW], writes=[bo])
        kb.op("dve", lambda e: e.tensor_tensor(LW, L_, LW, ALU.subtract), reads=[bo, bLW], writes=[bLW])
        kb.op("act", lambda e: e.activation(LW, LW, AF.Exp), reads=[bLW], writes=[bLW])
        kb.op("dve", lambda e: e.scalar_tensor_tensor(aT, X1, -1.0, LW, ALU.mult, ALU.mult), reads=[bX1, bLW], writes=[baT])
        kb.op("act", lambda e: e.activation(LW, L_, AF.Exp, scale=-1.0), reads=[bo, bLW], writes=[bLW])
        kb.op("dve", lambda e: e.tensor_tensor(X1, X1, At, ALU.mult), reads=[bX1, bAt], writes=[bX1])
        kb.op("dve", lambda e: e.tensor_tensor(bT, X1, LW, ALU.mult), reads=[bX1, bLW], writes=[bbT])
        kb.op("dve", lambda e: e.tensor_tensor(kT, K, LW, ALU.mult), reads=[bK, bLW], writes=[bkT])
        kb.op("act", lambda e: e.activation(EL, L_, AF.Exp), reads=[bo, bAt], writes=[bAt])
        kb.op("dve", lambda e: e.tensor_tensor(rT, R, EL, ALU.mult), reads=[bR, bAt], writes=[brT])
        kb.op("act", lambda e: e.activation(VT, V, AF.Copy), reads=[bV], writes=[bVT])
        if getattr(self, 'cstage', 99) == 3:
            return
        for blk in range(16):
            tsl = slice(blk * 128, (blk + 1) * 128)
            pT, pTb = self.psT[blk % 2], self.psTb[blk % 2]
            for i, (src, bsrc) in enumerate(((aT, baT), (bT, bbT), (kT, bkT), (VT, bVT))):
                kb.op("pe", lambda e: e.transpose(pT[:, i * 128:(i + 1) * 128], src[:, tsl], self.ident_b[:]), reads=[bsrc, identb], writes=[pTb])
            kb.op("act", lambda e: e.activation(A_tok[:, tsl], pT[:, 0:128], AF.Copy), writes=[pTb, bAtok])
            kb.op("dve", lambda e: e.tensor_copy(B_tok[:, tsl], pT[:, 128:256]), writes=[pTb, bBtok])
            kb.op("act", lambda e: e.activation(K_tok[:, tsl], pT[:, 256:384], AF.Copy), writes=[pTb, bKtok])
            kb.op("dve", lambda e: e.tensor_copy(V_tok[:, tsl], pT[:, 384:512]), writes=[pTb, bVtok])
        if getattr(self, 'cstage', 99) == 4:
            return
        for hh_ in range(2):
            for nm in ("Xa", "Xb", "P", "TTb", "Aak", "AV", "Ub", "Tbf", "Tst", "tmpS"):
                HS[hh_]["b" + nm] = kb.buf("c_hs", hh_, nm)
            HS[hh_]["banks"] = [(self.pb[3 * hh_ + i], self.pbuf[3 * hh_ + i]) for i in range(3)]
        bon = kb.buf("c_on")
        bXa = HS[0]["bXa"]
        bY = bR
        for blk in range(16):
            tsl = slice(blk * 128, (blk + 1) * 128)
            chains = []
            for hh in range(2):
                z_ = HS[hh]
                hs = slice(hh * 64, (hh + 1) * 64)
                (p0, p0b), (p1, p1b), (p2, p2b) = z_["banks"]
                kb.op("pe", lambda e: e.matmul(p0[:, 0:128], bT[hs, tsl], aT[hs, tsl], start=True, stop=True), reads=[bbT, baT], writes=[p0b])
                kb.op("pe", lambda e: e.matmul(p1[:, 0:128], aT[hs, tsl], bT[hs, tsl], start=True, stop=True), reads=[bbT, baT], writes=[p1b])
                kb.op("pe", lambda e: e.matmul(p2[:, 0:128], kT[hs, tsl], aT[hs, tsl], start=True, stop=True), reads=[bkT, baT], writes=[p2b])
                kb.op("dve", lambda e: e.tensor_tensor(z_["Xa"], mU_s, p0[:, 0:128], ALU.mult), reads=[bcm], writes=[p0b, z_["bXa"]])
                kb.op("dve", lambda e: e.tensor_tensor(z_["XaT"], mL_s, p1[:, 0:128], ALU.mult), reads=[bcm], writes=[p1b, z_["bXa"]])
                kb.op("dve", lambda e: e.tensor_tensor(z_["AakT"], mU_s, p2[:, 0:128], ALU.mult), reads=[bcm], writes=[p2b, z_["bAak"]])
                kb.op("pe", lambda e: e.matmul(p0[:, 0:128], bT[hs, tsl], rT[hs, tsl], start=True, stop=True), reads=[bbT, brT], writes=[p0b])
                kb.op("pe", lambda e: e.matmul(p1[:, 0:128], kT[hs, tsl], rT[hs, tsl], start=True, stop=True), reads=[bkT, brT], writes=[p1b])
                kb.op("dve", lambda e: e.tensor_tensor(ArbTs[hh][:, tsl], mU_i, p0[:, 0:128], ALU.mult), reads=[bcm], writes=[p0b, bArbs[hh]])
                kb.op("dve", lambda e: e.tensor_tensor(ArkTs[hh][:, tsl], mU_i, p1[:, 0:128], ALU.mult), reads=[bcm], writes=[p1b, bArks[hh]])
                chains.append(dict(X=z_["Xa"], XT=z_["XaT"], bX=z_["bXa"], P=z_["Pm"], bP=z_["bP"], scr=(z_["Xb"], z_["XbT"]), bscr=z_["bXb"],
                                   banks=z_["banks"]))
            self.neumann_multi(chains)
            for hh in range(2):
                z_ = HS[hh]
                hs = slice(hh * 64, (hh + 1) * 64)
                vsl = slice(blk * 128 + hh * 64, blk * 128 + (hh + 1) * 64)
                (p0, p0b), (p1, p1b), (p2, p2b) = z_["banks"]
                kb.op("act", lambda e: e.activation(z_["TTb"], z_["Pm"], AF.Copy), reads=[z_["bP"]], writes=[z_["bTTb"]])
                kb.op("pe", lambda e: e.matmul(p2[:, 0:64], z_["AakT"], V_tok[:, vsl], start=True, stop=True), reads=[z_["bAak"], bVtok], writes=[p2b])
                kb.op("act", lambda e: e.activation(z_["AVb"], p2[:, 0:64], AF.Copy), writes=[p2b, z_["bAV"]])
                kb.op("pe", lambda e: e.matmul(p0[:, 0:64], z_["TTb"], z_["AVb"], start=True, stop=True), reads=[z_["bTTb"], z_["bAV"]], writes=[p0b])
                kb.op("pe", lambda e: e.matmul(p1[:, 0:128], A_tok[:, tsl], z_["TTb"], start=True, stop=True), reads=[bAtok, z_["bTTb"]], writes=[p1b])
                kb.op("act", lambda e: e.activation(Y_all[:, vsl], p0[:, 0:64], AF.Copy), writes=[p0b, bY])
                kb.op("dve", lambda e: e.tensor_copy(WtT[hs, tsl], p1[hs, 0:128]), writes=[p1b, bWtT])
        if getattr(self, 'cstage', 99) == 5:
            return
        for hh in range(2):
            z_ = HS[hh]
            z_["bTbfs"] = [kb.buf("c_hs", hh, "Tbf0"), kb.buf("c_hs", hh, "Tbf1")]
            kb.op("dve", lambda e: e.memset(z_["Tst"], 0.0), writes=[z_["bTst"], bX1])
            kb.op("dve", lambda e: e.memset(z_["Tbfs"][0], 0.0), writes=[z_["bTbfs"][0]])
            kb.op("dve", lambda e: e.memset(z_["Tbfs"][1], 0.0), writes=[z_["bTbfs"][1]])
            kb.op("dve", lambda e: e.memset(z_["Ub"], 0.0), writes=[z_["bUb"]])
        for c in range(32):
            blk, ch = c // 2, c % 2
            tsl = slice(blk * 128, (blk + 1) * 128)
            rs = slice(ch * 64, (ch + 1) * 64)
            tlast = c * 64 + 63
            for hh in range(2):
                z_ = HS[hh]
                hs = slice(hh * 64, (hh + 1) * 64)
                vsl = slice(blk * 128 + hh * 64, blk * 128 + (hh + 1) * 64)
                (p0, p0b), (p1, p1b), (p2, p2b) = z_["banks"]
                Ub, Tst, tmpS = z_["Ub"], z_["Tst"], z_["tmpS"]
                bUb, bTst, btmpS = z_["bUb"], z_["bTst"], z_["btmpS"]
                Tc, bTc = z_["Tbfs"][c % 2], z_["bTbfs"][c % 2]
                Tn, bTn = z_["Tbfs"][(c + 1) % 2], z_["bTbfs"][(c + 1) % 2]
                elc = EL[hs, tlast:tlast + 1]
                kb.op("dve", lambda e: e.tensor_scalar(tmpS[hs, :], Tst[hs, :], elc, None, ALU.mult), reads=[bTst, bAt], writes=[btmpS, bX1])
                kb.op("pe", lambda e: e.matmul(p0[:, 0:64], WtT[hs, tsl], Tc[hs, :], start=True, stop=True), reads=[bWtT, bTc], writes=[p0b])
                kb.op("dve", lambda e: e.tensor_tensor(Ub[rs, :], Y_all[rs, vsl], p0[rs, 0:64], ALU.add), reads=[bY], writes=[p0b, bUb])
                kb.mm_group([lambda e: e.matmul(p2[:, 0:64], B_tok[rs, tsl], Ub[rs, :], start=True, stop=False),
                             lambda e: e.matmul(p2[:, 0:64], K_tok[rs, tsl], V_tok[rs, vsl], start=False, stop=True)],
                            reads=[bBtok, bKtok, bUb, bVtok], writes=[p2b])
                kb.mm_group([lambda e: e.matmul(p1[:, 0:64], rT[hs, tsl], Tc[hs, :], start=True, stop=False),
                             lambda e: e.matmul(p1[:, 0:64], ArbTs[hh][:, tsl], Ub[:, :], start=False, stop=False),
                             lambda e: e.matmul(p1[:, 0:64], ArkTs[hh][:, tsl], V_tok[:, vsl], start=False, stop=True)],
                            reads=[brT, bTc, bArbs[hh], bArks[hh], bUb, bVtok], writes=[p1b])
                kb.op("dve", lambda e: e.scalar_tensor_tensor(Tn[hs, :], p2[hs, 0:64], elc, tmpS[hs, :], ALU.mult, ALU.add),
                      reads=[btmpS, bAt], writes=[p2b, bTn])
                kb.op("dve", lambda e: e.scalar_tensor_tensor(Tst[hs, :], p2[hs, 0:64], elc, tmpS[hs, :], ALU.mult, ALU.add),
                      reads=[btmpS, bAt], writes=[p2b, bTst, bX1])
                kb.op("act", lambda e: e.activation(o_all[rs, vsl], p1[rs, 0:64], AF.Copy), writes=[p1b, bo])
        if getattr(self, 'cstage', 99) == 6:
            return
        bonh = [kb.buf("c_onh", 0), kb.buf("c_onh", 1)]
        bch = [kb.buf("c_colsh", 0), kb.buf("c_colsh", 1)]
        bjk = [HS[1]["bP"], HS[1]["bXb"]]
        jk = [HS[1]["Pm"], HS[1]["Xb"]]
        for blk in range(16):
            tsl = slice(blk * 128, (blk + 1) * 128)
            hv = []
            for hh in range(2):
                hv.append(dict(vsl=slice(blk * 128 + hh * 64, blk * 128 + (hh + 1) * 64), osl=slice(hh * 64, (hh + 1) * 64),
                               mcol=cols_[:, 8 + hh * 4:9 + hh * 4], vcol=cols_[:, 9 + hh * 4:10 + hh * 4], bon=bonh[hh], bc=bch[hh],
                               jk=jk[hh], bjk=bjk[hh]))
            for w_ in hv:
                kb.op("dve", lambda e: e.tensor_reduce(w_["mcol"], o_all[:, w_["vsl"]], AX.X, ALU.add), reads=[bo], writes=[w_["bc"]])
            for w_ in hv:
                kb.op("dve", lambda e: e.tensor_scalar(w_["mcol"], w_["mcol"], 1.0 / 64, None, ALU.mult), reads=[w_["bc"]], writes=[w_["bc"]])
            for w_ in hv:
                kb.op("dve", lambda e: e.tensor_scalar(on_[:, w_["osl"]], o_all[:, w_["vsl"]], w_["mcol"], None, ALU.subtract),
                      reads=[bo, w_["bc"]], writes=[w_["bon"]])
            for w_ in hv:
                kb.op("act", lambda e: e.activation(w_["jk"][:, 0:64], on_[:, w_["osl"]], AF.Square, accum_out=w_["vcol"]),
                      reads=[w_["bon"]], writes=[w_["bjk"], w_["bc"]])
            for w_ in hv:
                kb.op("dve", lambda e: e.tensor_scalar(w_["vcol"], w_["vcol"], 1.0 / 64, 64e-5, ALU.mult, ALU.add), reads=[w_["bc"]], writes=[w_["bc"]])
            for w_ in hv:
                kb.op("act", lambda e: e.activation(w_["vcol"], w_["vcol"], AF.Sqrt), reads=[w_["bc"]], writes=[w_["bc"]])
            for w_ in hv:
                kb.op("dve", lambda e: e.reciprocal(w_["vcol"], w_["vcol"]), reads=[w_["bc"]], writes=[w_["bc"]])
            for w_ in hv:
                kb.op("dve", lambda e: e.tensor_scalar(on_[:, w_["osl"]], on_[:, w_["osl"]], w_["vcol"], None, ALU.mult),
                      reads=[w_["bon"], w_["bc"]], writes=[w_["bon"]])
            p3, p3b = self.pb[3 + blk % 2], self.pbuf[3 + blk % 2]
            kb.op("pe", lambda e: e.transpose(p3[:, 0:128], on_, self.ident_f[:]), reads=[bonh[0], bonh[1], identf], writes=[p3b])
            kb.op("act", lambda e: e.activation(Xa, p3[:, 0:128], AF.Identity, bias=c_gb, scale=c_gw), reads=[pvb], writes=[p3b, bXa])
            kb.op("dve", lambda e: e.tensor_tensor(Xa, Xa, X2[:, tsl], ALU.add), reads=[bXa, bX2], writes=[bXa])
            kb.op("dve", lambda e: e.tensor_tensor(yT[:, tsl], Xa, Gb[:, tsl], ALU.mult), reads=[bXa, bGb], writes=[byT])
        kb.dma("sp", self.YT[1536 + cc * 128:1536 + (cc + 1) * 128, :], yT, reads=[byT], writes=[kb.buf("YT", 12 + cc)])


Prog.mixer_c = _mixer_c


def _merge_proj(self, l):
    kb = self.kb
    NK = 20
    yall = self.ACTT[:, 0:NK * S].rearrange("p (k t) -> p k t", k=NK)
    ybufs = []
    for k in range(NK):
        b = kb.buf("m_y", k)
        kb.dma("sp", yall[:, k, :], self.YT[k * 128:(k + 1) * 128, :], reads=[kb.buf("YT", k)], writes=[b])
        ybufs.append(b)
    gt = [self.ar(i * 1024, 1024, BF16) for i in range(6)]
    gtb = [kb.buf("m_gt", i) for i in range(6)]
    acc = [self.ar(6 * 1024 + i * 512, 512) for i in range(2)]
    accb = [kb.buf("m_acc", i) for i in range(2)]
    tmp = [self.ar(6 * 1024 + 1024 + i * 512, 512) for i in range(2)]
    tmpb = [kb.buf("m_tmp", i) for i in range(2)]
    mst = [self.ar(6 * 1024 + 2048 + i * 1024, 1024, BF16) for i in range(2)]
    mstb = [kb.buf("m_mst", i) for i in range(2)]
    segs = [("proj_a", 0, 4), ("proj_b", 4, 8), ("proj_c", 12, 8)]
    rot = 0
    for dcp in range(KC // 2):
        slot = self.wrot % 2
        self.wrot += 1
        wb = self.WB[slot][:, 0:NK * 256].rearrange("p (k n) -> p k n", k=NK)
        pieces = []
        for (nm, k0, nk) in segs:
            pieces.append((wb[:, k0:k0 + nk, :], self.w[nm][l][:, dcp * 256:(dcp + 1) * 256]))
        wbufs = self.load_w(slot, pieces)
        for dd in range(2):
            dc = dcp * 2 + dd
            gs = (dc % 2) * 3
            for x in range(3):
                r0 = x * D + dc * 128
                kb.dma("sp", gt[gs + x], self.GT[r0:r0 + 128, :], reads=[kb.buf("GT", r0)], writes=[gtb[gs + x]])
            m_s, m_b = mst[dc % 2], mstb[dc % 2]
            for tg in range(NTG):
                ts_ = slice(tg * 512, (tg + 1) * 512)
                a_, a_b = acc[rot % 2], accb[rot % 2]
                t_, t_b = tmp[rot % 2], tmpb[rot % 2]
                rot += 1
                for x, (nm, k0, nk) in enumerate(segs):
                    p_, p_b = self.gbank()
                    kb.mm_group([(lambda e, k=k: e.matmul(p_[:, :], wb[:, k0 + k, dd * 128:(dd + 1) * 128], yall[:, k0 + k, ts_],
                                                          start=(k == 0), stop=(k == nk - 1))) for k in range(nk)],
                                reads=ybufs[k0:k0 + nk] + wbufs, writes=[p_b])
                    if x == 0:
                        kb.op("dve", lambda e: e.tensor_tensor(a_, gt[gs + x][:, ts_], p_[:, :], ALU.mult), reads=[gtb[gs + x]], writes=[p_b, a_b])
                    else:
                        kb.op("dve", lambda e: e.tensor_tensor(t_, gt[gs + x][:, ts_], p_[:, :], ALU.mult), reads=[gtb[gs + x]], writes=[p_b, t_b])
                        if x == 1:
                            kb.op("dve", lambda e: e.tensor_tensor(a_, a_, t_, ALU.add), reads=[a_b, t_b], writes=[a_b])
                        else:
                            kb.op("dve", lambda e: e.tensor_tensor(m_s[:, ts_], a_, t_, ALU.add), reads=[a_b, t_b], writes=[m_b])
            kb.dma("sp", self.MT[dc * 128:(dc + 1) * 128, :], m_s, reads=[m_b], writes=[kb.buf("MT", dc)])


def _wout(self, l):
    kb = self.kb
    W = self.w["w_out"][l]
    mT = self.hT()
    mb = []
    for kc in range(KC):
        b = kb.buf("hT", kc)
        kb.dma("sp", mT[:, kc, :], self.MT[kc * 128:(kc + 1) * 128, :], reads=[kb.buf("MT", kc)], writes=[b])
        mb.append(b)
    xo = [self.ar(i * 512, 512) for i in range(4)]
    xob = [kb.buf("d_xo", i) for i in range(4)]
    rot = 0
    for blk in range(D // 512):
        slot = self.wrot % 2
        self.wrot += 1
        wb = self.WB[slot][:, 0:KC * 512].rearrange("p (k n) -> p k n", k=KC)
        pieces = [(wb[:, q * 4:(q + 1) * 4, :], W[q * 512:(q + 1) * 512, blk * 512:(blk + 1) * 512]) for q in range(4)]
        wbufs = self.load_w(slot, pieces)
        for dd in range(4):
            dc = blk * 4 + dd
            for tg in range(NTG):
                x_ = xo[rot % 4]
                x_b = xob[rot % 4]
                rot += 1
                xdb = kb.buf("XT", dc, tg)
                kb.dma("pool", x_, self.XT[dc * 128:(dc + 1) * 128, tg * 512:(tg + 1) * 512], reads=[xdb], writes=[x_b])
                p_, p_b = self.gbank()
                kb.mm_group([(lambda e, kc=kc: e.matmul(p_[:, :], wb[:, kc, dd * 128:(dd + 1) * 128], mT[:, kc, tg * 512:(tg + 1) * 512],
                                                        start=(kc == 0), stop=(kc == KC - 1))) for kc in range(KC)],
                            reads=mb + wbufs, writes=[p_b])
                kb.op("dve", lambda e: e.tensor_tensor(x_, x_, p_[:, :], ALU.add), reads=[x_b], writes=[p_b, x_b])
                kb.dma("sp", self.XT[dc * 128:(dc + 1) * 128, tg * 512:(tg + 1) * 512], x_, reads=[x_b], writes=[xdb])


Prog.merge_proj = _merge_proj
Prog.wout = _wout
```

```python
import contextlib
import numpy as np
import concourse.bass as bass
import concourse.mybir as mybir
from concourse.bass_utils import run_bass_kernel_spmd

F32 = mybir.dt.float32
BF16 = mybir.dt.bfloat16
AF = mybir.ActivationFunctionType
ALU = mybir.AluOpType
AX = mybir.AxisListType

D = 2048
S = 2048
DEPTH = 4
FF = 5632
KC = D // 128
NTG = S // 512
A_COLS, B_COLS, C_COLS, G_COLS = 4608, 4112, 3520, 6144
IN_COLS = A_COLS + B_COLS + C_COLS + G_COLS
B0 = A_COLS
C0 = A_COLS + B_COLS
G0 = C0 + C_COLS
NEG = -1.0e30


class Buf:
    __slots__ = ("key", "w", "r")

    def __init__(self, key):
        self.key = key
        self.w = []
        self.r = {}


class KB:
    NPOOL = 8

    def __init__(self, nc, stack):
        self.nc = nc
        self.stack = stack
        self.eng = {"pe": nc.tensor, "dve": nc.vector, "act": nc.scalar, "pool": nc.gpsimd, "sp": nc.sync}
        self.sem = {}
        self.cnt = {}
        for e in self.eng:
            self.sem[e] = stack.enter_context(nc.semaphore("s_" + e))
            self.cnt[e] = 0
        self.dsem = {}
        self.dcnt = {}
        for q in ("sp", "pool"):
            self.dsem[q] = [stack.enter_context(nc.semaphore(f"d_{q}{i}")) for i in range(self.NPOOL)]
            self.dcnt[q] = 0
        self.seen = {e: {} for e in self.eng}
        self.bufs = {}
        self.n_ins = 0
        self.n_wait = 0

    def sb(self, name, shape, dtype):
        return self.stack.enter_context(self.nc.sbuf_tensor(name, list(shape), dtype))

    def ps(self, name, shape, dtype=F32):
        return self.stack.enter_context(self.nc.psum_tensor(name, list(shape), dtype))

    def buf(self, *key):
        b = self.bufs.get(key)
        if b is None:
            b = self.bufs[key] = Buf(key)
        return b

    def _wait(self, e, ev):
        semkey, sem, val, src = ev
        if self.seen[e].get(semkey, 0) >= val:
            return
        self.eng[e].wait_ge(sem, val)
        self.seen[e][semkey] = val
        self.n_wait += 1

    def _deps(self, e, reads, writes, is_dma=False):
        for b in reads:
            for ev in b.w:
                if ev[3] == e and e == "pe" and not is_dma:
                    continue
                self._wait(e, ev)
        for b in writes:
            for ev in b.w:
                if ev[3] == e and not is_dma and ev[0][0] == "c":
                    continue
                self._wait(e, ev)
            for ev in b.r.values():
                if ev[3] == e and not is_dma and ev[0][0] == "c":
                    continue
                self._wait(e, ev)

    def _record(self, ev, reads, writes):
        for b in writes:
            b.w = [ev]
            b.r = {}
        for b in reads:
            b.r[ev[0]] = ev

    def op(self, e, fn, reads=(), writes=()):
        self._deps(e, reads, writes)
        ins = fn(self.eng[e])
        self.cnt[e] += 1
        ins.then_inc(self.sem[e], 1)
        ev = (("c", e), self.sem[e], self.cnt[e], e)
        self._record(ev, reads, writes)
        self.n_ins += 1
        return ins

    def mm_group(self, fns, reads=(), writes=()):
        e = "pe"
        self._deps(e, reads, writes)
        ins = None
        for fn in fns:
            ins = fn(self.eng[e])
            self.n_ins += 1
        self.cnt[e] += 1
        ins.then_inc(self.sem[e], 1)
        ev = (("c", e), self.sem[e], self.cnt[e], e)
        self._record(ev, reads, writes)

    def dma(self, q, out, in_, reads=(), writes=(), **kw):
        e = q
        i = self.dcnt[q]
        slot = i % self.NPOOL
        rnd = i // self.NPOOL
        sem = self.dsem[q][slot]
        semkey = ("d", q, slot)
        if rnd > 0:
            self._wait(e, (semkey, sem, 16 * rnd, e))
        self._deps(e, reads, writes, is_dma=True)
        ins = self.eng[e].dma_start(out=out, in_=in_, **kw)
        ins.then_inc(sem, 16)
        self.dcnt[q] += 1
        ev = (semkey, sem, 16 * (rnd + 1), e)
        self._record(ev, reads, writes)
        self.n_ins += 1
        return ev

    def all_events(self):
        evs = []
        for q in self.dsem:
            n = self.dcnt[q]
            for slot in range(self.NPOOL):
                k = (n - slot + self.NPOOL - 1) // self.NPOOL
                if k > 0:
                    evs.append((("d", q, slot), self.dsem[q][slot], 16 * k, q))
        for x in self.cnt:
            if self.cnt[x] > 0:
                evs.append((("c", x), self.sem[x], self.cnt[x], x))
        return evs

    def barrier(self, engines=("pe", "dve", "act", "pool", "sp")):
        evs = self.all_events()
        for e in engines:
            for ev in evs:
                if ev[0] == ("c", e):
                    continue
                self._wait(e, ev)

    def wait_all(self, e):
        for ev in self.all_events():
            if ev[0] == ("c", e):
                continue
            self._wait(e, ev)


NPV = 256
PV_F1, PV_MIX, PV_F2, PV_FIN = 0, 16, 32, 48
PV_CONV = 64
PV_MU = 160
PV_W0, PV_A0, PV_KK, PV_KA, PV_RK, PV_GNW, PV_GNB = 188, 196, 204, 212, 220, 228, 236
PV_ALOG, PV_DTB = 244, 245

C_CHUNKS = [(i * 128, 128) for i in range(24)] + [(3072, 96), (3168, 96), (3264, 128), (3392, 128)]


def _cols(v):
    return np.ascontiguousarray(v.reshape(-1, 128).T)


def pack_pv(inp, l):
    pv = np.zeros((128, NPV), np.float32)
    pv[:, PV_F1:PV_F1 + 16] = _cols(inp["ffn1_norm"][l])
    pv[:, PV_MIX:PV_MIX + 16] = _cols(inp["mix_norm"][l])
    pv[:, PV_F2:PV_F2 + 16] = _cols(inp["ffn2_norm"][l])
    pv[:, PV_FIN:PV_FIN + 16] = _cols(inp["final_norm"])
    bc = inp["b_conv"][l]
    for cc in range(24):
        for j in range(4):
            pv[:, PV_CONV + cc * 4 + j] = bc[j, cc * 128:(cc + 1) * 128]
    mu = inp["c_mu"][l]
    for ci, (c0, n) in enumerate(C_CHUNKS):
        pv[:n, PV_MU + ci] = mu[c0:c0 + n]
    pv[:, PV_W0:PV_W0 + 8] = _cols(inp["c_w0"][l])
    pv[:, PV_A0:PV_A0 + 8] = _cols(inp["c_a0"][l])
    pv[:, PV_KK:PV_KK + 8] = _cols(inp["c_k_k"][l])
    pv[:, PV_KA:PV_KA + 8] = _cols(inp["c_k_a"][l])
    pv[:, PV_RK:PV_RK + 8] = _cols(inp["c_r_k"][l].reshape(-1))
    pv[:, PV_GNW:PV_GNW + 8] = _cols(inp["c_gn_w"][l])
    pv[:, PV_GNB:PV_GNB + 8] = _cols(inp["c_gn_b"][l])
    pv[0:8, PV_ALOG] = inp["b_a_log"][l]
    pv[0:8, PV_DTB] = inp["b_dt_bias"][l]
    return pv


def make_consts():
    c = {}
    c["ident"] = np.eye(128, dtype=np.float32)
    s = np.arange(128)[:, None]
    t = np.arange(128)[None, :]
    same = (s // 64) == (t // 64)
    m = np.zeros((128, 6 * 128), np.float32)
    m[:, 0:128] = (same & (t > s))
    m[:, 128:256] = (same & (t >= s))
    m[:, 256:384] = (same & (t < s))
    m[:, 384:512] = np.where(same & (t >= s), 0.0, NEG)
    m[:, 512:640] = np.where(same & (t <= s), 0.0, NEG)
    m[:, 640:768] = same
    c["cmask"] = m
    slopes = 2.0 ** (-8.0 * (np.arange(12, dtype=np.float64) + 1.0) / 12)
    dil = [1, 4, 16]
    qi = np.arange(128)[:, None]
    ki = np.arange(256)[None, :]
    delta = qi + 128 - ki
    valid = (delta >= 0) & (delta <= 128)
    ab = np.zeros((128, 12, 256), np.float32)
    for h in range(12):
        d = dil[h // 4]
        ab[:, h, :] = np.where(valid, -slopes[h] * (delta * d), NEG)
    c["abias"] = ab.reshape(128, 12 * 256)
    sel = np.zeros((16, 16 * 128), np.float32)
    for h in range(16):
        sel[h, h * 128:(h + 1) * 128] = 1.0
    c["sel"] = sel
    rm = np.ones((128, S), np.float32)
    rm[:, 0::64] = 0.0
    c["rmask"] = rm
    return c


W_SHAPES = {
    "ffn1_w_gu": [D, 2 * FF], "ffn1_w_down": [FF, D], "w_in": [D, IN_COLS],
    "c_w2": [96, 1024], "c_a2": [96, 1024], "c_g2": [256, 1024],
    "proj_a": [512, D], "proj_b": [1024, D], "proj_c": [1024, D], "w_out": [D, D],
    "ffn2_w_gu": [D, 2 * FF], "ffn2_w_down": [FF, D],
}
ARENA_F32 = 18432


class Prog:
    def __init__(self, nl=DEPTH, mode="full"):
        self.nl = nl
        self.mode = mode
        nc = bass.Bass("TRN2", target_bir_lowering=False)
        self.nc = nc
        dt = nc.dram_tensor
        self.xT_in = dt("xT", [D, S], F32, kind="ExternalInput").ap()
        self.w = {k: dt(k, [nl] + shp, F32, kind="ExternalInput").ap() for k, shp in W_SHAPES.items()}
        self.pv_d = dt("pv", [nl, 128, NPV], F32, kind="ExternalInput").ap()
        self.bnorm_d = dt("bnorm", [nl, 128, 128], F32, kind="ExternalInput").ap()
        self.c_ident = dt("c_ident", [128, 128], F32, kind="ExternalInput").ap()
        self.c_cmask = dt("c_cmask", [128, 768], F32, kind="ExternalInput").ap()
        self.c_abias = dt("c_abias", [128, 12 * 256], F32, kind="ExternalInput").ap()
        self.c_sel = dt("c_sel", [16, 16 * 128], F32, kind="ExternalInput").ap()
        self.c_rmask = dt("c_rmask", [128, S], F32, kind="ExternalInput").ap()
        dbg = (mode != "full")
        kind_s = "ExternalOutput" if dbg else "Internal"
        self.outT = dt("outT", [D, S], F32, kind="ExternalOutput").ap()
        self.XT = dt("XT", [D, S], F32, kind=kind_s).ap()
        self.AT = dt("AT", [FF, S], BF16, kind="Internal").ap()
        self.COLS = dt("COLS", [G0, S], F32, kind=("ExternalInput" if mode.startswith("mix_in") else kind_s)).ap()
        self.GT = dt("GT", [G_COLS, S], BF16, kind="Internal").ap()
        self.YT = dt("YT", [2560, S], BF16, kind=kind_s).ap()
        self.MT = dt("MT", [D, S], BF16, kind="Internal").ap()
        with contextlib.ExitStack() as st:
            self.kb = kb = KB(nc, st)
            self.ACTT = kb.sb("ACTT", [128, 45056], BF16)
            self.WB = [kb.sb("WB0", [128, 8192], BF16), kb.sb("WB1", [128, 8192], BF16)]
            self.AR = kb.sb("ARENA", [128, ARENA_F32], F32)
            self.ident_f = kb.sb("ident_f", [128, 128], F32)
            self.ident_b = kb.sb("ident_b", [128, 128], BF16)
            self.onesD = kb.sb("onesD", [128, 128], BF16)
            self.ones_f = kb.sb("ones_f", [128, 128], F32)
            self.cmask = kb.sb("cmask", [128, 768], F32)
            self.pv = [kb.sb("pv0", [128, NPV], F32), kb.sb("pv1", [128, NPV], F32)]
            self.pb = [kb.ps(f"pb{i}", [128, 512], F32) for i in range(6)]
            self.pbuf = [kb.buf("pb", i) for i in range(6)]
            self.psT = [kb.ps(f"psT{i}", [128, 1024], BF16) for i in range(2)]
            self.psTb = [kb.buf("psT", i) for i in range(2)]
            self.grot = 0
            self.wrot = 0
            self.build()
            kb.wait_all("sp")
            kb.wait_all("pe")
            kb.wait_all("act")
            kb.wait_all("dve")
            kb.wait_all("pool")

    def ar(self, off, n, dtype=F32, shape=None):
        v = self.AR[:, off:off + n]
        if dtype == BF16:
            v = v.bitcast(BF16)
        return v

    def ac(self, off, n, dtype=BF16):
        v = self.ACTT[:, off:off + n]
        if dtype == F32:
            v = v.bitcast(F32)
        return v

    def hT(self):
        return self.ACTT[:, 0:KC * S].rearrange("p (k t) -> p k t", k=KC)

    def load_consts(self):
        kb = self.kb
        kb.dma("sp", self.ident_f[:], self.c_ident[:, :], writes=[kb.buf("ident_f")])
        kb.dma("pool", self.ident_b[:], self.c_ident[:, :], writes=[kb.buf("ident_b")])
        kb.dma("sp", self.cmask[:], self.c_cmask[:, :], writes=[kb.buf("cmask")])
        kb.op("dve", lambda e: e.memset(self.onesD[:], 1.0 / D), writes=[kb.buf("onesD")])
        kb.op("dve", lambda e: e.memset(self.ones_f[:], 1.0), writes=[kb.buf("ones_f")])

    def load_pv(self, l):
        kb = self.kb
        kb.dma("sp", self.pv[l % 2][:], self.pv_d[l], writes=[kb.buf("pv", l % 2)])

    def copy_x_in(self):
        kb = self.kb
        for kc in range(KC):
            kb.dma("sp", self.XT[kc * 128:(kc + 1) * 128, :], self.xT_in[kc * 128:(kc + 1) * 128, :],
                   writes=[kb.buf("XT", kc, tg) for tg in range(NTG)])

    def norm(self, l, pvoff, final=False):
        kb = self.kb
        pv = self.pv[l % 2]
        pvb = kb.buf("pv", l % 2)
        xt = [self.ar(i * S, S) for i in range(2)]
        sq = [self.ar((2 + i) * S, S) for i in range(2)]
        sqh = [self.ar((2 + i) * S, S // 2, BF16) for i in range(2)]
        rstd = self.ar(4 * S, S)
        xtb = [kb.buf("n_xt", i) for i in range(2)]
        sqb = [kb.buf("n_sq", i) for i in range(2)]
        rsb = [kb.buf("n_rstd", tg) for tg in range(NTG)]
        hT = self.hT()
        for kc in range(KC):
            xb = [kb.buf("XT", kc, tg) for tg in range(NTG)]
            kb.dma("pool", xt[kc % 2], self.XT[kc * 128:(kc + 1) * 128, :], reads=xb, writes=[xtb[kc % 2]])
            kb.op("act", lambda e: e.activation(sqh[kc % 2], xt[kc % 2], AF.Square),
                  reads=[xtb[kc % 2]], writes=[sqb[kc % 2]])
            for tg in range(NTG):
                kb.op("pe", lambda e: e.matmul(self.pb[2 + tg][:], self.onesD[:], sqh[kc % 2][:, tg * 512:(tg + 1) * 512],
                                               start=(kc == 0), stop=(kc == KC - 1)),
                      reads=[sqb[kc % 2], kb.buf("onesD")], writes=[self.pbuf[2 + tg]])
        for tg in range(NTG):
            sl = slice(tg * 512, (tg + 1) * 512)
            kb.op("dve", lambda e: e.tensor_scalar(rstd[:, sl], self.pb[2 + tg][:], 1e-6, None, ALU.add),
                  reads=[self.pbuf[2 + tg]], writes=[rsb[tg]])
            kb.op("act", lambda e: e.activation(rstd[:, sl], rstd[:, sl], AF.Sqrt), reads=[rsb[tg]], writes=[rsb[tg]])
            kb.op("dve", lambda e: e.reciprocal(rstd[:, sl], rstd[:, sl]), reads=[rsb[tg]], writes=[rsb[tg]])
        for kc in range(KC):
            xb = [kb.buf("XT", kc, tg) for tg in range(NTG)]
            kb.dma("pool", xt[kc % 2], self.XT[kc * 128:(kc + 1) * 128, :], reads=xb, writes=[xtb[kc % 2]])
            g = pv[:, pvoff + kc:pvoff + kc + 1]
            if not final:
                kb.op("dve", lambda e: e.scalar_tensor_tensor(hT[:, kc, :], xt[kc % 2], g, rstd, ALU.mult, ALU.mult),
                      reads=[xtb[kc % 2], pvb] + rsb, writes=[kb.buf("hT", kc)])
            else:
                kb.op("dve", lambda e: e.scalar_tensor_tensor(sq[kc % 2], xt[kc % 2], g, rstd, ALU.mult, ALU.mult),
                      reads=[xtb[kc % 2], pvb] + rsb, writes=[sqb[kc % 2]])
                kb.dma("sp", self.outT[kc * 128:(kc + 1) * 128, :], sq[kc % 2], reads=[sqb[kc % 2]],
                       writes=[kb.buf("outT", kc)])

    def load_w(self, slot, pieces):
        kb = self.kb
        bufs = []
        for i, (dst, src) in enumerate(pieces):
            b = kb.buf("wb", slot, i)
            kb.dma("pool", dst, src.rearrange("(k p) n -> p k n", p=128), writes=[b])
            bufs.append(b)
        return bufs

    def gbank(self):
        i = self.grot % 6
        self.grot += 1
        return self.pb[i], self.pbuf[i]

    def ffn_gateup(self, l, wname):
        kb = self.kb
        W = self.w[wname][l]
        hT = self.hT()
        hb = [kb.buf("hT", kc) for kc in range(KC)]
        sg = [self.ar(5 * S + i * 512, 512) for i in range(2)]
        sgb = [kb.buf("f_sg", i) for i in range(2)]
        ast = [self.ar(5 * S + 1024 + i * 1024, 1024, BF16) for i in range(2)]
        astb = [kb.buf("f_ast", i) for i in range(2)]
        rot = 0
        jcount = 0
        for blk in range(FF // 256):
            slot = self.wrot % 2
            self.wrot += 1
            wb = self.WB[slot][:, 0:KC * 512].rearrange("p (k n) -> p k n", k=KC)
            pieces = []
            for half in range(2):
                ks = slice(half * 8, (half + 1) * 8)
                rs = slice(half * 1024, (half + 1) * 1024)
                pieces.append((wb[:, ks, 0:256], W[rs, blk * 256:(blk + 1) * 256]))
                pieces.append((wb[:, ks, 256:512], W[rs, FF + blk * 256:FF + (blk + 1) * 256]))
            wbufs = self.load_w(slot, pieces)
            for jj in range(2):
                j = blk * 2 + jj
                a_s = ast[jcount % 2]
                a_b = astb[jcount % 2]
                jcount += 1
                for tg in range(NTG):
                    ts_ = slice(tg * 512, (tg + 1) * 512)
                    pg, pgb = self.gbank()
                    pu, pub = self.gbank()
                    kb.mm_group([(lambda e, kc=kc: e.matmul(pg[:], wb[:, kc, jj * 128:(jj + 1) * 128], hT[:, kc, ts_],
                                                            start=(kc == 0), stop=(kc == KC - 1))) for kc in range(KC)],
                                reads=hb + wbufs, writes=[pgb])
                    kb.mm_group([(lambda e, kc=kc: e.matmul(pu[:], wb[:, kc, 256 + jj * 128:256 + (jj + 1) * 128], hT[:, kc, ts_],
                                                            start=(kc == 0), stop=(kc == KC - 1))) for kc in range(KC)],
                                reads=hb + wbufs, writes=[pub])
                    s_ = sg[rot % 2]
                    s_b = sgb[rot % 2]
                    rot += 1
                    kb.op("act", lambda e: e.activation(s_, pg[:], AF.Silu), reads=[pgb], writes=[s_b])
                    kb.op("dve", lambda e: e.tensor_tensor(a_s[:, ts_], s_, pu[:], ALU.mult), reads=[s_b, pub], writes=[a_b])
                kb.dma("sp", self.AT[j * 128:(j + 1) * 128, :], a_s, reads=[a_b], writes=[kb.buf("AT", j)])

    def ffn_down(self, l, wname, scale=0.5):
        kb = self.kb
        W = self.w[wname][l]
        NJ = FF // 128
        aT = self.ACTT[:, 0:NJ * 1024].rearrange("p (j t) -> p j t", j=NJ)
        xo = [self.ar(i * 512, 512) for i in range(4)]
        xob = [kb.buf("d_xo", i) for i in range(4)]
        rot = 0
        for half in range(2):
            ab = []
            for j in range(NJ):
                b = kb.buf("aTh", j)
                kb.dma("sp", aT[:, j, :], self.AT[j * 128:(j + 1) * 128, half * 1024:(half + 1) * 1024],
                       reads=[kb.buf("AT", j)], writes=[b])
                ab.append(b)
            for dc in range(KC):
                slot = self.wrot % 2
                self.wrot += 1
                wb = self.WB[slot][:, 0:NJ * 128].rearrange("p (k n) -> p k n", k=NJ)
                pieces = []
                for q in range(4):
                    pieces.append((wb[:, q * 11:(q + 1) * 11, :], W[q * 11 * 128:(q + 1) * 11 * 128, dc * 128:(dc + 1) * 128]))
                wbufs = self.load_w(slot, pieces)
                for tgl in range(2):
                    tg = half * 2 + tgl
                    x_ = xo[rot % 4]
                    x_b = xob[rot % 4]
                    rot += 1
                    xdb = kb.buf("XT", dc, tg)
                    kb.dma("pool", x_, self.XT[dc * 128:(dc + 1) * 128, tg * 512:(tg + 1) * 512], reads=[xdb], writes=[x_b])
                    p_, p_b = self.gbank()
                    kb.mm_group([(lambda e, j=j: e.matmul(p_[:], wb[:, j, :], aT[:, j, tgl * 512:(tgl + 1) * 512],
                                                          start=(j == 0), stop=(j == NJ - 1))) for j in range(NJ)],
                                reads=ab + wbufs, writes=[p_b])
                    kb.op("dve", lambda e: e.scalar_tensor_tensor(x_, p_[:], float(scale), x_, ALU.mult, ALU.add),
                          reads=[p_b, x_b], writes=[x_b])
                    kb.dma("sp", self.XT[dc * 128:(dc + 1) * 128, tg * 512:(tg + 1) * 512], x_, reads=[x_b], writes=[xdb])

    def build(self):
        kb = self.kb
        self.load_consts()
        if self.mode.startswith("mix_in"):
            self.load_pv(0)
            which = self.mode.split(":")[1]
            if "a" in which:
                self.mixer_a(0)
                kb.barrier()
            if "b" in which:
                self.mixer_b(0)
                kb.barrier()
            if "c" in which:
                self.mixer_c(0)
                kb.barrier()
            return
        self.copy_x_in()
        for l in range(self.nl):
            self.load_pv(l)
            self.norm(l, PV_F1)
            self.ffn_gateup(l, "ffn1_w_gu")
            kb.barrier()
            self.ffn_down(l, "ffn1_w_down")
            kb.barrier()
            if self.mode == "ffn1":
                break
            self.norm(l, PV_MIX)
            self.win_gemm(l)
            kb.barrier()
            self.mixer_a(l)
            kb.barrier()
            self.mixer_b(l)
            kb.barrier()
            self.mixer_c(l)
            kb.barrier()
            self.merge_proj(l)
            kb.barrier()
            self.wout(l)
            kb.barrier()
            if self.mode == "mix":
                break
            self.norm(l, PV_F2)
            self.ffn_gateup(l, "ffn2_w_gu")
            kb.barrier()
            self.ffn_down(l, "ffn2_w_down")
            kb.barrier()
        self.norm(self.nl - 1, PV_FIN, final=True)


def make_in_maps(inputs, nl, n_cores=8, batch_ids=None):
    consts = make_consts()
    pv = np.stack([pack_pv(inputs, l) for l in range(nl)])
    bnorm = np.stack([np.ascontiguousarray(np.broadcast_to(inputs["b_norm"][l][None, :], (128, 128))) for l in range(nl)]).astype(np.float32)
    shared = {k: np.ascontiguousarray(inputs[k][:nl]) for k in W_SHAPES}
    shared.update({"pv": pv, "bnorm": bnorm, "c_ident": consts["ident"], "c_cmask": consts["cmask"],
                   "c_abias": consts["abias"], "c_sel": consts["sel"], "c_rmask": consts["rmask"]})
    if batch_ids is None:
        batch_ids = list(range(n_cores))
    maps = []
    for b in batch_ids:
        m = dict(shared)
        m["xT"] = np.ascontiguousarray(inputs["x"][b].T)
        maps.append(m)
    return maps


_PROG_CACHE = {}


def kernel(**inputs):
    inputs = {k: np.asarray(v) for k, v in inputs.items()}
    key = ("full", DEPTH)
    if key not in _PROG_CACHE:
        _PROG_CACHE[key] = Prog(DEPTH, "full")
    prog = _PROG_CACHE[key]
    maps = make_in_maps(inputs, DEPTH)
    res = run_bass_kernel_spmd(prog.nc, maps, core_ids=list(range(8)))
    out = np.stack([np.ascontiguousarray(r["outT"].T) for r in res.results]).astype(np.float32)
    return out


def win_chunks():
    ch = []
    for i in range(36 + 32):
        ch.append((i * 128, 128, ("C", i * 128)))
    ch.append((B0 + 4096, 16, ("C", B0 + 4096)))
    for (c0, n) in C_CHUNKS:
        ch.append((C0 + c0, n, ("C", C0 + c0)))
    for i in range(48):
        ch.append((G0 + i * 128, 128, ("G", i * 128)))
    return ch


def _win_gemm(self, l):
    kb = self.kb
    W = self.w["w_in"][l]
    hT = self.hT()
    hb = [kb.buf("hT", kc) for kc in range(KC)]
    chunks = win_chunks()
    blocks = []
    cur = []
    for c in chunks:
        if cur and (cur[0][0] + sum(x[1] for x in cur) == c[0]) and (sum(x[1] for x in cur) + c[1] <= 512):
            cur.append(c)
        else:
            if cur:
                blocks.append(cur)
            cur = [c]
    blocks.append(cur)
    st = [self.ar(i * S, S) for i in range(3)]
    stb = [kb.buf("w_st", i) for i in range(3)]
    rot = 0
    ecount = 0
    for blkc in blocks:
        c0 = blkc[0][0]
        bw = sum(x[1] for x in blkc)
        slot = self.wrot % 2
        self.wrot += 1
        wb = self.WB[slot][:, 0:KC * bw].rearrange("p (k n) -> p k n", k=KC)
        pieces = []
        for q in range(4):
            pieces.append((wb[:, q * 4:(q + 1) * 4, :], W[q * 512:(q + 1) * 512, c0:c0 + bw]))
        wbufs = self.load_w(slot, pieces)
        off = 0
        for (cc0, n, dest) in blkc:
            s_ = st[rot % 3]
            s_b = stb[rot % 3]
            rot += 1
            isg = dest[0] == "G"
            sv = s_.bitcast(BF16)[:, 0:S] if isg else s_
            for tg in range(NTG):
                ts_ = slice(tg * 512, (tg + 1) * 512)
                p_, p_b = self.gbank()
                kb.mm_group([(lambda e, kc=kc: e.matmul(p_[0:n, :], wb[:, kc, off:off + n], hT[:, kc, ts_],
                                                        start=(kc == 0), stop=(kc == KC - 1))) for kc in range(KC)],
                            reads=hb + wbufs, writes=[p_b])
                if isg:
                    kb.op("act", lambda e: e.activation(sv[0:n, ts_], p_[0:n, :], AF.Sigmoid), reads=[p_b], writes=[s_b])
                else:
                    eng = "act" if (ecount % 2 == 0) else "dve"
                    ecount += 1
                    if eng == "act":
                        kb.op("act", lambda e: e.activation(sv[0:n, ts_], p_[0:n, :], AF.Copy), reads=[p_b], writes=[s_b])
                    else:
                        kb.op("dve", lambda e: e.tensor_copy(sv[0:n, ts_], p_[0:n, :]), reads=[p_b], writes=[s_b])
            if isg:
                kb.dma("sp", self.GT[dest[1]:dest[1] + n, :], sv[0:n, :], reads=[s_b], writes=[kb.buf("GT", dest[1])])
            else:
                kb.dma("sp", self.COLS[dest[1]:dest[1] + n, :], sv[0:n, :], reads=[s_b], writes=[kb.buf("COLS", dest[1])])
            off += n


Prog.win_gemm = _win_gemm


def _mixer_a(self, l):
    kb = self.kb
    SCALE = 128 ** -0.5
    DIL = [1, 4, 16]
    abias = self.ar(0, 3072)
    oT = [self.ar(3072 + g * S, S) for g in range(3)]
    lB = [self.ar(3072 + (3 + g) * S, S) for g in range(3)]
    Mt = self.ar(3072 + 6 * S, S)
    sc0 = 8 * S
    s_sb = [self.ac(sc0 + i * 512, 512, F32) for i in range(2)]
    p_sb = [self.ac(sc0 + 1024 + i * 256, 256) for i in range(2)]
    pT_sb = [self.ac(sc0 + 1536 + i * 256, 256) for i in range(2)]
    o_n = [self.ac(sc0 + 2048 + i * 256, 256, F32) for i in range(2)]
    lrep = [self.ac(sc0 + 2560 + i * 256, 256, F32) for i in range(2)]
    cols_ = self.ac(sc0 + 3072, 64, F32)

    def att(i):
        return self.ACTT[:, i * S:(i + 1) * S]
    qkv = [[att(hb * 3 + j) for j in range(3)] for hb in range(2)]
    vtok = att(6)
    ybf = att(7)
    b_abias = kb.buf("a_abias")
    kb.dma("sp", abias, self.c_abias[:, :], writes=[b_abias])
    identb = kb.buf("ident_b")
    identf = kb.buf("ident_f")
    blkrot = 0
    hcount = 0
    for slot in range(4):
        for g in range(3):
            h = g * 4 + slot
            d = DIL[g]
            nb = 16 // d
            hb = hcount % 2
            hcount += 1
            q_, k_, v_ = qkv[hb]
            bq, bk, bv = [kb.buf("a_qkv", hb, j) for j in range(3)]
            for j, (t_, b_) in enumerate(((q_, bq), (k_, bk), (v_, bv))):
                r0 = j * 1536 + h * 128
                kb.dma("pool", t_, self.COLS[r0:r0 + 128, :], reads=[kb.buf("COLS", r0)], writes=[b_])

            def sl(start, cnt):
                return slice(start, start + (cnt - 1) * d + 1, d) if d > 1 else slice(start, start + cnt)
            bvt = kb.buf("a_vtok")
            blist = [(r, b) for r in range(d) for b in range(nb)]
            for q4 in range(4):
                pT = self.psT[q4 % 2]
                pTb = self.psTb[q4 % 2]
                for i4 in range(4):
                    r, b = blist[q4 * 4 + i4]
                    st0 = 128 * b * d + r
                    kb.op("pe", lambda e: e.transpose(pT[:, i4 * 128:(i4 + 1) * 128], v_[:, sl(st0, 128)], self.ident_b[:]),
                          reads=[bv, identb], writes=[pTb])
                if q4 % 2:
                    kb.op("act", lambda e: e.activation(vtok[:, q4 * 512:(q4 + 1) * 512], pT[:, 0:512], AF.Copy), writes=[pTb, bvt])
                else:
                    kb.op("dve", lambda e: e.tensor_copy(vtok[:, q4 * 512:(q4 + 1) * 512], pT[:, 0:512]), writes=[pTb, bvt])
            boT = kb.buf("a_oT", g)
            blB = kb.buf("a_lB", g)
            def setup(r, b, u):
                v = dict(u=u)
                bi = r * nb + b
                st0 = 128 * b * d + r
                v["st0"] = st0
                v["qs"] = q_[:, sl(st0, 128)]
                if b >= 1:
                    v["nk"] = 256
                    v["ks"] = k_[:, sl(st0 - 128 * d, 256)]
                    v["bias"] = abias[:, h * 256:(h + 1) * 256]
                    v["kblks"] = [bi - 1, bi]
                else:
                    v["nk"] = 128
                    v["ks"] = k_[:, sl(st0, 128)]
                    v["bias"] = abias[:, h * 256 + 128:(h + 1) * 256]
                    v["kblks"] = [bi]
                v["ps_s"], v["ps_sb"] = self.pb[u], self.pbuf[u]
                v["ps_o"], v["ps_ob"] = self.pb[2 + u], self.pbuf[2 + u]
                v["ps_m"], v["ps_mb"] = self.pb[4 + u], self.pbuf[4 + u]
                v["pT"], v["pTb"] = self.psT[u], self.psTb[u]
                v["cb"] = kb.buf("a_cols", u)
                v["c"] = [cols_[:, u * 8 + i:u * 8 + i + 1] for i in range(6)]
                v["bs"] = kb.buf("a_s", u)
                v["ss"] = s_sb[u][:, 0:v["nk"]]
                v["bp"] = kb.buf("a_p", u)
                v["pp"] = p_sb[u][:, 0:v["nk"]]
                v["bpt"] = kb.buf("a_pT", u)
                v["bon"] = kb.buf("a_on", u)
                v["blr"] = kb.buf("a_lrep", u)
                return v

            def st1(v):
                kb.op("pe", lambda e: e.matmul(v["ps_s"][:, 0:v["nk"]], v["qs"], v["ks"], start=True, stop=True), reads=[bq, bk], writes=[v["ps_sb"]])

            def st2(v):
                c_m, c_nm = v["c"][0], v["c"][1]
                kb.op("dve", lambda e: e.scalar_tensor_tensor(v["ss"], v["ps_s"][:, 0:v["nk"]], SCALE, v["bias"], ALU.mult, ALU.add),
                      reads=[b_abias], writes=[v["ps_sb"], v["bs"]])
                kb.op("dve", lambda e: e.tensor_reduce(c_m, v["ss"], AX.X, ALU.max), reads=[v["bs"]], writes=[v["cb"]])
                kb.op("dve", lambda e: e.tensor_scalar(c_nm, c_m, -1.0, None, ALU.mult), reads=[v["cb"]], writes=[v["cb"]])

            def st3(v):
                c_m, c_nm, c_sum, c_ln, c_lse, c_rs = v["c"]
                kb.op("act", lambda e: e.activation(v["pp"], v["ss"], AF.Exp, bias=c_nm, scale=1.0, accum_out=c_sum),
                      reads=[v["bs"], v["cb"]], writes=[v["bp"], v["cb"]])
                kb.op("act", lambda e: e.activation(c_ln, c_sum, AF.Ln), reads=[v["cb"]], writes=[v["cb"]])

            def st4(v):
                c_m, c_nm, c_sum, c_ln, c_lse, c_rs = v["c"]
                kb.op("dve", lambda e: e.tensor_tensor(c_lse, c_ln, c_m, ALU.add), reads=[v["cb"]], writes=[v["cb"]])
                kb.op("dve", lambda e: e.reciprocal(c_rs, c_sum), reads=[v["cb"]], writes=[v["cb"]])
                kb.op("dve", lambda e: e.tensor_scalar(lrep[v["u"]], self.ones_f[:], c_lse, None, ALU.mult),
                      reads=[v["cb"], kb.buf("ones_f")], writes=[v["blr"]])

            def st5(v):
                for kk in range(v["nk"] // 128):
                    kb.op("pe", lambda e: e.transpose(v["pT"][:, kk * 128:(kk + 1) * 128], v["pp"][:, kk * 128:(kk + 1) * 128], self.ident_b[:]),
                          reads=[v["bp"], identb], writes=[v["pTb"]])

            def st6(v):
                kb.op("act", lambda e: e.activation(pT_sb[v["u"]][:, 0:v["nk"]], v["pT"][:, 0:v["nk"]], AF.Copy), writes=[v["pTb"], v["bpt"]])

            def st7(v):
                kblks = v["kblks"]
                kb.mm_group([(lambda e, kk=kk: e.matmul(v["ps_o"][:, 0:128], pT_sb[v["u"]][:, kk * 128:(kk + 1) * 128],
                                                        vtok[:, kblks[kk] * 128:(kblks[kk] + 1) * 128],
                                                        start=(kk == 0), stop=(kk == len(kblks) - 1)))
                             for kk in range(len(kblks))], reads=[v["bpt"], bvt], writes=[v["ps_ob"]])

            def st8(v):
                kb.op("act", lambda e: e.activation(o_n[v["u"]], v["ps_o"][:, 0:128], AF.Copy, scale=v["c"][5]),
                      reads=[v["cb"]], writes=[v["ps_ob"], v["bon"]])

            def st9(v):
                kb.op("pe", lambda e: e.transpose(v["ps_m"][:, 0:128], o_n[v["u"]], self.ident_f[:]), reads=[v["bon"], identf], writes=[v["ps_mb"]])
                kb.op("pe", lambda e: e.matmul(v["ps_m"][:, 128:256], lrep[v["u"]], self.ident_f[:], start=True, stop=True),
                      reads=[v["blr"], identf], writes=[v["ps_mb"]])

            def st10(v):
                kb.op("dve", lambda e: e.tensor_copy(oT[g][:, sl(v["st0"], 128)], v["ps_m"][:, 0:128]), writes=[v["ps_mb"], boT])
                kb.op("act", lambda e: e.activation(lB[g][:, sl(v["st0"], 128)], v["ps_m"][:, 128:256], AF.Copy), writes=[v["ps_mb"], blB])

            for i2 in range(0, len(blist), 2):
                pair = [setup(blist[i2 + q][0], blist[i2 + q][1], q) for q in range(2)]
                for stg in (st1, st2, st3, st4, st5, st6, st7, st8, st9, st10):
                    for v in pair:
                        stg(v)
        bo = [kb.buf("a_oT", g) for g in range(3)]
        bl = [kb.buf("a_lB", g) for g in range(3)]
        bM = kb.buf("a_M")
        kb.op("dve", lambda e: e.tensor_tensor(Mt, lB[0], lB[1], ALU.max), reads=[bl[0], bl[1]], writes=[bM])
        kb.op("dve", lambda e: e.tensor_tensor(Mt, Mt, lB[2], ALU.max), reads=[bM, bl[2]], writes=[bM])
        for g in range(3):
            kb.op("dve", lambda e: e.tensor_tensor(lB[g], lB[g], Mt, ALU.subtract), reads=[bl[g], bM], writes=[bl[g]])
            kb.op("act", lambda e: e.activation(lB[g], lB[g], AF.Exp), reads=[bl[g]], writes=[bl[g]])
        kb.op("dve", lambda e: e.tensor_tensor(Mt, lB[0], lB[1], ALU.add), reads=[bl[0], bl[1], bM], writes=[bM])
        kb.op("dve", lambda e: e.tensor_tensor(Mt, Mt, lB[2], ALU.add), reads=[bM, bl[2]], writes=[bM])
        kb.op("dve", lambda e: e.reciprocal(Mt, Mt), reads=[bM], writes=[bM])
        for g in range(3):
            kb.op("dve", lambda e: e.tensor_tensor(oT[g], oT[g], lB[g], ALU.mult), reads=[bo[g], bl[g]], writes=[bo[g]])
        kb.op("dve", lambda e: e.tensor_tensor(oT[0], oT[0], oT[1], ALU.add), reads=[bo[0], bo[1]], writes=[bo[0]])
        kb.op("dve", lambda e: e.tensor_tensor(oT[0], oT[0], oT[2], ALU.add), reads=[bo[0], bo[2]], writes=[bo[0]])
        by = kb.buf("a_ybf")
        kb.op("dve", lambda e: e.tensor_tensor(ybf, oT[0], Mt, ALU.mult), reads=[bo[0], bM], writes=[by])
        kb.dma("sp", self.YT[slot * 128:(slot + 1) * 128, :], ybf, reads=[by], writes=[kb.buf("YT", slot)])


Prog.mixer_a = _mixer_a


def _neumann(self, X0, X0T, bX, P, bP, scr, bscr, banks):
    self.neumann_multi([dict(X=X0, XT=X0T, bX=bX, P=P, bP=bP, scr=scr, bscr=bscr, banks=banks)])


def _neumann_multi(self, chains):
    kb = self.kb
    identf = kb.buf("ident_f")
    st = []
    for c in chains:
        st.append(dict(X=c["X"], XT=c["XT"], Xn=c["scr"][0], XTn=c["scr"][1], bX=c["bX"], bscr=c["bscr"],
                       P=c["P"], bP=c["bP"], banks=c["banks"]))
    for c in st:
        kb.op("dve", lambda e: e.tensor_tensor(c["P"], c["X"], self.ident_f[:], ALU.add), reads=[c["bX"], identf], writes=[c["bP"]])
    for lvl in range(5):
        last = lvl == 4
        for c in st:
            (pa, pab), (pbk, pbb), (pc, pcb) = c["banks"]
            if not last:
                kb.op("pe", lambda e: e.matmul(pa[:, 0:128], c["XT"], c["X"], start=True, stop=True), reads=[c["bX"]], writes=[pab])
            kb.op("pe", lambda e: e.matmul(pbk[:, 0:128], c["X"], c["XT"], start=True, stop=True), reads=[c["bX"]], writes=[pbb])
        for c in st:
            (pa, pab), (pbk, pbb), (pc, pcb) = c["banks"]
            if not last:
                kb.op("act", lambda e: e.activation(c["Xn"], pa[:, 0:128], AF.Copy), writes=[pab, c["bscr"]])
            kb.op("dve", lambda e: e.tensor_copy(c["XTn"], pbk[:, 0:128]), writes=[pbb, c["bscr"]])
        for c in st:
            (pa, pab), (pbk, pbb), (pc, pcb) = c["banks"]
            kb.op("pe", lambda e: e.matmul(pc[:, 0:128], c["XTn"], c["P"], start=True, stop=True), reads=[c["bscr"], c["bP"]], writes=[pcb])
        for c in st:
            (pa, pab), (pbk, pbb), (pc, pcb) = c["banks"]
            kb.op("dve", lambda e: e.tensor_tensor(c["P"], c["P"], pc[:, 0:128], ALU.add), reads=[c["bP"]], writes=[pcb, c["bP"]])
        for c in st:
            c["X"], c["XT"], c["Xn"], c["XTn"] = c["Xn"], c["XTn"], c["X"], c["XT"]
            c["bX"], c["bscr"] = c["bscr"], c["bX"]


Prog.neumann_multi = _neumann_multi
Prog.neumann = _neumann


def _mixer_b(self, l):
    kb = self.kb
    pv = self.pv[l % 2]
    pvb = kb.buf("pv", l % 2)
    U = [self.ar(i * S, S) for i in range(9)]
    T = [self.ac(i * S, S) for i in range(12)]
    qT, qdT, kT, kbT, vT, zs, wT, attT, kdec, kbg, vb, yT = T
    o3 = lambda t: t.rearrange("p (b d) -> p b d", b=16)
    sc0 = 12 * S

    def m128(i):
        return self.ac(sc0 + i * 256, 256, F32)
    Dt, Dl, t1, Xa, XaT, Pm, Xb, XbT, on_ = [m128(i) for i in range(9)]
    SET1 = [m128(40 + i) for i in range(8)]
    TTb1 = self.ac(sc0 + 48 * 256, 128)
    TTb = self.ac(sc0 + 9 * 256, 128)
    vnew = self.ac(sc0 + 9 * 256 + 128, 128)
    Sst = m128(10)
    Sbf = self.ac(sc0 + 11 * 256, 128)
    junk = m128(12)
    TQ = self.ac(sc0 + 13 * 256, 1024, F32).rearrange("p (b c) -> p b c", b=16)
    selt = self.ac(sc0 + 13 * 256 + 1024, 2048, F32)
    bnrep = self.ac(sc0 + 13 * 256 + 3072, 256, F32)
    cols_ = self.ac(sc0 + 13 * 256 + 3328, 64, F32)
    cm = self.cmask
    mU_s, mU_i, mL_s, nU_i, nL_i = [cm[:, i * 128:(i + 1) * 128] for i in range(5)]
    bcm = kb.buf("cmask")
    identf = kb.buf("ident_f")
    identb = kb.buf("ident_b")
    onesb = kb.buf("ones_f")
    R_g, R_be, R_x, R_bg, R_kd = U[0], U[1], U[2], U[3], U[4]
    bRg, bRbe, bRx, bRbg, bRkd = [kb.buf("b_R", i) for i in range(5)]
    bsel = kb.buf("b_sel")
    bbn = kb.buf("b_bn")
    bcols = kb.buf("b_cols")
    bTQ = kb.buf("b_TQ")
    abrow = B0 + 4096
    abbuf = kb.buf("COLS", abrow)
    kb.dma("sp", selt[0:8, 0:1024], self.c_sel[0:8, 0:1024], writes=[bsel])
    kb.dma("sp", bnrep, self.bnorm_d[l], writes=[bbn])
    kb.dma("sp", R_x[0:8, :], self.COLS[abrow:abrow + 8, :], reads=[abbuf], writes=[bRx])
    kb.dma("sp", R_be[0:8, :], self.COLS[abrow + 8:abrow + 16, :], reads=[abbuf], writes=[bRbe])
    kb.dma("sp", R_kd[0:8, :], self.c_rmask[0:8, :], writes=[bRkd])
    nea = cols_[0:8, 0:1]
    kb.op("act", lambda e: e.activation(nea, pv[0:8, PV_ALOG:PV_ALOG + 1], AF.Exp), reads=[pvb], writes=[bcols])
    kb.op("dve", lambda e: e.tensor_scalar(nea, nea, -1.0, None, ALU.mult), reads=[bcols], writes=[bcols])
    x8, be8, g8, bg8, kd8 = R_x[0:8, :], R_be[0:8, :], R_g[0:8, :], R_bg[0:8, :], R_kd[0:8, :]
    kb.op("dve", lambda e: e.tensor_scalar(x8, x8, pv[0:8, PV_DTB:PV_DTB + 1], None, ALU.add), reads=[bRx, pvb], writes=[bRx])
    kb.op("act", lambda e: e.activation(bg8, x8, AF.Abs), reads=[bRx], writes=[bRbg])
    kb.op("act", lambda e: e.activation(bg8, bg8, AF.Exp, scale=-1.0), reads=[bRbg], writes=[bRbg])
    kb.op("act", lambda e: e.activation(bg8, bg8, AF.Ln, bias=1.0), reads=[bRbg], writes=[bRbg])
    kb.op("dve", lambda e: e.scalar_tensor_tensor(bg8, x8, 0.0, bg8, ALU.max, ALU.add), reads=[bRx, bRbg], writes=[bRbg])
    kb.op("dve", lambda e: e.tensor_scalar(bg8, bg8, nea, None, ALU.mult), reads=[bRbg, bcols], writes=[bRbg])
    kb.op("dve", lambda e: e.tensor_tensor_scan(g8, kd8, bg8, 0.0, ALU.mult, ALU.add), reads=[bRkd, bRbg], writes=[bRg])
    kb.op("act", lambda e: e.activation(be8, be8, AF.Sigmoid), reads=[bRbe], writes=[bRbe])
    kb.op("act", lambda e: e.activation(x8, g8, AF.Exp), reads=[bRg], writes=[bRx])
    kb.op("dve", lambda e: e.tensor_tensor(bg8, be8, x8, ALU.mult), reads=[bRbe, bRx], writes=[bRbg])
    for c in range(32):
        cs = slice(c * 64, (c + 1) * 64)
        kb.op("dve", lambda e: e.tensor_scalar(kd8[:, cs], g8[:, cs], g8[:, c * 64 + 63:c * 64 + 64], -1.0, ALU.subtract, ALU.mult),
              reads=[bRg], writes=[bRkd])
    kb.op("act", lambda e: e.activation(kd8, kd8, AF.Exp), reads=[bRkd], writes=[bRkd])
    for blk in range(16):
        tsl = slice(blk * 128, (blk + 1) * 128)
        p_, p_b = self.pb[blk % 2], self.pbuf[blk % 2]
        for qi, (rr, rb) in enumerate(((g8, bRg), (be8, bRbe), (bg8, bRbg), (kd8, bRkd))):
            kb.op("pe", lambda e: e.matmul(p_[:, qi * 8:(qi + 1) * 8], rr[:, tsl], self.ident_f[0:8, 0:8], start=True, stop=True),
                  reads=[rb, identf], writes=[p_b])
        kb.op("act", lambda e: e.activation(TQ[:, blk, :], p_[:, 0:32], AF.Copy), writes=[p_b, bTQ])
    gB, egB, beB, Xs, Ys, u_all, o_all = U[2], U[3], U[4], U[5], U[6], U[7], U[8]
    bgB, begB, bbeB, bXs, bYs, bu, bo = [kb.buf("b_U", i) for i in range(7)]
    for h in range(8):
        for tg in range(NTG):
            ts_ = slice(tg * 512, (tg + 1) * 512)
            p_, p_b = self.pb[tg % 2], self.pbuf[tg % 2]
            kb.op("pe", lambda e: e.matmul(p_[:, :], selt[0:8, h * 128:(h + 1) * 128], g8[:, ts_], start=True, stop=True),
                  reads=[bsel, bRg], writes=[p_b])
            kb.op("dve", lambda e: e.tensor_copy(gB[:, ts_], p_[:, :]), writes=[p_b, bgB, bRx])
            kb.op("act", lambda e: e.activation(egB[:, ts_], p_[:, :], AF.Exp), writes=[p_b, begB, bRbg])
            p2, p2b = self.pb[2 + tg % 2], self.pbuf[2 + tg % 2]
            kb.op("pe", lambda e: e.matmul(p2[:, :], selt[0:8, h * 128:(h + 1) * 128], be8[:, ts_], start=True, stop=True),
                  reads=[bsel, bRbe], writes=[p2b])
            kb.op("act", lambda e: e.activation(beB[:, ts_], p2[:, :], AF.Copy), writes=[p2b, bbeB, bRkd])
        for j in range(3):
            r0 = B0 + j * 1024 + h * 128
            kb.dma("sp", Xs, self.COLS[r0:r0 + 128, :], reads=[kb.buf("COLS", r0)], writes=[bXs])
            cc = j * 8 + h
            wc = [pv[:, PV_CONV + cc * 4 + t:PV_CONV + cc * 4 + t + 1] for t in range(4)]
            kb.op("dve", lambda e: e.tensor_scalar(Ys, Xs, wc[3], None, ALU.mult), reads=[bXs, pvb], writes=[bYs])
            for sh in (1, 2, 3):
                kb.op("dve", lambda e: e.scalar_tensor_tensor(Ys[:, sh:], Xs[:, 0:S - sh], wc[3 - sh], Ys[:, sh:], ALU.mult, ALU.add),
                      reads=[bXs, pvb, bYs], writes=[bYs])
            kb.op("act", lambda e: e.activation(Ys, Ys, AF.Silu), reads=[bYs], writes=[bYs])
            if j < 2:
                kb.op("dve", lambda e: e.tensor_tensor(Xs, Ys, Ys, ALU.mult), reads=[bYs], writes=[bXs])
                for tg in range(NTG):
                    ts_ = slice(tg * 512, (tg + 1) * 512)
                    p_, p_b = self.pb[4 + tg % 2], self.pbuf[4 + tg % 2]
                    kb.op("pe", lambda e: e.matmul(p_[:, :], self.ones_f[:], Xs[:, ts_], start=True, stop=True),
                          reads=[bXs, onesb], writes=[p_b])
                    kb.op("dve", lambda e: e.tensor_scalar(Xs[:, ts_], p_[:, :], 1e-6, None, ALU.add), reads=[bXs], writes=[p_b, bXs])
                kb.op("act", lambda e: e.activation(Xs, Xs, AF.Sqrt), reads=[bXs], writes=[bXs])
                kb.op("dve", lambda e: e.reciprocal(Xs, Xs), reads=[bXs], writes=[bXs])
                if j == 0:
                    kb.op("dve", lambda e: e.scalar_tensor_tensor(Ys, Ys, 128 ** -0.5, Xs, ALU.mult, ALU.mult), reads=[bYs, bXs], writes=[bYs])
                    kb.op("act", lambda e: e.activation(qT, Ys, AF.Copy), reads=[bYs], writes=[kb.buf("b_T", 0)])
                    kb.op("dve", lambda e: e.tensor_tensor(qdT, Ys, egB, ALU.mult), reads=[bYs, begB], writes=[kb.buf("b_T", 1)])
                else:
                    kb.op("dve", lambda e: e.tensor_tensor(Ys, Ys, Xs, ALU.mult), reads=[bYs, bXs], writes=[bYs])
                    kb.op("act", lambda e: e.activation(kT, Ys, AF.Copy), reads=[bYs], writes=[kb.buf("b_T", 2)])
                    kb.op("dve", lambda e: e.tensor_tensor(kbT, Ys, beB, ALU.mult), reads=[bYs, bbeB], writes=[kb.buf("b_T", 3)])
            else:
                kb.op("act", lambda e: e.activation(vT, Ys, AF.Copy), reads=[bYs], writes=[kb.buf("b_T", 4)])
        r0 = B0 + 3072 + h * 128
        kb.dma("sp", Xs, self.COLS[r0:r0 + 128, :], reads=[kb.buf("COLS", r0)], writes=[bXs])
        kb.op("act", lambda e: e.activation(zs, Xs, AF.Silu), reads=[bXs], writes=[kb.buf("b_T", 5)])
        bq, bqd, bk, bkb, bv, bz, bw, batt, bkd, bkbg, bvb, by = [kb.buf("b_T", i) for i in range(12)]
        for blk in range(16):
            tsl = slice(blk * 128, (blk + 1) * 128)
            pT, pTb = self.psT[blk % 2], self.psTb[blk % 2]
            kb.op("pe", lambda e: e.transpose(pT[:, 0:128], kT[:, tsl], self.ident_b[:]), reads=[bk, identb], writes=[pTb])
            kb.op("pe", lambda e: e.transpose(pT[:, 128:256], vT[:, tsl], self.ident_b[:]), reads=[bv, identb], writes=[pTb])
            kb.op("act", lambda e: e.activation(kdec[:, tsl], pT[:, 0:128], AF.Copy, scale=TQ[:, blk, 24 + h:25 + h]),
                  reads=[bTQ], writes=[pTb, bkd])
            kb.op("dve", lambda e: e.tensor_scalar(kbg[:, tsl], pT[:, 0:128], TQ[:, blk, 16 + h:17 + h], None, ALU.mult),
                  reads=[bTQ], writes=[pTb, bkbg])
            kb.op("act", lambda e: e.activation(vb[:, tsl], pT[:, 128:256], AF.Copy, scale=TQ[:, blk, 8 + h:9 + h]),
                  reads=[bTQ], writes=[pTb, bvb])
        bm = [kb.buf("b_m", i) for i in range(8)]
        bDt, bDl, bt1, bXa, bXb, bP, bTTb, bon = bm
        bm1 = [kb.buf("b_m1", i) for i in range(8)]
        sets = [dict(Dt=Dt, Dl=Dl, t1=t1, Xa=Xa, XaT=XaT, Pm=Pm, Xb=Xb, XbT=XbT, TTb=TTb,
                     bDt=bDt, bDl=bDl, bt1=bt1, bXa=bXa, bXb=bXb, bP=bP, bTTb=bTTb, banks=(0, 1, 2)),
                dict(Dt=SET1[0], Dl=SET1[1], t1=SET1[2], Xa=SET1[3], XaT=SET1[4], Pm=SET1[5], Xb=SET1[6], XbT=SET1[7], TTb=TTb1,
                     bDt=bm1[0], bDl=bm1[1], bt1=bm1[2], bXa=bm1[3], bXb=bm1[4], bP=bm1[5], bTTb=bm1[6], banks=(3, 4, 5))]
        for bp in range(8):
            chains = []
            for q in range(2):
                blk = bp * 2 + q
                z_ = sets[q]
                tsl = slice(blk * 128, (blk + 1) * 128)
                gcol = TQ[:, blk, h:h + 1]
                (p0, p0b), (p1, p1b), (p2, p2b) = [(self.pb[i], self.pbuf[i]) for i in z_["banks"]]
                kb.op("pe", lambda e: e.matmul(p0[:, 0:128], kT[:, tsl], kbT[:, tsl], start=True, stop=True), reads=[bk, bkb], writes=[p0b])
                kb.op("pe", lambda e: e.matmul(p1[:, 0:128], kbT[:, tsl], kT[:, tsl], start=True, stop=True), reads=[bk, bkb], writes=[p1b])
                kb.op("pe", lambda e: e.matmul(p2[:, 0:128], kT[:, tsl], qT[:, tsl], start=True, stop=True), reads=[bk, bq], writes=[p2b])
                kb.op("dve", lambda e: e.tensor_scalar(z_["Dt"], gB[:, tsl], gcol, None, ALU.subtract), reads=[bgB, bTQ], writes=[z_["bDt"]])
                kb.op("dve", lambda e: e.tensor_tensor(z_["Dt"], z_["Dt"], nU_i, ALU.add), reads=[z_["bDt"], bcm], writes=[z_["bDt"]])
                kb.op("act", lambda e: e.activation(z_["Dt"], z_["Dt"], AF.Exp), reads=[z_["bDt"]], writes=[z_["bDt"]])
                kb.op("dve", lambda e: e.tensor_scalar(z_["Dl"], gB[:, tsl], gcol, -1.0, ALU.subtract, ALU.mult), reads=[bgB, bTQ], writes=[z_["bDl"]])
                kb.op("dve", lambda e: e.tensor_tensor(z_["Dl"], z_["Dl"], nL_i, ALU.add), reads=[z_["bDl"], bcm], writes=[z_["bDl"]])
                kb.op("act", lambda e: e.activation(z_["Dl"], z_["Dl"], AF.Exp), reads=[z_["bDl"]], writes=[z_["bDl"]])
                kb.op("dve", lambda e: e.tensor_tensor(z_["t1"], z_["Dt"], p0[:, 0:128], ALU.mult), reads=[z_["bDt"]], writes=[p0b, z_["bt1"]])
                kb.op("dve", lambda e: e.scalar_tensor_tensor(z_["Xa"], z_["t1"], -1.0, mU_s, ALU.mult, ALU.mult), reads=[z_["bt1"], bcm], writes=[z_["bXa"]])
                kb.op("dve", lambda e: e.tensor_tensor(z_["t1"], z_["Dl"], p1[:, 0:128], ALU.mult), reads=[z_["bDl"]], writes=[p1b, z_["bt1"]])
                kb.op("dve", lambda e: e.scalar_tensor_tensor(z_["XaT"], z_["t1"], -1.0, mL_s, ALU.mult, ALU.mult), reads=[z_["bt1"], bcm], writes=[z_["bXa"]])
                kb.op("dve", lambda e: e.tensor_tensor(attT[:, tsl], z_["Dt"], p2[:, 0:128], ALU.mult), reads=[z_["bDt"]], writes=[p2b, batt])
                chains.append(dict(X=z_["Xa"], XT=z_["XaT"], bX=z_["bXa"], P=z_["Pm"], bP=z_["bP"], scr=(z_["Xb"], z_["XbT"]), bscr=z_["bXb"],
                                   banks=((p0, p0b), (p1, p1b), (p2, p2b))))
            self.neumann_multi(chains)
            for q in range(2):
                blk = bp * 2 + q
                z_ = sets[q]
                tsl = slice(blk * 128, (blk + 1) * 128)
                (p0, p0b), (p1, p1b), (p2, p2b) = [(self.pb[i], self.pbuf[i]) for i in z_["banks"]]
                kb.op("act", lambda e: e.activation(z_["TTb"], z_["Pm"], AF.Copy), reads=[z_["bP"]], writes=[z_["bTTb"]])
                kb.op("pe", lambda e: e.matmul(p0[:, 0:128], z_["TTb"], vb[:, tsl], start=True, stop=True), reads=[z_["bTTb"], bvb], writes=[p0b])
                kb.op("pe", lambda e: e.matmul(p1[:, 0:128], kbg[:, tsl], z_["TTb"], start=True, stop=True), reads=[z_["bTTb"], bkbg], writes=[p1b])
                kb.op("act", lambda e: e.activation(u_all[:, tsl], p0[:, 0:128], AF.Copy), writes=[p0b, bu])
                kb.op("dve", lambda e: e.tensor_copy(wT[:, tsl], p1[:, 0:128]), writes=[p1b, bw])
        bS = kb.buf("b_S")
        Sbfs = [Sbf, self.ac(sc0 + 11 * 256 + 128, 128)]
        bSbs = [kb.buf("b_Sbf", 0), kb.buf("b_Sbf", 1)]
        bvn = kb.buf("b_vnew")
        kb.op("dve", lambda e: e.memset(Sst, 0.0), writes=[bS])
        kb.op("dve", lambda e: e.memset(Sbfs[0], 0.0), writes=[bSbs[0]])
        for c in range(32):
            blk, ch = c // 2, c % 2
            tsl = slice(blk * 128, (blk + 1) * 128)
            rs = slice(ch * 64, (ch + 1) * 64)
            tlast = c * 64 + 63
            Sc, bSc = Sbfs[c % 2], bSbs[c % 2]
            Sn, bSn = Sbfs[(c + 1) % 2], bSbs[(c + 1) % 2]
            p0, p0b = self.pb[0], self.pbuf[0]
            p1, p1b = self.pb[1], self.pbuf[1]
            p2, p2b = self.pb[2], self.pbuf[2]
            kb.op("pe", lambda e: e.matmul(p0[:, 0:128], wT[:, tsl], Sc, start=True, stop=True), reads=[bw, bSc], writes=[p0b])
            kb.op("dve", lambda e: e.tensor_tensor(vnew[rs, :], u_all[rs, tsl], p0[rs, 0:128], ALU.subtract), reads=[bu], writes=[p0b, bvn])
            kb.op("pe", lambda e: e.matmul(p2[:, 0:128], kdec[rs, tsl], vnew[rs, :], start=True, stop=True), reads=[bkd, bvn], writes=[p2b])
            kb.mm_group([lambda e: e.matmul(p1[:, 0:128], qdT[:, tsl], Sc, start=True, stop=False),
                         lambda e: e.matmul(p1[:, 0:128], attT[rs, tsl], vnew[rs, :], start=False, stop=True)],
                        reads=[bqd, bSc, batt, bvn], writes=[p1b])
            kb.op("dve", lambda e: e.scalar_tensor_tensor(Sn, Sst, egB[:, tlast:tlast + 1], p2[:, 0:128], ALU.mult, ALU.add),
                  reads=[bS, begB], writes=[p2b, bSn])
            kb.op("dve", lambda e: e.scalar_tensor_tensor(Sst, Sst, egB[:, tlast:tlast + 1], p2[:, 0:128], ALU.mult, ALU.add),
                  reads=[bS, begB], writes=[p2b, bS])
            kb.op("act", lambda e: e.activation(o_all[rs, tsl], p1[rs, 0:128], AF.Copy), writes=[p1b, bo])
        ons = [on_, SET1[0]]
        bons = [bon, bm1[0]]
        jks = [junk, SET1[1]]
        bjks = [kb.buf("b_junk"), bm1[1]]
        bmsc = [kb.buf("b_msc", 0), kb.buf("b_msc", 1)]
        for bp in range(8):
            pv_ = []
            for q in range(2):
                blk = bp * 2 + q
                pv_.append(dict(tsl=slice(blk * 128, (blk + 1) * 128), ms=cols_[:, 8 + q * 2:9 + q * 2], on=ons[q], bon=bons[q],
                                jk=jks[q], bjk=bjks[q], bc=bmsc[q], p3=self.pb[3 + q], p3b=self.pbuf[3 + q]))
            for w_ in pv_:
                kb.op("act", lambda e: e.activation(w_["jk"], o_all[:, w_["tsl"]], AF.Square, accum_out=w_["ms"]), reads=[bo], writes=[w_["bjk"], w_["bc"]])
            for w_ in pv_:
                kb.op("dve", lambda e: e.tensor_scalar(w_["ms"], w_["ms"], 1.0 / 128, 1e-6, ALU.mult, ALU.add), reads=[w_["bc"]], writes=[w_["bc"]])
            for w_ in pv_:
                kb.op("act", lambda e: e.activation(w_["ms"], w_["ms"], AF.Sqrt), reads=[w_["bc"]], writes=[w_["bc"]])
            for w_ in pv_:
                kb.op("dve", lambda e: e.reciprocal(w_["ms"], w_["ms"]), reads=[w_["bc"]], writes=[w_["bc"]])
            for w_ in pv_:
                kb.op("dve", lambda e: e.scalar_tensor_tensor(w_["on"], o_all[:, w_["tsl"]], w_["ms"], bnrep, ALU.mult, ALU.mult),
                      reads=[bo, w_["bc"], bbn], writes=[w_["bon"]])
            for w_ in pv_:
                kb.op("pe", lambda e: e.transpose(w_["p3"][:, 0:128], w_["on"], self.ident_f[:]), reads=[w_["bon"], identf], writes=[w_["p3b"]])
            for w_ in pv_:
                kb.op("dve", lambda e: e.tensor_tensor(yT[:, w_["tsl"]], zs[:, w_["tsl"]], w_["p3"][:, 0:128], ALU.mult), reads=[bz], writes=[w_["p3b"], by])
        kb.dma("sp", self.YT[512 + h * 128:512 + (h + 1) * 128, :], yT, reads=[by], writes=[kb.buf("YT", 4 + h)])


Prog.mixer_b = _mixer_b


def _mixer_c(self, l):
    kb = self.kb
    pv = self.pv[l % 2]
    pvb = kb.buf("pv", l % 2)
    U = [self.ar(i * S, S) for i in range(9)]
    T = [self.ac(i * S, S) for i in range(20)]
    TW, TA, SG0, SG1, aT, bT, kT, rT, VT, A_tok, B_tok, K_tok, V_tok, ArbT0, ArkT0, WtT, Gb, yT, ArbT1, ArkT1 = T
    ArbTs, ArkTs = [ArbT0, ArbT1], [ArkT0, ArkT1]
    sc0 = 20 * S

    def m128at(off):
        return self.ac(off, 256, F32)
    HS = []
    for hh_ in range(2):
        base = sc0 + hh_ * 1536
        HS.append(dict(Xa=m128at(base), XaT=m128at(base + 256), Xb=m128at(base + 512), XbT=m128at(base + 768), Pm=m128at(base + 1024),
                       TTb=self.ac(base + 1280, 128), AakT=self.ac(base + 1408, 128)))
    on_ = m128at(sc0 + 3072)
    o2 = sc0 + 3328
    for hh_ in range(2):
        HS[hh_]["AVb"] = self.ac(o2 + hh_ * 256, 64)
        HS[hh_]["Ub"] = self.ac(o2 + hh_ * 256 + 64, 64)
        HS[hh_]["Tbfs"] = [self.ac(o2 + hh_ * 256 + 128, 64), self.ac(o2 + hh_ * 256 + 192, 64)]
        HS[hh_]["Tst"] = self.ar(6 * S + hh_ * 128, 64)
        HS[hh_]["tmpS"] = self.ar(6 * S + hh_ * 128 + 64, 64)
    cols_ = self.ac(o2 + 512, 128, F32)
    junk = HS[1]["Pm"]
    Xa = HS[0]["Xa"]
    cm = self.cmask
    mU_s, mU_i, mL_s = cm[:, 0:128], cm[:, 128:256], cm[:, 256:384]
    bd64 = cm[:, 640:768]
    bcm = kb.buf("cmask")
    identf = kb.buf("ident_f")
    identb = kb.buf("ident_b")
    rmask, R, K, V, LW, At, X1, X2, o_all = U
    Y_all = U[1]
    EL = U[5]
    L_ = U[8]
    bU = [kb.buf("c_U", i) for i in range(9)]
    brm, bR, bK, bV, bLW, bAt, bX1, bX2, bo = bU
    bTl = [kb.buf("c_T", i) for i in range(20)]
    bTW, bTA, bSG0, bSG1, baT, bbT, bkT, brT, bVT, bAtok, bBtok, bKtok, bVtok, bArb0, bArk0, bWtT, bGb, byT, bArb1, bArk1 = bTl
    bArbs, bArks = [bArb0, bArb1], [bArk0, bArk1]
    bcols = kb.buf("c_cols")
    kb.dma("sp", rmask, self.c_rmask[:, :], writes=[brm])
    kb.op("dve", lambda e: e.memset(WtT, 0.0), writes=[bWtT])
    w2 = self.WB[0][0:96, 0:1024]
    a2 = self.WB[0][0:96, 1024:2048]
    g2 = self.WB[0][:, 2048:4096].rearrange("p (k n) -> p k n", k=2)
    bw = kb.buf("c_lw")
    kb.dma("pool", w2, self.w["c_w2"][l], writes=[kb.buf("c_lw", 0)])
    kb.dma("pool", a2, self.w["c_a2"][l], writes=[kb.buf("c_lw", 1)])
    kb.dma("pool", g2, self.w["c_g2"][l].rearrange("(k p) n -> p k n", p=128), writes=[kb.buf("c_lw", 2)])
    blw = [kb.buf("c_lw", i) for i in range(3)]

    def shift(dst, bdst, ci, n=128):
        mu = pv[0:n, PV_MU + ci:PV_MU + ci + 1]
        r0 = C0 + C_CHUNKS[ci][0]
        kb.dma("sp", X1[0:n, :], self.COLS[r0:r0 + n, :], reads=[kb.buf("COLS", r0)], writes=[bX1])
        kb.op("dve", lambda e: e.tensor_tensor(dst[0:n, 1:], X1[0:n, 0:S - 1], X1[0:n, 1:], ALU.subtract), reads=[bX1], writes=[bdst])
        kb.op("dve", lambda e: e.scalar_tensor_tensor(dst[0:n, 1:], dst[0:n, 1:], mu, X1[0:n, 1:], ALU.mult, ALU.add),
              reads=[bX1, pvb, bdst], writes=[bdst])
        kb.op("dve", lambda e: e.tensor_scalar(dst[0:n, 0:1], X1[0:n, 0:1], mu, None, ALU.mult), reads=[bX1, pvb, bdst], writes=[bdst])
        kb.op("dve", lambda e: e.tensor_tensor(dst[0:n, 0:1], X1[0:n, 0:1], dst[0:n, 0:1], ALU.subtract), reads=[bX1, bdst], writes=[bdst])

    shift(X2, bX2, 24, 96)
    kb.op("act", lambda e: e.activation(TW[0:96, :], X2[0:96, :], AF.Tanh), reads=[bX2], writes=[bTW])
    shift(X2, bX2, 25, 96)
    kb.op("act", lambda e: e.activation(TA[0:96, :], X2[0:96, :], AF.Copy), reads=[bX2], writes=[bTA])
    shift(X2, bX2, 26)
    kb.op("act", lambda e: e.activation(SG0, X2, AF.Sigmoid), reads=[bX2], writes=[bSG0])
    shift(X2, bX2, 27)
    kb.op("act", lambda e: e.activation(SG1, X2, AF.Sigmoid), reads=[bX2], writes=[bSG1])
    SG = [SG0, SG1]

    if getattr(self, 'cstage', 99) == 0:
        return
    for cc in range(8):
        c_w0, c_a0, c_kk, c_ka, c_rk, c_gw, c_gb = [pv[:, o + cc:o + cc + 1] for o in (PV_W0, PV_A0, PV_KK, PV_KA, PV_RK, PV_GNW, PV_GNB)]
        negw0 = cols_[:, 0:1]
        omka = cols_[:, 1:2]
        kb.op("dve", lambda e: e.tensor_scalar(negw0, c_w0, -1.0, None, ALU.mult), reads=[pvb], writes=[bcols])
        kb.op("dve", lambda e: e.tensor_scalar(omka, c_ka, -1.0, 1.0, ALU.mult, ALU.add), reads=[pvb], writes=[bcols])
        shift(R, bR, cc)
        shift(K, bK, 8 + cc)
        shift(V, bV, 16 + cc)
        csl = slice(cc * 128, (cc + 1) * 128)
        for tg in range(NTG):
            ts_ = slice(tg * 512, (tg + 1) * 512)
            p_, p_b = self.pb[tg % 2], self.pbuf[tg % 2]
            kb.op("pe", lambda e: e.matmul(p_[:, :], w2[:, csl], TW[0:96, ts_], start=True, stop=True), reads=[blw[0], bTW], writes=[p_b])
            kb.op("act", lambda e: e.activation(X2[:, ts_], p_[:, :], AF.Identity, bias=negw0, scale=-1.0), reads=[bcols], writes=[p_b, bX2])
            p2, p2b = self.pb[2 + tg % 2], self.pbuf[2 + tg % 2]
            kb.op("pe", lambda e: e.matmul(p2[:, :], a2[:, csl], TA[0:96, ts_], start=True, stop=True), reads=[blw[1], bTA], writes=[p2b])
            kb.op("act", lambda e: e.activation(At[:, ts_], p2[:, :], AF.Sigmoid, bias=c_a0, scale=1.0), reads=[pvb], writes=[p2b, bAt])
            p3, p3b = self.pb[4 + tg % 2], self.pbuf[4 + tg % 2]
            kb.mm_group([(lambda e, k2=k2: e.matmul(p3[:, :], g2[:, k2, csl], SG[k2][:, ts_], start=(k2 == 0), stop=(k2 == 1))) for k2 in range(2)],
                        reads=[blw[2], bSG0, bSG1], writes=[p3b])
            kb.op("dve", lambda e: e.tensor_copy(Gb[:, ts_], p3[:, :]), writes=[p3b, bGb])
        kb.op("act", lambda e: e.activation(X1, X2, AF.Abs), reads=[bX2], writes=[bX1])
        kb.op("act", lambda e: e.activation(X1, X1, AF.Exp, scale=-1.0), reads=[bX1], writes=[bX1])
        kb.op("act", lambda e: e.activation(X1, X1, AF.Ln, bias=1.0), reads=[bX1], writes=[bX1])
        kb.op("dve", lambda e: e.scalar_tensor_tensor(X1, X2, 0.0, X1, ALU.max, ALU.add), reads=[bX2, bX1], writes=[bX1])
        kb.op("act", lambda e: e.activation(X1, X1, AF.Exp, bias=-0.5, scale=-1.0), reads=[bX1], writes=[bX1])
        kb.op("dve", lambda e: e.tensor_scalar(LW, X1, -1.0, None, ALU.mult), reads=[bX1], writes=[bLW])
        if getattr(self, 'cstage', 99) == 1:
            return
        kb.op("dve", lambda e: e.tensor_scalar(X1, K, c_kk, None, ALU.mult), reads=[bK, pvb], writes=[bX1])
        kb.op("dve", lambda e: e.tensor_tensor(X2, X1, X1, ALU.mult), reads=[bX1], writes=[bX2])
        for tg in range(NTG):
            ts_ = slice(tg * 512, (tg + 1) * 512)
            p_, p_b = self.pb[tg % 2], self.pbuf[tg % 2]
            kb.op("pe", lambda e: e.matmul(p_[:, :], bd64, X2[:, ts_], start=True, stop=True), reads=[bcm, bX2], writes=[p_b])
            kb.op("dve", lambda e: e.tensor_scalar(X2[:, ts_], p_[:, :], 1e-6, None, ALU.add), reads=[bX2], writes=[p_b, bX2])
        kb.op("act", lambda e: e.activation(X2, X2, AF.Sqrt), reads=[bX2], writes=[bX2])
        kb.op("dve", lambda e: e.reciprocal(X2, X2), reads=[bX2], writes=[bX2])
        kb.op("dve", lambda e: e.tensor_tensor(X1, X1, X2, ALU.mult), reads=[bX1, bX2], writes=[bX1])
        kb.op("dve", lambda e: e.tensor_scalar(X2, At, c_ka, omka, ALU.mult, ALU.add), reads=[bAt, pvb, bcols], writes=[bX2])
        kb.op("dve", lambda e: e.tensor_tensor(K, K, X2, ALU.mult), reads=[bK, bX2], writes=[bK])
        kb.op("dve", lambda e: e.scalar_tensor_tensor(X2, R, c_rk, K, ALU.mult, ALU.mult), reads=[bR, bK, pvb], writes=[bX2])
        for tg in range(NTG):
            ts_ = slice(tg * 512, (tg + 1) * 512)
            p_, p_b = self.pb[2 + tg % 2], self.pbuf[2 + tg % 2]
            kb.op("pe", lambda e: e.matmul(p_[:, :], bd64, X2[:, ts_], start=True, stop=True), reads=[bcm, bX2], writes=[p_b])
            kb.op("dve", lambda e: e.tensor_tensor(X2[:, ts_], V[:, ts_], p_[:, :], ALU.mult), reads=[bV, bX2], writes=[p_b, bX2])
        if getattr(self, 'cstage', 99) == 2:
            return
        kb.op("dve", lambda e: e.tensor_tensor_scan(L_, rmask, LW, 0.0, ALU.mult, ALU.add), reads=[brm, bLW], writes=[bo])
        kb.op("dve", lambda e: e.tensor_tensor(LW, L_, LW, ALU.subtract), reads=[bo, bLW], writes=[bLW])
        kb.op("act", lambda e: e.activation(LW, LW, AF.Exp), reads=[bLW], writes=[bLW])
        kb.op("dve", lambda e: e.scalar_tensor_tensor(aT, X1, -1.0, LW, ALU.mult, ALU.mult), reads=[bX1, bLW], writes=[baT])
        kb.op("act", lambda e: e.activation(LW, L_, AF.Exp, scale=-1.0), reads=[bo, bLW], writes=[bLW])
        kb.op("dve", lambda e: e.tensor_tensor(X1, X1, At, ALU.mult), reads=[bX1, bAt], writes=[bX1])
        kb.op("dve", lambda e: e.tensor_tensor(bT, X1, LW, ALU.mult), reads=[bX1, bLW], writes=[bbT])
        kb.op("dve", lambda e: e.tensor_tensor(kT, K, LW, ALU.mult), reads=[bK, bLW], writes=[bkT])
        kb.op("act", lambda e: e.activation(EL, L_, AF.Exp), reads=[bo, bAt], writes=[bAt])
        kb.op("dve", lambda e: e.tensor_tensor(rT, R, EL, ALU.mult), reads=[bR, bAt], writes=[brT])
        kb.op("act", lambda e: e.activation(VT, V, AF.Copy), reads=[bV], writes=[bVT])
        if getattr(self, 'cstage', 99) == 3:
            return
        for blk in range(16):
            tsl = slice(blk * 128, (blk + 1) * 128)
            pT, pTb = self.psT[blk % 2], self.psTb[blk % 2]
            for i, (src, bsrc) in enumerate(((aT, baT), (bT, bbT), (kT, bkT), (VT, bVT))):
                kb.op("pe", lambda e: e.transpose(pT[:, i * 128:(i + 1) * 128], src[:, tsl], self.ident_b[:]), reads=[bsrc, identb], writes=[pTb])
            kb.op("act", lambda e: e.activation(A_tok[:, tsl], pT[:, 0:128], AF.Copy), writes=[pTb, bAtok])
            kb.op("dve", lambda e: e.tensor_copy(B_tok[:, tsl], pT[:, 128:256]), writes=[pTb, bBtok])
            kb.op("act", lambda e: e.activation(K_tok[:, tsl], pT[:, 256:384], AF.Copy), writes=[pTb, bKtok])
            kb.op("dve", lambda e: e.tensor_copy(V_tok[:, tsl], pT[:, 384:512]), writes=[pTb, bVtok])
        if getattr(self, 'cstage', 99) == 4:
            return
        for hh_ in range(2):
            for nm in ("Xa", "Xb", "P", "TTb", "Aak", "AV", "Ub", "Tbf", "Tst", "tmpS"):
                HS[hh_]["b" + nm] = kb.buf("c_hs", hh_, nm)
            HS[hh_]["banks"] = [(self.pb[3 * hh_ + i], self.pbuf[3 * hh_ + i]) for i in range(3)]
        bon = kb.buf("c_on")
        bXa = HS[0]["bXa"]
        bY = bR
        for blk in range(16):
            tsl = slice(blk * 128, (blk + 1) * 128)
            chains = []
            for hh in range(2):
                z_ = HS[hh]
                hs = slice(hh * 64, (hh + 1) * 64)
                (p0, p0b), (p1, p1b), (p2, p2b) = z_["banks"]
                kb.op("pe", lambda e: e.matmul(p0[:, 0:128], bT[hs, tsl], aT[hs, tsl], start=True, stop=True), reads=[bbT, baT], writes=[p0b])
                kb.op("pe", lambda e: e.matmul(p1[:, 0:128], aT[hs, tsl], bT[hs, tsl], start=True, stop=True), reads=[bbT, baT], writes=[p1b])
                kb.op("pe", lambda e: e.matmul(p2[:, 0:128], kT[hs, tsl], aT[hs, tsl], start=True, stop=True), reads=[bkT, baT], writes=[p2b])
                kb.op("dve", lambda e: e.tensor_tensor(z_["Xa"], mU_s, p0[:, 0:128], ALU.mult), reads=[bcm], writes=[p0b, z_["bXa"]])
                kb.op("dve", lambda e: e.tensor_tensor(z_["XaT"], mL_s, p1[:, 0:128], ALU.mult), reads=[bcm], writes=[p1b, z_["bXa"]])
                kb.op("dve", lambda e: e.tensor_tensor(z_["AakT"], mU_s, p2[:, 0:128], ALU.mult), reads=[bcm], writes=[p2b, z_["bAak"]])
                kb.op("pe", lambda e: e.matmul(p0[:, 0:128], bT[hs, tsl], rT[hs, tsl], start=True, stop=True), reads=[bbT, brT], writes=[p0b])
                kb.op("pe", lambda e: e.matmul(p1[:, 0:128], kT[hs, tsl], rT[hs, tsl], start=True, stop=True), reads=[bkT, brT], writes=[p1b])
                kb.op("dve", lambda e: e.tensor_tensor(ArbTs[hh][:, tsl], mU_i, p0[:, 0:128], ALU.mult), reads=[bcm], writes=[p0b, bArbs[hh]])
                kb.op("dve", lambda e: e.tensor_tensor(ArkTs[hh][:, tsl], mU_i, p1[:, 0:128], ALU.mult), reads=[bcm], writes=[p1b, bArks[hh]])
                chains.append(dict(X=z_["Xa"], XT=z_["XaT"], bX=z_["bXa"], P=z_["Pm"], bP=z_["bP"], scr=(z_["Xb"], z_["XbT"]), bscr=z_["bXb"],
                                   banks=z_["banks"]))
            self.neumann_multi(chains)
            for hh in range(2):
                z_ = HS[hh]
                hs = slice(hh * 64, (hh + 1) * 64)
                vsl = slice(blk * 128 + hh * 64, blk * 128 + (hh + 1) * 64)
                (p0, p0b), (p1, p1b), (p2, p2b) = z_["banks"]
                kb.op("act", lambda e: e.activation(z_["TTb"], z_["Pm"], AF.Copy), reads=[z_["bP"]], writes=[z_["bTTb"]])
                kb.op("pe", lambda e: e.matmul(p2[:, 0:64], z_["AakT"], V_tok[:, vsl], start=True, stop=True), reads=[z_["bAak"], bVtok], writes=[p2b])
                kb.op("act", lambda e: e.activation(z_["AVb"], p2[:, 0:64], AF.Copy), writes=[p2b, z_["bAV"]])
                kb.op("pe", lambda e: e.matmul(p0[:, 0:64], z_["TTb"], z_["AVb"], start=True, stop=True), reads=[z_["bTTb"], z_["bAV"]], writes=[p0b])
                kb.op("pe", lambda e: e.matmul(p1[:, 0:128], A_tok[:, tsl], z_["TTb"], start=True, stop=True), reads=[bAtok, z_["bTTb"]], writes=[p1b])
                kb.op("act", lambda e: e.activation(Y_all[:, vsl], p0[:, 0:64], AF.Copy), writes=[p0b, bY])
                kb.op("dve", lambda e: e.tensor_copy(WtT[hs, tsl], p1[hs, 0:128]), writes=[p1b, bWtT])
        if getattr(self, 'cstage', 99) == 5:
            return
        for hh in range(2):
            z_ = HS[hh]
            z_["bTbfs"] = [kb.buf("c_hs", hh, "Tbf0"), kb.buf("c_hs", hh, "Tbf1")]
            kb.op("dve", lambda e: e.memset(z_["Tst"], 0.0), writes=[z_["bTst"], bX1])
            kb.op("dve", lambda e: e.memset(z_["Tbfs"][0], 0.0), writes=[z_["bTbfs"][0]])
            kb.op("dve", lambda e: e.memset(z_["Tbfs"][1], 0.0), writes=[z_["bTbfs"][1]])
            kb.op("dve", lambda e: e.memset(z_["Ub"], 0.0), writes=[z_["bUb"]])
        for c in range(32):
            blk, ch = c // 2, c % 2
            tsl = slice(blk * 128, (blk + 1) * 128)
            rs = slice(ch * 64, (ch + 1) * 64)
            tlast = c * 64 + 63
            for hh in range(2):
                z_ = HS[hh]
                hs = slice(hh * 64, (hh + 1) * 64)
                vsl = slice(blk * 128 + hh * 64, blk * 128 + (hh + 1) * 64)
                (p0, p0b), (p1, p1b), (p2, p2b) = z_["banks"]
                Ub, Tst, tmpS = z_["Ub"], z_["Tst"], z_["tmpS"]
                bUb, bTst, btmpS = z_["bUb"], z_["bTst"], z_["btmpS"]
                Tc, bTc = z_["Tbfs"][c % 2], z_["bTbfs"][c % 2]
                Tn, bTn = z_["Tbfs"][(c + 1) % 2], z_["bTbfs"][(c + 1) % 2]
                elc = EL[hs, tlast:tlast + 1]
                kb.op("dve", lambda e: e.tensor_scalar(tmpS[hs, :], Tst[hs, :], elc, None, ALU.mult), reads=[bTst, bAt], writes=[btmpS, bX1])
                kb.op("pe", lambda e: e.matmul(p0[:, 0:64], WtT[hs, tsl], Tc[hs, :], start=True, stop=True), reads=[bWtT, bTc], writes=[p0b])
                kb.op("dve", lambda e: e.tensor_tensor(Ub[rs, :], Y_all[rs, vsl], p0[rs, 0:64], ALU.add), reads=[bY], writes=[p0b, bUb])
                kb.mm_group([lambda e: e.matmul(p2[:, 0:64], B_tok[rs, tsl], Ub[rs, :], start=True, stop=False),
                             lambda e: e.matmul(p2[:, 0:64], K_tok[rs, tsl], V_tok[rs, vsl], start=False, stop=True)],
                            reads=[bBtok, bKtok, bUb, bVtok], writes=[p2b])
                kb.mm_group([lambda e: e.matmul(p1[:, 0:64], rT[hs, tsl], Tc[hs, :], start=True, stop=False),
                             lambda e: e.matmul(p1[:, 0:64], ArbTs[hh][:, tsl], Ub[:, :], start=False, stop=False),
                             lambda e: e.matmul(p1[:, 0:64], ArkTs[hh][:, tsl], V_tok[:, vsl], start=False, stop=True)],
                            reads=[brT, bTc, bArbs[hh], bArks[hh], bUb, bVtok], writes=[p1b])
                kb.op("dve", lambda e: e.scalar_tensor_tensor(Tn[hs, :], p2[hs, 0:64], elc, tmpS[hs, :], ALU.mult, ALU.add),
                      reads=[btmpS, bAt], writes=[p2b, bTn])
                kb.op("dve", lambda e: e.scalar_tensor_tensor(Tst[hs, :], p2[hs, 0:64], elc, tmpS[hs, :], ALU.mult, ALU.add),
                      reads=[btmpS, bAt], writes=[p2b, bTst, bX1])
                kb.op("act", lambda e: e.activation(o_all[rs, vsl], p1[rs, 0:64], AF.Copy), writes=[p1b, bo])
        if getattr(self, 'cstage', 99) == 6:
            return
        bonh = [kb.buf("c_onh", 0), kb.buf("c_onh", 1)]
        bch = [kb.buf("c_colsh", 0), kb.buf("c_colsh", 1)]
        bjk = [HS[1]["bP"], HS[1]["bXb"]]
        jk = [HS[1]["Pm"], HS[1]["Xb"]]
        for blk in range(16):
            tsl = slice(blk * 128, (blk + 1) * 128)
            hv = []
            for hh in range(2):
                hv.append(dict(vsl=slice(blk * 128 + hh * 64, blk * 128 + (hh + 1) * 64), osl=slice(hh * 64, (hh + 1) * 64),
                               mcol=cols_[:, 8 + hh * 4:9 + hh * 4], vcol=cols_[:, 9 + hh * 4:10 + hh * 4], bon=bonh[hh], bc=bch[hh],
                               jk=jk[hh], bjk=bjk[hh]))
            for w_ in hv:
                kb.op("dve", lambda e: e.tensor_reduce(w_["mcol"], o_all[:, w_["vsl"]], AX.X, ALU.add), reads=[bo], writes=[w_["bc"]])
            for w_ in hv:
                kb.op("dve", lambda e: e.tensor_scalar(w_["mcol"], w_["mcol"], 1.0 / 64, None, ALU.mult), reads=[w_["bc"]], writes=[w_["bc"]])
            for w_ in hv:
                kb.op("dve", lambda e: e.tensor_scalar(on_[:, w_["osl"]], o_all[:, w_["vsl"]], w_["mcol"], None, ALU.subtract),
                      reads=[bo, w_["bc"]], writes=[w_["bon"]])
            for w_ in hv:
                kb.op("act", lambda e: e.activation(w_["jk"][:, 0:64], on_[:, w_["osl"]], AF.Square, accum_out=w_["vcol"]),
                      reads=[w_["bon"]], writes=[w_["bjk"], w_["bc"]])
            for w_ in hv:
                kb.op("dve", lambda e: e.tensor_scalar(w_["vcol"], w_["vcol"], 1.0 / 64, 64e-5, ALU.mult, ALU.add), reads=[w_["bc"]], writes=[w_["bc"]])
            for w_ in hv:
                kb.op("act", lambda e: e.activation(w_["vcol"], w_["vcol"], AF.Sqrt), reads=[w_["bc"]], writes=[w_["bc"]])
            for w_ in hv:
                kb.op("dve", lambda e: e.reciprocal(w_["vcol"], w_["vcol"]), reads=[w_["bc"]], writes=[w_["bc"]])
            for w_ in hv:
                kb.op("dve", lambda e: e.tensor_scalar(on_[:, w_["osl"]], on_[:, w_["osl"]], w_["vcol"], None, ALU.mult),
                      reads=[w_["bon"], w_["bc"]], writes=[w_["bon"]])
            p3, p3b = self.pb[3 + blk % 2], self.pbuf[3 + blk % 2]
            kb.op("pe", lambda e: e.transpose(p3[:, 0:128], on_, self.ident_f[:]), reads=[bonh[0], bonh[1], identf], writes=[p3b])
            kb.op("act", lambda e: e.activation(Xa, p3[:, 0:128], AF.Identity, bias=c_gb, scale=c_gw), reads=[pvb], writes=[p3b, bXa])
            kb.op("dve", lambda e: e.tensor_tensor(Xa, Xa, X2[:, tsl], ALU.add), reads=[bXa, bX2], writes=[bXa])
            kb.op("dve", lambda e: e.tensor_tensor(yT[:, tsl], Xa, Gb[:, tsl], ALU.mult), reads=[bXa, bGb], writes=[byT])
        kb.dma("sp", self.YT[1536 + cc * 128:1536 + (cc + 1) * 128, :], yT, reads=[byT], writes=[kb.buf("YT", 12 + cc)])


Prog.mixer_c = _mixer_c


def _merge_proj(self, l):
    kb = self.kb
    NK = 20
    yall = self.ACTT[:, 0:NK * S].rearrange("p (k t) -> p k t", k=NK)
    ybufs = []
    for k in range(NK):
        b = kb.buf("m_y", k)
        kb.dma("sp", yall[:, k, :], self.YT[k * 128:(k + 1) * 128, :], reads=[kb.buf("YT", k)], writes=[b])
        ybufs.append(b)
    gt = [self.ar(i * 1024, 1024, BF16) for i in range(6)]
    gtb = [kb.buf("m_gt", i) for i in range(6)]
    acc = [self.ar(6 * 1024 + i * 512, 512) for i in range(2)]
    accb = [kb.buf("m_acc", i) for i in range(2)]
    tmp = [self.ar(6 * 1024 + 1024 + i * 512, 512) for i in range(2)]
    tmpb = [kb.buf("m_tmp", i) for i in range(2)]
    mst = [self.ar(6 * 1024 + 2048 + i * 1024, 1024, BF16) for i in range(2)]
    mstb = [kb.buf("m_mst", i) for i in range(2)]
    segs = [("proj_a", 0, 4), ("proj_b", 4, 8), ("proj_c", 12, 8)]
    rot = 0
    for dcp in range(KC // 2):
        slot = self.wrot % 2
        self.wrot += 1
        wb = self.WB[slot][:, 0:NK * 256].rearrange("p (k n) -> p k n", k=NK)
        pieces = []
        for (nm, k0, nk) in segs:
            pieces.append((wb[:, k0:k0 + nk, :], self.w[nm][l][:, dcp * 256:(dcp + 1) * 256]))
        wbufs = self.load_w(slot, pieces)
        for dd in range(2):
            dc = dcp * 2 + dd
            gs = (dc % 2) * 3
            for x in range(3):
                r0 = x * D + dc * 128
                kb.dma("sp", gt[gs + x], self.GT[r0:r0 + 128, :], reads=[kb.buf("GT", r0)], writes=[gtb[gs + x]])
            m_s, m_b = mst[dc % 2], mstb[dc % 2]
            for tg in range(NTG):
                ts_ = slice(tg * 512, (tg + 1) * 512)
                a_, a_b = acc[rot % 2], accb[rot % 2]
                t_, t_b = tmp[rot % 2], tmpb[rot % 2]
                rot += 1
                for x, (nm, k0, nk) in enumerate(segs):
                    p_, p_b = self.gbank()
                    kb.mm_group([(lambda e, k=k: e.matmul(p_[:, :], wb[:, k0 + k, dd * 128:(dd + 1) * 128], yall[:, k0 + k, ts_],
                                                          start=(k == 0), stop=(k == nk - 1))) for k in range(nk)],
                                reads=ybufs[k0:k0 + nk] + wbufs, writes=[p_b])
                    if x == 0:
                        kb.op("dve", lambda e: e.tensor_tensor(a_, gt[gs + x][:, ts_], p_[:, :], ALU.mult), reads=[gtb[gs + x]], writes=[p_b, a_b])
                    else:
                        kb.op("dve", lambda e: e.tensor_tensor(t_, gt[gs + x][:, ts_], p_[:, :], ALU.mult), reads=[gtb[gs + x]], writes=[p_b, t_b])
                        if x == 1:
                            kb.op("dve", lambda e: e.tensor_tensor(a_, a_, t_, ALU.add), reads=[a_b, t_b], writes=[a_b])
                        else:
                            kb.op("dve", lambda e: e.tensor_tensor(m_s[:, ts_], a_, t_, ALU.add), reads=[a_b, t_b], writes=[m_b])
            kb.dma("sp", self.MT[dc * 128:(dc + 1) * 128, :], m_s, reads=[m_b], writes=[kb.buf("MT", dc)])


def _wout(self, l):
    kb = self.kb
    W = self.w["w_out"][l]
    mT = self.hT()
    mb = []
    for kc in range(KC):
        b = kb.buf("hT", kc)
        kb.dma("sp", mT[:, kc, :], self.MT[kc * 128:(kc + 1) * 128, :], reads=[kb.buf("MT", kc)], writes=[b])
        mb.append(b)
    xo = [self.ar(i * 512, 512) for i in range(4)]
    xob = [kb.buf("d_xo", i) for i in range(4)]
    rot = 0
    for blk in range(D // 512):
        slot = self.wrot % 2
        self.wrot += 1
        wb = self.WB[slot][:, 0:KC * 512].rearrange("p (k n) -> p k n", k=KC)
        pieces = [(wb[:, q * 4:(q + 1) * 4, :], W[q * 512:(q + 1) * 512, blk * 512:(blk + 1) * 512]) for q in range(4)]
        wbufs = self.load_w(slot, pieces)
        for dd in range(4):
            dc = blk * 4 + dd
            for tg in range(NTG):
                x_ = xo[rot % 4]
                x_b = xob[rot % 4]
                rot += 1
                xdb = kb.buf("XT", dc, tg)
                kb.dma("pool", x_, self.XT[dc * 128:(dc + 1) * 128, tg * 512:(tg + 1) * 512], reads=[xdb], writes=[x_b])
                p_, p_b = self.gbank()
                kb.mm_group([(lambda e, kc=kc: e.matmul(p_[:, :], wb[:, kc, dd * 128:(dd + 1) * 128], mT[:, kc, tg * 512:(tg + 1) * 512],
                                                        start=(kc == 0), stop=(kc == KC - 1))) for kc in range(KC)],
                            reads=mb + wbufs, writes=[p_b])
                kb.op("dve", lambda e: e.tensor_tensor(x_, x_, p_[:, :], ALU.add), reads=[x_b], writes=[p_b, x_b])
                kb.dma("sp", self.XT[dc * 128:(dc + 1) * 128, tg * 512:(tg + 1) * 512], x_, reads=[x_b], writes=[xdb])


Prog.merge_proj = _merge_proj
Prog.wout = _wout
```
